# Optimizing a Trainium2 kernel written in Bass

```python
import math
import jax, jax.numpy as jnp
from jax import lax
import numpy as np

D_MODEL = 1024
BATCH = 8
SEQ = 4096
DEPTH = 1
DEC_BATCH = 32
DEC_SEQ = 2048
PAST_LEN = 128

N_HEADS = 4
HEAD_DIM = 64
V_DIM = 2 * HEAD_DIM
ATT_WIDTH = N_HEADS * V_DIM
ROT_DIM = HEAD_DIM // 4
ROPE_THETA = 500000.0
Q_BLOCK = 128
CONV_CH = 512
CONV_K = 31
D_FF = 2816
N_BRANCH = 2
LN_EPS = 1e-5
ALPHA = (2.0 * DEPTH) ** 0.25
BETA = (8.0 * DEPTH) ** -0.25
Q_COLS = N_HEADS * 2 * HEAD_DIM
IN_COLS = 2 * Q_COLS + ATT_WIDTH + 2 * CONV_CH + N_BRANCH * D_MODEL

kernel_name = "hybrid_diffattn_conformer_gated_encoder"


def layer_norm(x, g, b):
    xf = x.astype(jnp.float32)
    mu = jnp.mean(xf, axis=-1, keepdims=True)
    var = jnp.mean(jnp.square(xf - mu), axis=-1, keepdims=True)
    y = (xf - mu) * lax.rsqrt(var + LN_EPS) * g.astype(jnp.float32) + b.astype(jnp.float32)
    return y.astype(x.dtype)


def rms_norm(x, g):
    xf = x.astype(jnp.float32)
    ms = jnp.mean(jnp.square(xf), axis=-1, keepdims=True)
    return (xf * lax.rsqrt(ms + LN_EPS) * g.astype(jnp.float32)).astype(x.dtype)


def swiglu(x, w_gate, w_up, w_down):
    return (jax.nn.silu(x @ w_gate) * (x @ w_up)) @ w_down


def rope_tables(seq, dtype):
    inv = ROPE_THETA ** (-jnp.arange(0, ROT_DIM, 2, dtype=jnp.float32) / ROT_DIM)
    ang = jnp.arange(seq, dtype=jnp.float32)[:, None] * inv[None, :]
    cos = jnp.cos(ang)[:, None, None, :].astype(dtype)
    sin = jnp.sin(ang)[:, None, None, :].astype(dtype)
    return cos, sin


def apply_rope(x, cos, sin):
    half = ROT_DIM // 2
    x1 = x[..., :half]
    x2 = x[..., half:ROT_DIM]
    xp = x[..., ROT_DIM:]
    return jnp.concatenate([x1 * cos - x2 * sin, x2 * cos + x1 * sin, xp], axis=-1)


def diff_attention(q, k, v, lam):
    B, S = q.shape[0], q.shape[1]
    nb = S // Q_BLOCK
    qb = (q * (HEAD_DIM ** -0.5)).reshape(B, nb, Q_BLOCK, N_HEADS, 2, HEAD_DIM)
    qb = qb.transpose(1, 0, 2, 3, 4, 5)

    def block(q_blk):
        s = jnp.einsum('bqhmd,bkhmd->bmhqk', q_blk, k)
        p = jax.nn.softmax(s.astype(jnp.float32), axis=-1)
        w = p[:, 0] - lam * p[:, 1]
        return jnp.einsum('bhqk,bkhe->bqhe', w.astype(v.dtype), v)

    o = lax.map(block, qb)
    return o.transpose(1, 0, 2, 3, 4).reshape(B, S, N_HEADS, V_DIM)


def conv_module(u, dw_w, dw_b, ln_g, ln_b, w_proj):
    y = lax.conv_general_dilated(
        u, dw_w.astype(u.dtype), window_strides=(1,),
        padding=[(CONV_K // 2, CONV_K // 2)],
        dimension_numbers=('NWC', 'WIO', 'NWC'),
        feature_group_count=CONV_CH) + dw_b
    y = jax.nn.silu(layer_norm(y, ln_g, ln_b))
    return y @ w_proj


def encoder_layer(x, l, p):
    lambda_init = 0.8 - 0.6 * math.exp(-0.3 * l)
    B, S, _ = x.shape
    h = swiglu(x, p['ffn1_w_gate'][l], p['ffn1_w_up'][l], p['ffn1_w_down'][l])
    x = layer_norm(ALPHA * x + 0.5 * h, p['ln1_g'][l], p['ln1_b'][l])
    z = x @ p['w_in'][l]
    c1 = Q_COLS
    c2 = 2 * Q_COLS
    c3 = c2 + ATT_WIDTH
    c4 = c3 + 2 * CONV_CH
    q, k, v, z_conv, z_gate = jnp.split(z, [c1, c2, c3, c4], axis=-1)
    cos, sin = rope_tables(S, x.dtype)
    q = apply_rope(q.reshape(B, S, N_HEADS, 2, HEAD_DIM), cos, sin)
    k = apply_rope(k.reshape(B, S, N_HEADS, 2, HEAD_DIM), cos, sin)
    v = v.reshape(B, S, N_HEADS, V_DIM)
    f32 = jnp.float32
    lam = (jnp.exp(jnp.sum(p['lambda_q1'][l].astype(f32) * p['lambda_k1'][l].astype(f32)))
           - jnp.exp(jnp.sum(p['lambda_q2'][l].astype(f32) * p['lambda_k2'][l].astype(f32)))
           + lambda_init)
    o = diff_attention(q, k, v, lam)
    o = rms_norm(o, p['subln_g'][l]) * (1.0 - lambda_init)
    a = o.reshape(B, S, ATT_WIDTH) @ p['w_att_proj'][l]
    za, zb = jnp.split(z_conv, 2, axis=-1)
    u = za * jax.nn.sigmoid(zb)
    c = conv_module(u, p['conv_dw_w'][l], p['conv_dw_b'][l],
                    p['conv_ln_g'][l], p['conv_ln_b'][l], p['w_conv_proj'][l])
    g = jax.nn.sigmoid(z_gate.reshape(B, S, N_BRANCH, D_MODEL) + p['b_gate'][l])
    m = g[:, :, 0] * a + g[:, :, 1] * c
    x = layer_norm(ALPHA * x + m @ p['w_out'][l], p['ln2_g'][l], p['ln2_b'][l])
    h = swiglu(x, p['ffn2_w_gate'][l], p['ffn2_w_up'][l], p['ffn2_w_down'][l])
    x = layer_norm(ALPHA * x + 0.5 * h, p['ln3_g'][l], p['ln3_b'][l])
    return x


def encoder_stack(x, p):
    for l in range(DEPTH):
        x = encoder_layer(x, l, p)
    return x


def setup_inputs(seed: int = 0) -> dict:
    key = jax.random.key(seed)
    ks = jax.random.split(key, 32)
    f32 = jnp.float32

    def nrm(k, shape, scale):
        return jax.random.normal(k, shape, f32) * scale

    L = DEPTH
    d = {}
    d['x_prompt'] = nrm(ks[0], (BATCH, SEQ, D_MODEL), 1.0)
    d['x_sample'] = nrm(ks[1], (DEC_BATCH, DEC_SEQ, D_MODEL), 1.0)
    d['ffn1_w_gate'] = nrm(ks[2], (L, D_MODEL, D_FF), D_MODEL ** -0.5)
    d['ffn1_w_up'] = nrm(ks[3], (L, D_MODEL, D_FF), D_MODEL ** -0.5)
    d['ffn1_w_down'] = nrm(ks[4], (L, D_FF, D_MODEL), BETA * D_FF ** -0.5)
    d['ln1_g'] = 1.0 + nrm(ks[5], (L, D_MODEL), 0.02)
    d['ln1_b'] = nrm(ks[6], (L, D_MODEL), 0.02)
    d['w_in'] = nrm(ks[7], (L, D_MODEL, IN_COLS), D_MODEL ** -0.5)
    d['b_gate'] = nrm(ks[8], (L, N_BRANCH, D_MODEL), 0.1)
    d['lambda_q1'] = nrm(ks[9], (L, HEAD_DIM), 0.1)
    d['lambda_k1'] = nrm(ks[10], (L, HEAD_DIM), 0.1)
    d['lambda_q2'] = nrm(ks[11], (L, HEAD_DIM), 0.1)
    d['lambda_k2'] = nrm(ks[12], (L, HEAD_DIM), 0.1)
    d['subln_g'] = 1.0 + nrm(ks[13], (L, V_DIM), 0.02)
    d['w_att_proj'] = nrm(ks[14], (L, ATT_WIDTH, D_MODEL), BETA * ATT_WIDTH ** -0.5)
    d['conv_dw_w'] = nrm(ks[15], (L, CONV_K, 1, CONV_CH), CONV_K ** -0.5)
    d['conv_dw_b'] = nrm(ks[16], (L, CONV_CH), 0.02)
    d['conv_ln_g'] = 1.0 + nrm(ks[17], (L, CONV_CH), 0.02)
    d['conv_ln_b'] = nrm(ks[18], (L, CONV_CH), 0.02)
    d['w_conv_proj'] = nrm(ks[19], (L, CONV_CH, D_MODEL), BETA * CONV_CH ** -0.5)
    d['w_out'] = nrm(ks[20], (L, D_MODEL, D_MODEL), BETA * D_MODEL ** -0.5)
    d['ln2_g'] = 1.0 + nrm(ks[21], (L, D_MODEL), 0.02)
    d['ln2_b'] = nrm(ks[22], (L, D_MODEL), 0.02)
    d['ffn2_w_gate'] = nrm(ks[23], (L, D_MODEL, D_FF), D_MODEL ** -0.5)
    d['ffn2_w_up'] = nrm(ks[24], (L, D_MODEL, D_FF), D_MODEL ** -0.5)
    d['ffn2_w_down'] = nrm(ks[25], (L, D_FF, D_MODEL), BETA * D_FF ** -0.5)
    d['ln3_g'] = 1.0 + nrm(ks[26], (L, D_MODEL), 0.02)
    d['ln3_b'] = nrm(ks[27], (L, D_MODEL), 0.02)
    return d


def reference(x_prompt, x_sample, ffn1_w_gate, ffn1_w_up, ffn1_w_down, ln1_g, ln1_b,
              w_in, b_gate, lambda_q1, lambda_k1, lambda_q2, lambda_k2, subln_g,
              w_att_proj, conv_dw_w, conv_dw_b, conv_ln_g, conv_ln_b, w_conv_proj,
              w_out, ln2_g, ln2_b, ffn2_w_gate, ffn2_w_up, ffn2_w_down, ln3_g, ln3_b):
    p = dict(ffn1_w_gate=ffn1_w_gate, ffn1_w_up=ffn1_w_up, ffn1_w_down=ffn1_w_down,
             ln1_g=ln1_g, ln1_b=ln1_b, w_in=w_in, b_gate=b_gate,
             lambda_q1=lambda_q1, lambda_k1=lambda_k1, lambda_q2=lambda_q2, lambda_k2=lambda_k2,
             subln_g=subln_g, w_att_proj=w_att_proj, conv_dw_w=conv_dw_w, conv_dw_b=conv_dw_b,
             conv_ln_g=conv_ln_g, conv_ln_b=conv_ln_b, w_conv_proj=w_conv_proj, w_out=w_out,
             ln2_g=ln2_g, ln2_b=ln2_b, ffn2_w_gate=ffn2_w_gate, ffn2_w_up=ffn2_w_up,
             ffn2_w_down=ffn2_w_down, ln3_g=ln3_g, ln3_b=ln3_b)
    y_prompt = encoder_stack(x_prompt, p)
    y_sample = encoder_stack(x_sample, p)
    return (y_prompt, y_sample)
```

```python
import math
from contextlib import ExitStack

import numpy as np
import concourse.bass as bass
import concourse.mybir as mybir
from concourse.bass_utils import run_bass_kernel_spmd

F32 = mybir.dt.float32
BF16 = mybir.dt.bfloat16
AF = mybir.ActivationFunctionType
ALU = mybir.AluOpType

D = 1024
FF = 2816
NFC = FF // 128
NDC = D // 128
ALPHA = 2.0 ** 0.25
EPS = 1e-5
LAMBDA_INIT = 0.8 - 0.6 * math.exp(-0.3 * 0)
CONV_K = 31
NCORES = 8
MAXT = 32
NXT = 6
DBG = {}


class Sem:
    __slots__ = ("h", "cnt")

    def __init__(self, h):
        self.h = h
        self.cnt = 0


class Tile:
    __slots__ = ("w", "r", "dsem", "name", "psum")

    def __init__(self, name="", psum=False):
        self.w = {}
        self.r = {}
        self.dsem = None
        self.name = name
        self.psum = psum


class Eng:
    def __init__(self, h, sem, name, is_pe=False):
        self.h = h
        self.sem = sem
        self.name = name
        self.is_pe = is_pe
        self.waited = {}


class K:
    def __init__(self, nc, es):
        self.nc = nc
        self.es = es
        self.allsems = []
        self.freesems = []
        self.pe = Eng(nc.tensor, self._mksem("s_pe"), "pe", True)
        self.act = Eng(nc.scalar, self._mksem("s_act"), "act")
        self.dve = Eng(nc.vector, self._mksem("s_dve"), "dve")
        self.pool = Eng(nc.gpsimd, self._mksem("s_pool"), "pool")
        self.sp = Eng(nc.sync, self._mksem("s_sp"), "sp")
        self.engs = [self.pe, self.act, self.dve, self.pool, self.sp]
        self.phase_sems = []
        self.nsem = 0

    def _mksem(self, name):
        s = Sem(self.es.enter_context(self.nc.semaphore(name)))
        self.allsems.append(s)
        return s

    def dsem(self):
        if self.freesems:
            s = self.freesems.pop()
        else:
            self.nsem += 1
            s = self._mksem("s_d%d" % self.nsem)
        self.phase_sems.append(s)
        return s

    def _waits(self, eng, reads, writes):
        need = {}
        for t in reads:
            for s, v in t.w.items():
                if need.get(s, 0) < v:
                    need[s] = v
            if t.psum:
                for s, v in t.r.items():
                    if s is not eng.sem and need.get(s, 0) < v:
                        need[s] = v
        for t in writes:
            for s, v in t.w.items():
                if need.get(s, 0) < v:
                    need[s] = v
            for s, v in t.r.items():
                if need.get(s, 0) < v:
                    need[s] = v
        for s, v in need.items():
            if eng.is_pe and s is eng.sem:
                continue
            if eng.waited.get(s, 0) < v:
                eng.h.wait_ge(s.h, v)
                eng.waited[s] = v

    def op(self, eng, fn, reads=(), writes=(), inc=True):
        self._waits(eng, reads, writes)
        ins = fn()
        s = eng.sem
        if inc:
            s.cnt += 1
            ins.then_inc(s.h, 1)
            seq = s.cnt
        else:
            seq = s.cnt + 1
        for t in reads:
            if t.r.get(s, 0) < seq:
                t.r[s] = seq
        for t in writes:
            if t.w.get(s, 0) < seq:
                t.w[s] = seq
        return ins

    def dma(self, q, out, in_, st, reads=(), writes=()):
        self._waits(q, reads, writes)
        if st.dsem is None:
            st.dsem = self.dsem()
        s = st.dsem
        q.h.dma_start(out=out, in_=in_).then_inc(s.h, 16)
        s.cnt += 16
        for t in reads:
            t.r[s] = s.cnt
        for t in writes:
            t.w[s] = s.cnt

    def barrier(self):
        for e in self.engs:
            for s in self.allsems:
                if s.cnt > 0 and e.waited.get(s, 0) < s.cnt:
                    e.h.wait_ge(s.h, s.cnt)
                    e.waited[s] = s.cnt
        self.freesems.extend(self.phase_sems)
        self.phase_sems = []


def _sb(nc, es, name, shape, dt):
    return es.enter_context(nc.sbuf_tensor(name, shape, dt))


def load_weight(k, dst, dst_tile, src, rows, cols, col0=0):
    nchunk = rows // 128
    step = next(st for st in (2048, 1408, 1280, 1024, 512, 256, 128) if cols % st == 0)
    for rc in range(nchunk):
        for c0 in range(0, cols, step):
            k.dma(k.pool, dst[:, rc, c0:c0 + step], src[rc * 128:(rc + 1) * 128, col0 + c0:col0 + c0 + step],
                  dst_tile, writes=[dst_tile])


def ln_epilogue(k, nc, xtj, xt_t, Yh, Yh_t, xscale, eps_eff, gB, bB, cst_t, stat, stat_t, mhalf, out_rows, out_t):
    for h in range(2):
        k.op(k.dve, lambda h=h: nc.vector.scalar_tensor_tensor(
            out=xtj[:, h * 512:(h + 1) * 512], in0=xtj[:, h * 512:(h + 1) * 512], scalar=float(xscale),
            in1=Yh[h][:, :], op0=ALU.mult, op1=ALU.add), reads=[Yh_t[h], xt_t], writes=[xt_t])
    for h in range(2):
        k.op(k.dve, lambda h=h: nc.vector.bn_stats(out=stat[:, h * 6:(h + 1) * 6], in_=xtj[:, h * 512:(h + 1) * 512]),
             reads=[xt_t], writes=[stat_t])
    k.op(k.dve, lambda: nc.vector.bn_aggr(out=stat[:, 12:14], in_=stat[:, 0:12]), reads=[stat_t], writes=[stat_t])
    k.op(k.dve, lambda: nc.vector.tensor_scalar(out=stat[:, 14:15], in0=stat[:, 13:14], scalar1=float(eps_eff),
                                                scalar2=None, op0=ALU.add), reads=[stat_t], writes=[stat_t])
    k.op(k.pool, lambda: nc.gpsimd.tensor_tensor(out=stat[:, 15:16], in0=stat[:, 14:15], in1=mhalf[:, 0:1], op=ALU.pow),
         reads=[stat_t, cst_t], writes=[stat_t])
    k.op(k.dve, lambda: nc.vector.tensor_scalar(out=stat[:, 16:17], in0=stat[:, 12:13], scalar1=-1.0,
                                                scalar2=stat[:, 15:16], op0=ALU.mult, op1=ALU.mult),
         reads=[stat_t], writes=[stat_t])
    k.op(k.act, lambda: nc.scalar.activation(out=xtj[:, :], in_=xtj[:, :], func=AF.Identity,
                                             scale=stat[:, 15:16], bias=stat[:, 16:17]),
         reads=[xt_t, stat_t], writes=[xt_t])
    k.op(k.pool, lambda: nc.gpsimd.tensor_tensor(out=xtj[:, :], in0=xtj[:, :], in1=gB, op=ALU.mult),
         reads=[xt_t, cst_t], writes=[xt_t])
    k.op(k.pool, lambda: nc.gpsimd.tensor_tensor(out=xtj[:, :], in0=xtj[:, :], in1=bB, op=ALU.add),
         reads=[xt_t, cst_t], writes=[xt_t])
    k.dma(k.sp, out_rows, xtj[:, :], xt_t, reads=[xt_t], writes=[out_t])


def transpose_block(k, nc, xt, xt_t, tiles, ident, cst_t, banks, bank_t, xT, xT_t, bi):
    for dc in range(NDC):
        bk = (bi * NDC + dc) % 4
        for j in range(4):
            k.op(k.pe, lambda j=j, dc=dc, bk=bk: nc.tensor.transpose(
                out=banks[bk][:, j * 128:(j + 1) * 128], in_=xt[tiles[j]][:, dc * 128:(dc + 1) * 128], identity=ident),
                reads=[xt_t[tiles[j]], cst_t], writes=[bank_t[bk]], inc=(j == 3))
        if dc % 2 == 0:
            k.op(k.dve, lambda dc=dc, bk=bk: nc.vector.tensor_copy(out=xT[:, dc, :], in_=banks[bk][:, :]),
                 reads=[bank_t[bk]], writes=[xT_t])
        else:
            k.op(k.act, lambda dc=dc, bk=bk: nc.scalar.copy(out=xT[:, dc, :], in_=banks[bk][:, :]),
                 reads=[bank_t[bk]], writes=[xT_t])


def ffn_phase(k, nc, T, xin, xin_t, Wg, Wu, Wd, lnp, ln_idx, ident_d, xout, xout_t, pname):
    nb = T // 512
    with ExitStack() as es:
        wg = _sb(nc, es, pname + "wg", [128, NDC, FF], BF16)
        wu = _sb(nc, es, pname + "wu", [128, NDC, FF], BF16)
        wd = _sb(nc, es, pname + "wd", [128, NFC, D], BF16)
        cst = _sb(nc, es, pname + "cst", [128, 2 * D + 128 + 8], F32)
        xt = [_sb(nc, es, pname + "xt%d" % i, [128, D], F32) for i in range(NXT)]
        xT = [_sb(nc, es, pname + "xT%d" % i, [128, NDC, 512], BF16) for i in range(2)]
        hT = _sb(nc, es, pname + "hT", [128, NFC, 512], BF16)
        sg = [_sb(nc, es, pname + "sg%d" % i, [128, 512], F32) for i in range(2)]
        stat = [_sb(nc, es, pname + "st%d" % i, [128, 32], F32) for i in range(4)]
        banks = [es.enter_context(nc.psum_tensor(pname + "bk%d" % i, [128, 512], F32)) for i in range(8)]
        wg_t, wu_t, wd_t, cst_t = Tile("wg"), Tile("wu"), Tile("wd"), Tile("cst")
        xt_t = [Tile("xt") for _ in range(NXT)]
        xT_t = [Tile("xT") for _ in range(2)]
        hT_t = [Tile("hT") for _ in range(NFC)]
        sg_t = [Tile("sg") for _ in range(2)]
        stat_t = [Tile("stat") for _ in range(4)]
        bank_t = [Tile("bank", True) for _ in range(8)]
        gB = cst[:, 0:D]
        bB = cst[:, D:2 * D]
        ident = cst[:, 2 * D:2 * D + 128]
        mhalf = cst[:, 2 * D + 128:2 * D + 129]
        k.dma(k.sp, gB, lnp[:, 2 * ln_idx, :], cst_t, writes=[cst_t])
        k.dma(k.sp, bB, lnp[:, 2 * ln_idx + 1, :], cst_t, writes=[cst_t])
        k.dma(k.sp, ident, ident_d[:, :], cst_t, writes=[cst_t])
        k.op(k.dve, lambda: nc.vector.memset(mhalf, -0.5), writes=[cst_t])

        def load_x(b, js=(0, 1, 2, 3)):
            for j in js:
                ti = (b * 4 + j) % NXT
                r0 = (b * 4 + j) * 128
                k.dma(k.sp, xt[ti][:, :], xin[r0:r0 + 128, :], xt_t[ti], reads=[xin_t[b]], writes=[xt_t[ti]])

        load_x(0)
        load_weight(k, wg, wg_t, Wg, D, FF)
        load_weight(k, wu, wu_t, Wu, D, FF)
        load_weight(k, wd, wd_t, Wd, FF, D)
        for b in range(nb):
            if b + 1 < nb:
                load_x(b + 1, (0, 1))
            tiles = [(b * 4 + j) % NXT for j in range(4)]
            xTb, xTb_t = xT[b % 2], xT_t[b % 2]
            transpose_block(k, nc, xt, xt_t, tiles, ident, cst_t, banks, bank_t, xTb, xTb_t, b)
            for fc in range(NFC):
                A, B = (fc % 2) * 2, (fc % 2) * 2 + 1
                for dc in range(NDC):
                    k.op(k.pe, lambda dc=dc, fc=fc, A=A: nc.tensor.matmul(
                        banks[A][:, :], lhsT=wg[:, dc, fc * 128:(fc + 1) * 128], rhs=xTb[:, dc, :],
                        start=(dc == 0), stop=(dc == NDC - 1)),
                        reads=[wg_t, xTb_t], writes=[bank_t[A]], inc=(dc == NDC - 1))
                for dc in range(NDC):
                    k.op(k.pe, lambda dc=dc, fc=fc, B=B: nc.tensor.matmul(
                        banks[B][:, :], lhsT=wu[:, dc, fc * 128:(fc + 1) * 128], rhs=xTb[:, dc, :],
                        start=(dc == 0), stop=(dc == NDC - 1)),
                        reads=[wu_t, xTb_t], writes=[bank_t[B]], inc=(dc == NDC - 1))
                k.op(k.act, lambda fc=fc, A=A: nc.scalar.activation(out=sg[fc % 2][:, :], in_=banks[A][:, :], func=AF.Silu),
                     reads=[bank_t[A]], writes=[sg_t[fc % 2]])
                k.op(k.dve, lambda fc=fc, B=B: nc.vector.tensor_tensor(
                    out=hT[:, fc, :], in0=banks[B][:, :], in1=sg[fc % 2][:, :], op=ALU.mult),
                    reads=[bank_t[B], sg_t[fc % 2]], writes=[hT_t[fc]])
            for j in range(4):
                Yb = [4 + (j % 2) * 2, 5 + (j % 2) * 2]
                for h in range(2):
                    for fc in range(NFC):
                        k.op(k.pe, lambda fc=fc, h=h, j=j, Yb=Yb: nc.tensor.matmul(
                            banks[Yb[h]][:, :], lhsT=hT[:, fc, j * 128:(j + 1) * 128], rhs=wd[:, fc, h * 512:(h + 1) * 512],
                            start=(fc == 0), stop=(fc == NFC - 1)),
                            reads=[hT_t[fc], wd_t], writes=[bank_t[Yb[h]]], inc=(fc == NFC - 1))
                ti = tiles[j]
                r0 = (b * 4 + j) * 128
                ln_epilogue(k, nc, xt[ti], xt_t[ti], [banks[Yb[0]], banks[Yb[1]]], [bank_t[Yb[0]], bank_t[Yb[1]]],
                            2.0 * ALPHA, 4.0 * EPS, gB, bB, cst_t, stat[j], stat_t[j], mhalf,
                            xout[r0:r0 + 128, :], xout_t[b])
                if j == 1 and b + 1 < nb:
                    load_x(b + 1, (2, 3))
        k.barrier()


def mix_phase(k, nc, seqs, x1s, x1_t, w_in, rope_d, convw_d, convp_d, subg_d, lamv_d, ident_d,
              qTs, syTs, onTs, sy_t, on_t):
    T = sum(seqs)
    nbt = T // 512
    Smax = max(seqs)
    NKT = Smax // 128
    with ExitStack() as es:
        win = _sb(nc, es, "m_win", [128, NDC, 2560], BF16)
        diag = _sb(nc, es, "m_diag", [128, 4, CONV_K, 128], BF16)
        kT = _sb(nc, es, "m_kT", [128, 4, Smax], BF16)
        V = _sb(nc, es, "m_V", [128, NKT, 4, 130], BF16)
        cst = _sb(nc, es, "m_cst", [128, 128 + 128 + 4 * 31 + 12 + 128 + 8], F32)
        ropet = _sb(nc, es, "m_rope", [128, MAXT, 32], F32)
        lamt = _sb(nc, es, "m_lam", [128, 4 * 64 + 16], F32)
        identb = _sb(nc, es, "m_idb", [128, 128], BF16)
        if DBG.get("pad"):
            _sb(nc, es, "m_pad", [128, 1024], F32)
        banks = [es.enter_context(nc.psum_tensor("m_bk%d" % i, [128, 512], F32)) for i in range(8)]
        bank_t = [Tile("bank", True) for _ in range(8)]
        win_t, diag_t, cst_t, rope_t, lam_t, idb_t = Tile(), Tile(), Tile(), Tile(), Tile(), Tile()
        kT_t = [Tile() for _ in range(NKT)]
        V_t = [Tile() for _ in range(NKT)]
        qTs_t = [Tile() for _ in range(nbt)]

        ident = cst[:, 0:128]
        ones = cst[:, 128:256]
        cw = cst[:, 256:256 + 124]
        cp = cst[:, 380:392]
        gsub = cst[:, 392:520]
        mhalf = cst[:, 520:521]
        neglam = lamt[:, 256 + 8:256 + 9]

        k.dma(k.sp, ident, ident_d[:, :], cst_t, writes=[cst_t])
        k.dma(k.sp, cw, convw_d[:, :], cst_t, writes=[cst_t])
        k.dma(k.sp, cp, convp_d[:, :], cst_t, writes=[cst_t])
        k.dma(k.sp, gsub, subg_d[:, :], cst_t, writes=[cst_t])
        k.dma(k.sp, ropet[:, :, :], rope_d[:, :, :], rope_t, writes=[rope_t])
        k.dma(k.sp, lamt[:, 0:256], lamv_d[:, :], lam_t, writes=[lam_t])
        k.op(k.dve, lambda: nc.vector.memset(ones, 1.0), writes=[cst_t])
        k.op(k.dve, lambda: nc.vector.memset(mhalf, -0.5), writes=[cst_t])
        k.op(k.dve, lambda: nc.vector.tensor_copy(out=identb[:, :], in_=ident), reads=[cst_t], writes=[idb_t])
        k.op(k.dve, lambda: nc.vector.tensor_scalar(out=gsub, in0=gsub, scalar1=float(1.0 - LAMBDA_INIT), scalar2=None,
                                                    op0=ALU.mult), reads=[cst_t], writes=[cst_t])
        k.op(k.dve, lambda: nc.vector.tensor_tensor(out=lamt[:, 0:64], in0=lamt[:, 0:64], in1=lamt[:, 64:128], op=ALU.mult),
             reads=[lam_t], writes=[lam_t])
        k.op(k.dve, lambda: nc.vector.tensor_tensor(out=lamt[:, 128:192], in0=lamt[:, 128:192], in1=lamt[:, 192:256], op=ALU.mult),
             reads=[lam_t], writes=[lam_t])
        k.op(k.dve, lambda: nc.vector.reduce_sum(out=lamt[:, 256:257], in_=lamt[:, 0:64], axis=mybir.AxisListType.X),
             reads=[lam_t], writes=[lam_t])
        k.op(k.dve, lambda: nc.vector.reduce_sum(out=lamt[:, 257:258], in_=lamt[:, 128:192], axis=mybir.AxisListType.X),
             reads=[lam_t], writes=[lam_t])
        k.op(k.act, lambda: nc.scalar.activation(out=lamt[:, 258:260], in_=lamt[:, 256:258], func=AF.Exp),
             reads=[lam_t], writes=[lam_t])
        k.op(k.dve, lambda: nc.vector.scalar_tensor_tensor(out=neglam, in0=lamt[:, 259:260], scalar=float(-LAMBDA_INIT),
                                                           in1=lamt[:, 258:259], op0=ALU.add, op1=ALU.subtract),
             reads=[lam_t], writes=[lam_t])
        for c in range(4):
            for j in range(CONV_K):
                k.op(k.dve, lambda c=c, j=j: nc.vector.tensor_scalar(
                    out=diag[:, c, j, :], in0=ident, scalar1=cw[:, c * CONV_K + j:c * CONV_K + j + 1], scalar2=None,
                    op0=ALU.mult), reads=[cst_t], writes=[diag_t])
        k.op(k.pool, lambda: nc.gpsimd.memset(V[:, :, :, 128:130], 1.0), writes=V_t)
        load_weight(k, win, win_t, w_in, D, 2560)

        def conv_block(gb, slot):
            if DBG.get("noconv"):
                return
            us = usl[slot]
            S1, S2 = 6, 7
            for c in range(4):
                YC = 4 + (c % 2)
                for j in range(CONV_K):
                    k.op(k.pe, lambda c=c, j=j, YC=YC: nc.tensor.matmul(
                        banks[YC][:, :], lhsT=diag[:, c, j, :], rhs=us[:, c, 1 + j:1 + j + 512],
                        start=(j == 0), stop=(j == CONV_K - 1)),
                        reads=[diag_t, usl_t[slot]], writes=[bank_t[YC]], inc=(j == CONV_K - 1))
                k.op(k.act, lambda c=c, YC=YC: nc.scalar.activation(
                    out=ysb[:, c, :], in_=banks[YC][:, :], func=AF.Identity, scale=0.5, bias=cp[:, c:c + 1]),
                    reads=[bank_t[YC], cst_t], writes=[ysb_t[c]])
                k.op(k.pool, lambda c=c: nc.gpsimd.tensor_tensor(out=ysq[:, c % 2, :], in0=ysb[:, c, :], in1=ysb[:, c, :], op=ALU.mult),
                     reads=[ysb_t[c]], writes=[ysq_t[c % 2]])
                k.op(k.pe, lambda c=c: nc.tensor.matmul(banks[S1][:, :], lhsT=ones, rhs=ysb[:, c, :], start=(c == 0), stop=(c == 3)),
                     reads=[cst_t, ysb_t[c]], writes=[bank_t[S1]], inc=True)
                k.op(k.pe, lambda c=c: nc.tensor.matmul(banks[S2][:, :], lhsT=ones, rhs=ysq[:, c % 2, :], start=(c == 0), stop=(c == 3)),
                     reads=[cst_t, ysq_t[c % 2]], writes=[bank_t[S2]], inc=True)
            mean, tmpv = cstat[:, 0, :], cstat[:, 1, :]
            rstd = tmpv
            k.op(k.dve, lambda: nc.vector.tensor_scalar(out=mean, in0=banks[S1][:, :], scalar1=1.0 / 512, scalar2=None, op0=ALU.mult),
                 reads=[bank_t[S1]], writes=[cstat_t])
            k.op(k.dve, lambda: nc.vector.tensor_tensor(out=tmpv, in0=mean, in1=mean, op=ALU.mult), reads=[cstat_t], writes=[cstat_t])
            k.op(k.dve, lambda: nc.vector.scalar_tensor_tensor(out=tmpv, in0=banks[S2][:, :], scalar=1.0 / 512, in1=tmpv,
                                                               op0=ALU.mult, op1=ALU.subtract),
                 reads=[bank_t[S2], cstat_t], writes=[cstat_t])
            k.op(k.dve, lambda: nc.vector.tensor_scalar(out=tmpv, in0=tmpv, scalar1=float(EPS), scalar2=None, op0=ALU.add),
                 reads=[cstat_t], writes=[cstat_t])
            k.op(k.act, lambda: nc.scalar.activation(out=tmpv, in_=tmpv, func=AF.Sqrt), reads=[cstat_t], writes=[cstat_t])
            k.op(k.dve, lambda: nc.vector.reciprocal(out=tmpv, in_=tmpv), reads=[cstat_t], writes=[cstat_t])
            so, so_t = syT[0], syT_t[0]
            for c in range(4):
                k.op(k.dve, lambda c=c: nc.vector.tensor_tensor(out=ysb[:, c, :], in0=ysb[:, c, :], in1=mean, op=ALU.subtract),
                     reads=[ysb_t[c], cstat_t], writes=[ysb_t[c]])
                k.op(k.pool, lambda c=c: nc.gpsimd.tensor_tensor(out=ysb[:, c, :], in0=ysb[:, c, :], in1=rstd, op=ALU.mult),
                     reads=[ysb_t[c], cstat_t], writes=[ysb_t[c]])
                k.op(k.act, lambda c=c: nc.scalar.activation(out=so[:, c, :], in_=ysb[:, c, :], func=AF.Silu,
                                                             scale=cp[:, 4 + c:5 + c], bias=cp[:, 8 + c:9 + c]),
                     reads=[ysb_t[c], cst_t], writes=[so_t])
            k.dma(k.sp, syTs[gb], so[:, :, :], so_t, reads=[so_t], writes=[sy_t[gb]])

        def rope_and_T(src_bank, dst, dst_tiles, dcol, tt, par):
            Q3 = banks[src_bank][:, :].rearrange("p (g d) -> p g d", g=8)
            q_ = qr[par]
            q3 = q_[:, :].rearrange("p (g d) -> p g d", g=8)
            t1 = rtmp[par][:, 0, :, :]
            t2 = rtmp[par][:, 1, :, :]
            cc = ropet[:, tt, 0:16].unsqueeze(1).to_broadcast([128, 8, 16])
            nsa = ropet[:, tt, 16:24].unsqueeze(1).to_broadcast([128, 8, 8])
            nsb = ropet[:, tt, 24:32].unsqueeze(1).to_broadcast([128, 8, 8])
            k.op(k.act, lambda: nc.scalar.copy(out=q3[:, :, 16:64], in_=Q3[:, :, 16:64]),
                 reads=[bank_t[src_bank]], writes=[qr_t[par]])
            k.op(k.dve, lambda: nc.vector.tensor_tensor(out=t1, in0=Q3[:, :, 0:16], in1=cc, op=ALU.mult),
                 reads=[bank_t[src_bank], rope_t], writes=[rtmp_t[par]])
            k.op(k.dve, lambda: nc.vector.tensor_tensor(out=t2[:, :, 0:8], in0=Q3[:, :, 8:16], in1=nsa, op=ALU.mult),
                 reads=[bank_t[src_bank], rope_t], writes=[rtmp_t[par]])
            k.op(k.dve, lambda: nc.vector.tensor_tensor(out=t2[:, :, 8:16], in0=Q3[:, :, 0:8], in1=nsb, op=ALU.mult),
                 reads=[bank_t[src_bank], rope_t], writes=[rtmp_t[par]])
            k.op(k.dve, lambda: nc.vector.tensor_tensor(out=q3[:, :, 0:16], in0=t1, in1=t2, op=ALU.add),
                 reads=[rtmp_t[par]], writes=[qr_t[par]])
            PT = banks[src_bank][:, :].bitcast(BF16)
            for h in range(4):
                k.op(k.pe, lambda h=h: nc.tensor.transpose(out=PT[:, h * 128:(h + 1) * 128], in_=q_[:, h * 128:(h + 1) * 128],
                                                           identity=identb[:, :]),
                     reads=[qr_t[par], idb_t], writes=[bank_t[src_bank]], inc=(h == 3))
            k.op(k.dve, lambda: nc.vector.tensor_copy(out=dst[:, :, dcol:dcol + 128],
                                                      in_=PT[:, 0:512].rearrange("p (h t) -> p h t", h=4)),
                 reads=[bank_t[src_bank]], writes=dst_tiles)

        gb0 = 0
        tok0 = 0
        for si, S in enumerate(seqs):
            nb = S // 512
            nkt = S // 128
            esa = ExitStack()
            usl = [_sb(nc, esa, "m%d_u%d" % (si, i), [128, 4, 544], BF16) for i in range(2)]
            xt = [_sb(nc, esa, "m%d_xt%d" % (si, i), [128, D], F32) for i in range(4)]
            xT = _sb(nc, esa, "m%d_xT" % si, [128, NDC, 512], BF16)
            qr = [_sb(nc, esa, "m%d_qr%d" % (si, i), [128, 512], BF16) for i in range(2)]
            rtmp = [_sb(nc, esa, "m%d_rt%d" % (si, i), [128, 2, 8, 16], F32) for i in range(2)]
            qTb = [_sb(nc, esa, "m%d_qT%d" % (si, i), [128, 4, 512], BF16) for i in range(1)]
            sig = [_sb(nc, esa, "m%d_sig%d" % (si, i), [128, 512], F32) for i in range(1)]
            ysb = _sb(nc, esa, "m%d_ysb" % si, [128, 4, 512], F32)
            ysq = _sb(nc, esa, "m%d_ysq" % si, [128, 2, 512], F32)
            cstat = _sb(nc, esa, "m%d_cstat" % si, [128, 2, 512], F32)
            syT = [_sb(nc, esa, "m%d_syT%d" % (si, i), [128, 4, 512], BF16) for i in range(1)]
            usl_t = [Tile(), Tile()]
            xt_t = [Tile() for _ in range(4)]
            xT_t = Tile()
            qr_t = [Tile(), Tile()]
            rtmp_t = [Tile(), Tile()]
            qTb_t = [Tile()]
            sig_t = [Tile()]
            ysb_t = [Tile() for _ in range(4)]
            ysq_t = [Tile() for _ in range(2)]
            cstat_t = Tile()
            syT_t = [Tile()]
            for b in range(nb):
                gb = gb0 + b
                for j in range(4):
                    r0 = tok0 + (b * 4 + j) * 128
                    k.dma(k.sp, xt[j][:, :], x1s[r0:r0 + 128, :], xt_t[j], reads=[x1_t[gb]], writes=[xt_t[j]])
                transpose_block(k, nc, xt, xt_t, [0, 1, 2, 3], ident, cst_t, banks, bank_t, xT, xT_t, gb)
                qo, qo_t = qTb[0], qTb_t[0]
                for j in range(4):
                    kt = b * 4 + j
                    QB, KB, VB = 0 + (j % 2), 2 + (j % 2), 4 + (j % 2)
                    for dc in range(NDC):
                        for (bk, c0) in ((QB, 0), (KB, 512), (VB, 1024)):
                            k.op(k.pe, lambda dc=dc, bk=bk, c0=c0, j=j: nc.tensor.matmul(
                                banks[bk][:, :], lhsT=xT[:, dc, j * 128:(j + 1) * 128], rhs=win[:, dc, c0:c0 + 512],
                                start=(dc == 0), stop=(dc == NDC - 1)),
                                reads=[xT_t, win_t], writes=[bank_t[bk]], inc=(dc == NDC - 1))
                    k.op(k.act, lambda VB=VB, kt=kt: nc.scalar.copy(
                        out=V[:, kt, :, 0:128], in_=banks[VB][:, :].rearrange("p (h e) -> p h e", h=4)),
                        reads=[bank_t[VB]], writes=[V_t[kt]])
                    rope_and_T(QB, qo, [qo_t], j * 128, kt, 0)
                    rope_and_T(KB, kT, [kT_t[kt]], kt * 128, kt, 1)
                k.dma(k.sp, qTs[gb], qo[:, :, :], qo_t, reads=[qo_t], writes=[qTs_t[gb]])
                slot = b % 2
                us = usl[slot]
                if b == 0:
                    k.op(k.pool, lambda us=us: nc.gpsimd.memset(us[:, :, 0:16], 0.0), writes=[usl_t[slot]])
                for c in range(4):
                    ZA, ZB = 6, 7
                    for dc in range(NDC):
                        k.op(k.pe, lambda dc=dc, c=c: nc.tensor.matmul(
                            banks[ZA][:, :], lhsT=win[:, dc, 1536 + c * 128:1536 + (c + 1) * 128], rhs=xT[:, dc, :],
                            start=(dc == 0), stop=(dc == NDC - 1)), reads=[win_t, xT_t], writes=[bank_t[ZA]], inc=(dc == NDC - 1))
                    for dc in range(NDC):
                        k.op(k.pe, lambda dc=dc, c=c: nc.tensor.matmul(
                            banks[ZB][:, :], lhsT=win[:, dc, 2048 + c * 128:2048 + (c + 1) * 128], rhs=xT[:, dc, :],
                            start=(dc == 0), stop=(dc == NDC - 1)), reads=[win_t, xT_t], writes=[bank_t[ZB]], inc=(dc == NDC - 1))
                    k.op(k.act, lambda c=c: nc.scalar.activation(out=sig[0][:, :], in_=banks[ZB][:, :], func=AF.Tanh, scale=0.5),
                         reads=[bank_t[ZB]], writes=[sig_t[0]])
                    k.op(k.dve, lambda c=c, us=us: nc.vector.scalar_tensor_tensor(
                        out=us[:, c, 16:528], in0=sig[0][:, :], scalar=1.0, in1=banks[ZA][:, :], op0=ALU.add, op1=ALU.mult),
                        reads=[sig_t[0], bank_t[ZA]], writes=[usl_t[slot]])
                if b > 0:
                    ps_ = usl[1 - slot]
                    k.op(k.pool, lambda us=us, ps_=ps_: nc.gpsimd.tensor_copy(out=ps_[:, :, 528:544], in_=us[:, :, 16:32]),
                         reads=[usl_t[slot]], writes=[usl_t[1 - slot]])
                    conv_block(gb - 1, 1 - slot)
                if b + 1 < nb:
                    ns_ = usl[1 - slot]
                    k.op(k.pool, lambda us=us, ns_=ns_: nc.gpsimd.tensor_copy(out=ns_[:, :, 0:16], in_=us[:, :, 512:528]),
                         reads=[usl_t[slot]], writes=[usl_t[1 - slot]])
                else:
                    k.op(k.pool, lambda us=us: nc.gpsimd.memset(us[:, :, 528:544], 0.0), writes=[usl_t[slot]])
                    conv_block(gb, slot)
            k.barrier()
            esa.close()
            esb = ExitStack()
            if DBG.get("noattn"):
                nb = 0
            qTb = [_sb(nc, esb, "m%d_qTi%d" % (si, i), [128, 4, 512], BF16) for i in range(2)]
            ET = [_sb(nc, esb, "m%d_E%d" % (si, i), [128, 2, 512], BF16) for i in range(2)]
            osb = [_sb(nc, esb, "m%d_osb%d" % (si, i), [128, 2, 128], F32) for i in range(4)]
            ost = [_sb(nc, esb, "m%d_ost%d" % (si, i), [128, 8], F32) for i in range(4)]
            onb = _sb(nc, esb, "m%d_on" % si, [128, 4, 512], BF16)
            onT = [_sb(nc, esb, "m%d_onT%d" % (si, i), [128, 4, 512], BF16) for i in range(2)]
            qTb_t = [Tile(), Tile()]
            ET_t = [Tile(), Tile()]
            osb_t = [Tile() for _ in range(4)]
            ost_t = [Tile() for _ in range(4)]
            onb_t = Tile()
            onT_t = [Tile(), Tile()]
            for b in range(nb):
                gb = gb0 + b
                qi, qi_t = qTb[gb % 2], qTb_t[gb % 2]
                if not DBG.get("noqiload"):
                    k.dma(k.sp, qi[:, :, :], qTs[gb], qi_t, reads=[qTs_t[gb]], writes=[qi_t])
                for h in range(4):
                    for kt in range(nkt):
                        par = kt % 2
                        ST = [par * 2, par * 2 + 1]
                        for m in range(0 if DBG.get("noqk") else (1 if DBG.get("noqk1") else 2)):
                            k.op(k.pe, lambda m=m, h=h, kt=kt, ST=ST: nc.tensor.matmul(
                                banks[ST[m]][:, :], lhsT=kT[m * 64:(m + 1) * 64, h, kt * 128:(kt + 1) * 128],
                                rhs=qi[m * 64:(m + 1) * 64, h, :], start=True, stop=True),
                                reads=[kT_t[kt], qi_t], writes=[bank_t[ST[m]]], inc=True)
                        for m in range(2 if not DBG.get("noexp") else 0):
                            k.op(k.act, lambda m=m, par=par, ST=ST: nc.scalar.activation(
                                out=ET[par][:, m, :], in_=banks[ST[m]][:, :], func=AF.Exp, scale=0.125),
                                reads=[bank_t[ST[m]]], writes=[ET_t[par]])
                        for j in range(4 if not DBG.get("nopv") else 0):
                            for m in range(2):
                                k.op(k.pe, lambda m=m, j=j, h=h, kt=kt, par=par: nc.tensor.matmul(
                                    banks[4 + j][:, m * 256:m * 256 + 129], lhsT=ET[par][:, m, j * 128:(j + 1) * 128],
                                    rhs=V[:, kt, h, 0:129], start=(kt == 0 and m == 0), stop=(kt == nkt - 1 and m == 1),
                                    skip_group_check=True),
                                    reads=[ET_t[par], V_t[kt]], writes=[bank_t[4 + j]], inc=(m == 1))
                    for j in range(4 if not DBG.get("noepi") else 0):
                        acc = banks[4 + j]
                        st_, st_t = ost[j], ost_t[j]
                        k.op(k.dve, lambda acc=acc, st_=st_: nc.vector.reciprocal(
                            out=st_[:, 0:2], in_=acc[:, :].rearrange("p (m c) -> p m c", m=2)[:, :, 128]),
                            reads=[bank_t[4 + j]], writes=[st_t])
                        k.op(k.dve, lambda st_=st_: nc.vector.tensor_tensor(out=st_[:, 2:3], in0=st_[:, 1:2], in1=neglam, op=ALU.mult),
                             reads=[st_t, lam_t], writes=[st_t])
                        k.op(k.dve, lambda acc=acc, st_=st_, j=j: nc.vector.tensor_scalar(
                            out=osb[j][:, 0, :], in0=acc[:, 256:384], scalar1=st_[:, 2:3], scalar2=None, op0=ALU.mult),
                            reads=[bank_t[4 + j], st_t], writes=[osb_t[j]])
                        k.op(k.dve, lambda acc=acc, st_=st_, j=j: nc.vector.scalar_tensor_tensor(
                            out=osb[j][:, 0, :], in0=acc[:, 0:128], scalar=st_[:, 0:1], in1=osb[j][:, 0, :], op0=ALU.mult, op1=ALU.add),
                            reads=[bank_t[4 + j], st_t, osb_t[j]], writes=[osb_t[j]])
                    for j in range(4 if not DBG.get("noepi2") else 0):
                        st_, st_t = ost[j], ost_t[j]
                        k.op(k.dve, lambda j=j: nc.vector.tensor_tensor(
                            out=osb[j][:, 1, :], in0=osb[j][:, 0, :], in1=osb[j][:, 0, :], op=ALU.mult),
                            reads=[osb_t[j]], writes=[osb_t[j]])
                        k.op(k.dve, lambda st_=st_, j=j: nc.vector.reduce_sum(
                            out=st_[:, 3:4], in_=osb[j][:, 1, :], axis=mybir.AxisListType.X),
                            reads=[osb_t[j]], writes=[st_t])
                        k.op(k.dve, lambda st_=st_: nc.vector.tensor_scalar(
                            out=st_[:, 4:5], in0=st_[:, 3:4], scalar1=1.0 / 128, scalar2=float(EPS), op0=ALU.mult, op1=ALU.add),
                            reads=[st_t], writes=[st_t])
                        k.op(k.pool, lambda st_=st_: nc.gpsimd.tensor_tensor(out=st_[:, 5:6], in0=st_[:, 4:5], in1=mhalf, op=ALU.pow),
                             reads=[st_t, cst_t], writes=[st_t])
                        k.op(k.dve, lambda st_=st_, j=j, h=h: nc.vector.scalar_tensor_tensor(
                            out=onb[:, j, h * 128:(h + 1) * 128], in0=osb[j][:, 0, :], scalar=st_[:, 5:6], in1=gsub,
                            op0=ALU.mult, op1=ALU.mult), reads=[osb_t[j], st_t, cst_t], writes=[onb_t])
                oo, oo_t = onT[gb % 2], onT_t[gb % 2]
                for h in range(4 if not DBG.get("noont") else 0):
                    bk = h
                    PT = banks[bk][:, :].bitcast(BF16)
                    for j in range(4):
                        k.op(k.pe, lambda h=h, j=j, PT=PT: nc.tensor.transpose(
                            out=PT[:, j * 128:(j + 1) * 128], in_=onb[:, j, h * 128:(h + 1) * 128], identity=identb[:, :]),
                            reads=[onb_t, idb_t], writes=[bank_t[bk]], inc=(j == 3))
                    if h % 2 == 0:
                        k.op(k.dve, lambda h=h, PT=PT: nc.vector.tensor_copy(out=oo[:, h, :], in_=PT[:, 0:512]),
                             reads=[bank_t[bk]], writes=[oo_t])
                    else:
                        k.op(k.act, lambda h=h, PT=PT: nc.scalar.copy(out=oo[:, h, :], in_=PT[:, 0:512]),
                             reads=[bank_t[bk]], writes=[oo_t])
                if not DBG.get("noonstore"):
                    k.dma(k.sp, onTs[gb], oo[:, :, :], oo_t, reads=[oo_t], writes=[on_t[gb]])
            k.barrier()
            esb.close()
            nb = S // 512
            gb0 += nb
            tok0 += S
        k.barrier()


def merge_phase(k, nc, T, x1s, x1_t, w_in, w_att, w_cp, w_out, lnp, bgate_d, ident_d, syTs, onTs, sy_t, on_t, x2s, x2_t):
    nb = T // 512
    with ExitStack() as es:
        wgt = _sb(nc, es, "g_wgt", [128, NDC, 2048], BF16)
        watt = _sb(nc, es, "g_watt", [128, 4, D], BF16)
        wcp = _sb(nc, es, "g_wcp", [128, 4, D], BF16)
        wout = _sb(nc, es, "g_wout", [128, NDC, D], BF16)
        cst = _sb(nc, es, "g_cst", [128, 2 * D + 128 + 16 + 8], F32)
        xt = [_sb(nc, es, "g_xt%d" % i, [128, D], F32) for i in range(8)]
        xT = [_sb(nc, es, "g_xT%d" % i, [128, NDC, 512], BF16) for i in range(2)]
        onT = [_sb(nc, es, "g_onT%d" % i, [128, 4, 512], BF16) for i in range(2)]
        syT = [_sb(nc, es, "g_syT%d" % i, [128, 4, 512], BF16) for i in range(2)]
        sga = [_sb(nc, es, "g_sga%d" % i, [128, 512], F32) for i in range(2)]
        sgc = [_sb(nc, es, "g_sgc%d" % i, [128, 512], F32) for i in range(2)]
        m1 = [_sb(nc, es, "g_m1%d" % i, [128, 512], F32) for i in range(2)]
        m2 = [_sb(nc, es, "g_m2%d" % i, [128, 512], F32) for i in range(2)]
        mT = _sb(nc, es, "g_mT", [128, NDC, 512], BF16)
        stat = [_sb(nc, es, "g_st%d" % i, [128, 32], F32) for i in range(4)]
        banks = [es.enter_context(nc.psum_tensor("g_bk%d" % i, [128, 512], F32)) for i in range(8)]
        bank_t = [Tile("bank", True) for _ in range(8)]
        wgt_t, watt_t, wcp_t, wout_t, cst_t = Tile(), Tile(), Tile(), Tile(), Tile()
        xt_t = [Tile() for _ in range(8)]
        xT_t = [Tile(), Tile()]
        onT_t = [Tile(), Tile()]
        syT_t = [Tile(), Tile()]
        sga_t = [Tile(), Tile()]
        sgc_t = [Tile(), Tile()]
        m1_t = [Tile(), Tile()]
        m2_t = [Tile(), Tile()]
        mT_t = [Tile() for _ in range(NDC)]
        stat_t = [Tile() for _ in range(4)]
        gB = cst[:, 0:D]
        bB = cst[:, D:2 * D]
        ident = cst[:, 2 * D:2 * D + 128]
        bg = cst[:, 2 * D + 128:2 * D + 144]
        mhalf = cst[:, 2 * D + 144:2 * D + 145]
        k.dma(k.sp, gB, lnp[:, 2, :], cst_t, writes=[cst_t])
        k.dma(k.sp, bB, lnp[:, 3, :], cst_t, writes=[cst_t])
        k.dma(k.sp, ident, ident_d[:, :], cst_t, writes=[cst_t])
        k.dma(k.sp, bg, bgate_d[:, :], cst_t, writes=[cst_t])
        k.op(k.dve, lambda: nc.vector.memset(mhalf, -0.5), writes=[cst_t])

        def load_blk(b):
            for j in range(4):
                ti = (b * 4 + j) % 8
                r0 = (b * 4 + j) * 128
                k.dma(k.sp, xt[ti][:, :], x1s[r0:r0 + 128, :], xt_t[ti], reads=[x1_t[b]], writes=[xt_t[ti]])
            k.dma(k.sp, onT[b % 2][:, :, :], onTs[b], onT_t[b % 2], reads=[on_t[b]], writes=[onT_t[b % 2]])
            k.dma(k.sp, syT[b % 2][:, :, :], syTs[b], syT_t[b % 2], reads=[sy_t[b]], writes=[syT_t[b % 2]])

        load_blk(0)
        load_weight(k, wgt, wgt_t, w_in, D, 2048, col0=2560)
        load_weight(k, watt, watt_t, w_att, 512, D)
        load_weight(k, wcp, wcp_t, w_cp, 512, D)
        load_weight(k, wout, wout_t, w_out, D, D)
        for b in range(nb):
            if b + 1 < nb:
                load_blk(b + 1)
            tiles = [(b * 4 + j) % 8 for j in range(4)]
            xTb, xTb_t = xT[b % 2], xT_t[b % 2]
            on_, on__t = onT[b % 2], onT_t[b % 2]
            sy_, sy__t = syT[b % 2], syT_t[b % 2]
            transpose_block(k, nc, xt, xt_t, tiles, ident, cst_t, banks, bank_t, xTb, xTb_t, b)
            for dm in range(NDC):
                GA, GC, AT, CT = 0, 1, 2, 3
                p = dm % 2
                for dc in range(NDC):
                    k.op(k.pe, lambda dc=dc, dm=dm: nc.tensor.matmul(
                        banks[GA][:, :], lhsT=wgt[:, dc, dm * 128:(dm + 1) * 128], rhs=xTb[:, dc, :],
                        start=(dc == 0), stop=(dc == NDC - 1)), reads=[wgt_t, xTb_t], writes=[bank_t[GA]], inc=(dc == NDC - 1))
                for dc in range(NDC):
                    k.op(k.pe, lambda dc=dc, dm=dm: nc.tensor.matmul(
                        banks[GC][:, :], lhsT=wgt[:, dc, 1024 + dm * 128:1024 + (dm + 1) * 128], rhs=xTb[:, dc, :],
                        start=(dc == 0), stop=(dc == NDC - 1)), reads=[wgt_t, xTb_t], writes=[bank_t[GC]], inc=(dc == NDC - 1))
                for h in range(4):
                    k.op(k.pe, lambda h=h, dm=dm: nc.tensor.matmul(
                        banks[AT][:, :], lhsT=watt[:, h, dm * 128:(dm + 1) * 128], rhs=on_[:, h, :],
                        start=(h == 0), stop=(h == 3)), reads=[watt_t, on__t], writes=[bank_t[AT]], inc=(h == 3))
                for c in range(4):
                    k.op(k.pe, lambda c=c, dm=dm: nc.tensor.matmul(
                        banks[CT][:, :], lhsT=wcp[:, c, dm * 128:(dm + 1) * 128], rhs=sy_[:, c, :],
                        start=(c == 0), stop=(c == 3)), reads=[wcp_t, sy__t], writes=[bank_t[CT]], inc=(c == 3))
                k.op(k.act, lambda dm=dm, p=p: nc.scalar.activation(out=sga[p][:, :], in_=banks[GA][:, :], func=AF.Sigmoid,
                                                                   bias=bg[:, dm:dm + 1]),
                     reads=[bank_t[GA], cst_t], writes=[sga_t[p]])
                k.op(k.act, lambda dm=dm, p=p: nc.scalar.activation(out=sgc[p][:, :], in_=banks[GC][:, :], func=AF.Sigmoid,
                                                                   bias=bg[:, 8 + dm:9 + dm]),
                     reads=[bank_t[GC], cst_t], writes=[sgc_t[p]])
                k.op(k.dve, lambda p=p: nc.vector.tensor_tensor(out=m1[p][:, :], in0=banks[AT][:, :], in1=sga[p][:, :], op=ALU.mult),
                     reads=[bank_t[AT], sga_t[p]], writes=[m1_t[p]])
                k.op(k.dve, lambda p=p: nc.vector.tensor_tensor(out=m2[p][:, :], in0=banks[CT][:, :], in1=sgc[p][:, :], op=ALU.mult),
                     reads=[bank_t[CT], sgc_t[p]], writes=[m2_t[p]])
                k.op(k.pool, lambda p=p, dm=dm: nc.gpsimd.tensor_tensor(out=mT[:, dm, :], in0=m1[p][:, :], in1=m2[p][:, :], op=ALU.add),
                     reads=[m1_t[p], m2_t[p]], writes=[mT_t[dm]])
            for j in range(4):
                Yb = [4 + (j % 2) * 2, 5 + (j % 2) * 2]
                for h in range(2):
                    for dm in range(NDC):
                        k.op(k.pe, lambda dm=dm, h=h, j=j, Yb=Yb: nc.tensor.matmul(
                            banks[Yb[h]][:, :], lhsT=mT[:, dm, j * 128:(j + 1) * 128], rhs=wout[:, dm, h * 512:(h + 1) * 512],
                            start=(dm == 0), stop=(dm == NDC - 1)),
                            reads=[mT_t[dm], wout_t], writes=[bank_t[Yb[h]]], inc=(dm == NDC - 1))
                ti = tiles[j]
                r0 = (b * 4 + j) * 128
                ln_epilogue(k, nc, xt[ti], xt_t[ti], [banks[Yb[0]], banks[Yb[1]]], [bank_t[Yb[0]], bank_t[Yb[1]]],
                            ALPHA, EPS, gB, bB, cst_t, stat[j], stat_t[j], mhalf, x2s[r0:r0 + 128, :], x2_t[b])
        k.barrier()


def build(seqs, phases=(1, 2, 3, 4)):
    T = sum(seqs)
    nb = T // 512
    nc = bass.Bass("TRN2", target_bir_lowering=False)

    def din(name, shape, dt=F32):
        return nc.dram_tensor(name, shape, dt, kind="ExternalInput").ap()

    x = din("x", [T, D])
    wg1, wu1, wd1 = din("wg1", [D, FF]), din("wu1", [D, FF]), din("wd1", [FF, D])
    wg2, wu2, wd2 = din("wg2", [D, FF]), din("wu2", [D, FF]), din("wd2", [FF, D])
    w_in = din("w_in", [D, 4608])
    w_att, w_cp, w_out = din("w_att", [512, D]), din("w_cp", [512, D]), din("w_out", [D, D])
    lnp = din("lnp", [128, 6, D])
    rope_d = din("rope", [128, MAXT, 32])
    convw_d = din("convw", [128, 124])
    convp_d = din("convp", [128, 12])
    bgate_d = din("bgate", [128, 16])
    subg_d = din("subg", [128, 128])
    lamv_d = din("lamv", [128, 256])
    ident_d = din("ident", [128, 128])
    y = nc.dram_tensor("y", [T, D], F32, kind="ExternalOutput").ap()
    qTs = nc.dram_tensor("qTs", [nb, 128, 4, 512], BF16, kind="Internal").ap()
    syTs = nc.dram_tensor("syTs", [nb, 128, 4, 512], BF16, kind="Internal").ap()
    onTs = nc.dram_tensor("onTs", [nb, 128, 4, 512], BF16, kind="Internal").ap()
    with ExitStack() as es:
        k = K(nc, es)
        x_t = [Tile() for _ in range(nb)]
        x1_t = [Tile() for _ in range(nb)]
        x2_t = [Tile() for _ in range(nb)]
        y_t = [Tile() for _ in range(nb)]
        sy_t = [Tile() for _ in range(nb)]
        on_t = [Tile() for _ in range(nb)]
        if 1 in phases:
            ffn_phase(k, nc, T, x, x_t, wg1, wu1, wd1, lnp, 0, ident_d, y, y_t, "a_")
        if 2 in phases:
            mix_phase(k, nc, seqs, y, y_t, w_in, rope_d, convw_d, convp_d, subg_d, lamv_d, ident_d,
                      qTs, syTs, onTs, sy_t, on_t)
        if 3 in phases:
            merge_phase(k, nc, T, y, y_t, w_in, w_att, w_cp, w_out, lnp, bgate_d, ident_d, syTs, onTs, sy_t, on_t, y, y_t)
        if 4 in phases:
            ffn_phase(k, nc, T, y, y_t, wg2, wu2, wd2, lnp, 2, ident_d, y, y_t, "b_")
        if DBG.get("dmapad"):
            with ExitStack() as es2:
                pa = _sb(nc, es2, "dpad_a", [128, D], F32)
                ta = Tile()
                for i in range(400):
                    k.dma(k.sp, pa[:, :], x[(i % 8) * 128:(i % 8 + 1) * 128, :], ta, writes=[ta])
                k.barrier()
        if DBG.get("pepad"):
            with ExitStack() as es2:
                pa = _sb(nc, es2, "pad_a", [128, 128], BF16)
                pp = es2.enter_context(nc.psum_tensor("pad_p", [128, 128], F32))
                ta, tp = Tile(), Tile()
                k.op(k.dve, lambda: nc.vector.memset(pa[:, :], 0.0), writes=[ta])
                for i in range(20000):
                    k.op(k.pe, lambda: nc.tensor.matmul(pp[:, :], lhsT=pa[:, :], rhs=pa[:, :], start=True, stop=True),
                         reads=[ta], writes=[tp], inc=(i == 19999))
                k.barrier()
        k.barrier()
    return nc


def rope_table():
    inv = (np.float32(500000.0) ** (-(np.arange(0, 16, 2, dtype=np.float32)) / np.float32(16))).astype(np.float32)
    pos = np.arange(MAXT * 128, dtype=np.float32)
    ang = (pos[:, None] * inv[None, :]).astype(np.float32)
    cos = np.cos(ang).astype(np.float32)
    sin = np.sin(ang).astype(np.float32)
    tab = np.concatenate([cos, cos, -sin, sin], axis=1)
    return np.ascontiguousarray(tab.reshape(MAXT, 128, 32).transpose(1, 0, 2))


def prep_shared(inp):
    f = lambda a: np.ascontiguousarray(np.asarray(a, dtype=np.float32))
    sh = {}
    sh["wg1"], sh["wu1"], sh["wd1"] = f(inp["ffn1_w_gate"][0]), f(inp["ffn1_w_up"][0]), f(inp["ffn1_w_down"][0])
    sh["wg2"], sh["wu2"], sh["wd2"] = f(inp["ffn2_w_gate"][0]), f(inp["ffn2_w_up"][0]), f(inp["ffn2_w_down"][0])
    sh["w_in"] = f(inp["w_in"][0])
    sh["w_att"], sh["w_cp"], sh["w_out"] = f(inp["w_att_proj"][0]), f(inp["w_conv_proj"][0]), f(inp["w_out"][0])
    lnv = np.stack([f(inp[n][0]) for n in ("ln1_g", "ln1_b", "ln2_g", "ln2_b", "ln3_g", "ln3_b")], axis=0)
    sh["lnp"] = np.ascontiguousarray(np.broadcast_to(lnv[None], (128, 6, D)))
    sh["rope"] = rope_table()
    cw = f(inp["conv_dw_w"][0])[:, 0, :]
    sh["convw"] = np.ascontiguousarray(cw.reshape(CONV_K, 4, 128).transpose(2, 1, 0).reshape(128, 124))
    cpv = np.stack([f(inp["conv_dw_b"][0]), f(inp["conv_ln_g"][0]), f(inp["conv_ln_b"][0])], axis=0)
    sh["convp"] = np.ascontiguousarray(cpv.reshape(3, 4, 128).transpose(2, 0, 1).reshape(128, 12))
    bgv = f(inp["b_gate"][0])
    sh["bgate"] = np.ascontiguousarray(bgv.reshape(2, 8, 128).transpose(2, 0, 1).reshape(128, 16))
    sh["subg"] = np.ascontiguousarray(np.broadcast_to(f(inp["subln_g"][0])[None], (128, 128)))
    lv = np.concatenate([f(inp[n][0]) for n in ("lambda_q1", "lambda_k1", "lambda_q2", "lambda_k2")])
    sh["lamv"] = np.ascontiguousarray(np.broadcast_to(lv[None], (128, 256)))
    sh["ident"] = np.eye(128, dtype=np.float32)
    return sh


SEQS = [4096, 2048, 2048, 2048, 2048]


def kernel(**inp):
    sh = prep_shared(inp)
    xp = np.asarray(inp["x_prompt"], dtype=np.float32)
    xs = np.asarray(inp["x_sample"], dtype=np.float32)
    nc = build(SEQS)
    in_maps = []
    for c in range(NCORES):
        xc = np.concatenate([xp[c].reshape(-1, D)] + [xs[4 * c + i].reshape(-1, D) for i in range(4)], axis=0)
        m = dict(sh)
        m["x"] = np.ascontiguousarray(xc)
        in_maps.append(m)
    res = run_bass_kernel_spmd(nc, in_maps, core_ids=list(range(NCORES)))
    yp = np.empty_like(xp)
    ys = np.empty_like(xs)
    for c in range(NCORES):
        yc = np.asarray(res.results[c]["y"], dtype=np.float32)
        yp[c] = yc[0:4096]
        for i in range(4):
            ys[4 * c + i] = yc[4096 + 2048 * i:4096 + 2048 * (i + 1)]
    return (yp, ys)
```

```python
import math
from contextlib import ExitStack

import numpy as np
import concourse.bass as bass
import concourse.mybir as mybir
from concourse.bass_utils import run_bass_kernel_spmd

F32 = mybir.dt.float32
BF16 = mybir.dt.bfloat16
AF = mybir.ActivationFunctionType
ALU = mybir.AluOpType

D = 1024
FF = 2816
NFC = FF // 128
NDC = D // 128
ALPHA = 2.0 ** 0.25
EPS = 1e-5
LAMBDA_INIT = 0.8 - 0.6 * math.exp(-0.3 * 0)
CONV_K = 31
NCORES = 8
MAXT = 32
NXT = 6
DBG = {}


class Sem:
    __slots__ = ("h", "cnt")

    def __init__(self, h):
        self.h = h
        self.cnt = 0


class Tile:
    __slots__ = ("w", "r", "dsem", "name", "psum")

    def __init__(self, name="", psum=False):
        self.w = {}
        self.r = {}
        self.dsem = None
        self.name = name
        self.psum = psum


class Eng:
    def __init__(self, h, sem, name, is_pe=False):
        self.h = h
        self.sem = sem
        self.name = name
        self.is_pe = is_pe
        self.waited = {}


class K:
    def __init__(self, nc, es):
        self.nc = nc
        self.es = es
        self.allsems = []
        self.freesems = []
        self.pe = Eng(nc.tensor, self._mksem("s_pe"), "pe", True)
        self.act = Eng(nc.scalar, self._mksem("s_act"), "act")
        self.dve = Eng(nc.vector, self._mksem("s_dve"), "dve")
        self.pool = Eng(nc.gpsimd, self._mksem("s_pool"), "pool")
        self.sp = Eng(nc.sync, self._mksem("s_sp"), "sp")
        self.engs = [self.pe, self.act, self.dve, self.pool, self.sp]
        self.phase_sems = []
        self.nsem = 0

    def _mksem(self, name):
        s = Sem(self.es.enter_context(self.nc.semaphore(name)))
        self.allsems.append(s)
        return s

    def dsem(self):
        if self.freesems:
            s = self.freesems.pop()
        else:
            self.nsem += 1
            s = self._mksem("s_d%d" % self.nsem)
        self.phase_sems.append(s)
        return s

    def _waits(self, eng, reads, writes):
        need = {}
        for t in reads:
            for s, v in t.w.items():
                if need.get(s, 0) < v:
                    need[s] = v
            if t.psum:
                for s, v in t.r.items():
                    if s is not eng.sem and need.get(s, 0) < v:
                        need[s] = v
        for t in writes:
            for s, v in t.w.items():
                if need.get(s, 0) < v:
                    need[s] = v
            for s, v in t.r.items():
                if need.get(s, 0) < v:
                    need[s] = v
        for s, v in need.items():
            if eng.is_pe and s is eng.sem:
                continue
            if eng.waited.get(s, 0) < v:
                eng.h.wait_ge(s.h, v)
                eng.waited[s] = v

    def op(self, eng, fn, reads=(), writes=(), inc=True):
        self._waits(eng, reads, writes)
        ins = fn()
        s = eng.sem
        if inc:
            s.cnt += 1
            ins.then_inc(s.h, 1)
            seq = s.cnt
        else:
            seq = s.cnt + 1
        for t in reads:
            if t.r.get(s, 0) < seq:
                t.r[s] = seq
        for t in writes:
            if t.w.get(s, 0) < seq:
                t.w[s] = seq
        return ins

    def dma(self, q, out, in_, st, reads=(), writes=()):
        self._waits(q, reads, writes)
        if st.dsem is None:
            st.dsem = self.dsem()
        s = st.dsem
        q.h.dma_start(out=out, in_=in_).then_inc(s.h, 16)
        s.cnt += 16
        for t in reads:
            t.r[s] = s.cnt
        for t in writes:
            t.w[s] = s.cnt

    def barrier(self):
        for e in self.engs:
            for s in self.allsems:
                if s.cnt > 0 and e.waited.get(s, 0) < s.cnt:
                    e.h.wait_ge(s.h, s.cnt)
                    e.waited[s] = s.cnt
        self.freesems.extend(self.phase_sems)
        self.phase_sems = []


def _sb(nc, es, name, shape, dt):
    return es.enter_context(nc.sbuf_tensor(name, shape, dt))


def load_weight(k, dst, dst_tile, src, rows, cols, col0=0):
    nchunk = rows // 128
    step = next(st for st in (2048, 1408, 1280, 1024, 512, 256, 128) if cols % st == 0)
    for rc in range(nchunk):
        for c0 in range(0, cols, step):
            k.dma(k.pool, dst[:, rc, c0:c0 + step], src[rc * 128:(rc + 1) * 128, col0 + c0:col0 + c0 + step],
                  dst_tile, writes=[dst_tile])


def ln_epilogue(k, nc, xtj, xt_t, Yh, Yh_t, xscale, eps_eff, gB, bB, cst_t, stat, stat_t, mhalf, out_rows, out_t):
    for h in range(2):
        k.op(k.dve, lambda h=h: nc.vector.scalar_tensor_tensor(
            out=xtj[:, h * 512:(h + 1) * 512], in0=xtj[:, h * 512:(h + 1) * 512], scalar=float(xscale),
            in1=Yh[h][:, :], op0=ALU.mult, op1=ALU.add), reads=[Yh_t[h], xt_t], writes=[xt_t])
    for h in range(2):
        k.op(k.dve, lambda h=h: nc.vector.bn_stats(out=stat[:, h * 6:(h + 1) * 6], in_=xtj[:, h * 512:(h + 1) * 512]),
             reads=[xt_t], writes=[stat_t])
    k.op(k.dve, lambda: nc.vector.bn_aggr(out=stat[:, 12:14], in_=stat[:, 0:12]), reads=[stat_t], writes=[stat_t])
    k.op(k.dve, lambda: nc.vector.tensor_scalar(out=stat[:, 14:15], in0=stat[:, 13:14], scalar1=float(eps_eff),
                                                scalar2=None, op0=ALU.add), reads=[stat_t], writes=[stat_t])
    k.op(k.pool, lambda: nc.gpsimd.tensor_tensor(out=stat[:, 15:16], in0=stat[:, 14:15], in1=mhalf[:, 0:1], op=ALU.pow),
         reads=[stat_t, cst_t], writes=[stat_t])
    k.op(k.dve, lambda: nc.vector.tensor_scalar(out=stat[:, 16:17], in0=stat[:, 12:13], scalar1=-1.0,
                                                scalar2=stat[:, 15:16], op0=ALU.mult, op1=ALU.mult),
         reads=[stat_t], writes=[stat_t])
    k.op(k.act, lambda: nc.scalar.activation(out=xtj[:, :], in_=xtj[:, :], func=AF.Identity,
                                             scale=stat[:, 15:16], bias=stat[:, 16:17]),
         reads=[xt_t, stat_t], writes=[xt_t])
    k.op(k.pool, lambda: nc.gpsimd.tensor_tensor(out=xtj[:, :], in0=xtj[:, :], in1=gB, op=ALU.mult),
         reads=[xt_t, cst_t], writes=[xt_t])
    k.op(k.pool, lambda: nc.gpsimd.tensor_tensor(out=xtj[:, :], in0=xtj[:, :], in1=bB, op=ALU.add),
         reads=[xt_t, cst_t], writes=[xt_t])
    k.dma(k.sp, out_rows, xtj[:, :], xt_t, reads=[xt_t], writes=[out_t])


def transpose_block(k, nc, xt, xt_t, tiles, ident, cst_t, banks, bank_t, xT, xT_t, bi):
    for dc in range(NDC):
        bk = (bi * NDC + dc) % 4
        for j in range(4):
            k.op(k.pe, lambda j=j, dc=dc, bk=bk: nc.tensor.transpose(
                out=banks[bk][:, j * 128:(j + 1) * 128], in_=xt[tiles[j]][:, dc * 128:(dc + 1) * 128], identity=ident),
                reads=[xt_t[tiles[j]], cst_t], writes=[bank_t[bk]], inc=(j == 3))
        if dc % 2 == 0:
            k.op(k.dve, lambda dc=dc, bk=bk: nc.vector.tensor_copy(out=xT[:, dc, :], in_=banks[bk][:, :]),
                 reads=[bank_t[bk]], writes=[xT_t])
        else:
            k.op(k.act, lambda dc=dc, bk=bk: nc.scalar.copy(out=xT[:, dc, :], in_=banks[bk][:, :]),
                 reads=[bank_t[bk]], writes=[xT_t])


def ffn_phase(k, nc, T, xin, xin_t, Wg, Wu, Wd, lnp, ln_idx, ident_d, xout, xout_t, pname):
    nb = T // 512
    with ExitStack() as es:
        wg = _sb(nc, es, pname + "wg", [128, NDC, FF], BF16)
        wu = _sb(nc, es, pname + "wu", [128, NDC, FF], BF16)
        wd = _sb(nc, es, pname + "wd", [128, NFC, D], BF16)
        cst = _sb(nc, es, pname + "cst", [128, 2 * D + 128 + 8], F32)
        xt = [_sb(nc, es, pname + "xt%d" % i, [128, D], F32) for i in range(NXT)]
        xT = [_sb(nc, es, pname + "xT%d" % i, [128, NDC, 512], BF16) for i in range(2)]
        hT = _sb(nc, es, pname + "hT", [128, NFC, 512], BF16)
        sg = [_sb(nc, es, pname + "sg%d" % i, [128, 512], F32) for i in range(2)]
        stat = [_sb(nc, es, pname + "st%d" % i, [128, 32], F32) for i in range(4)]
        banks = [es.enter_context(nc.psum_tensor(pname + "bk%d" % i, [128, 512], F32)) for i in range(8)]
        wg_t, wu_t, wd_t, cst_t = Tile("wg"), Tile("wu"), Tile("wd"), Tile("cst")
        xt_t = [Tile("xt") for _ in range(NXT)]
        xT_t = [Tile("xT") for _ in range(2)]
        hT_t = [Tile("hT") for _ in range(NFC)]
        sg_t = [Tile("sg") for _ in range(2)]
        stat_t = [Tile("stat") for _ in range(4)]
        bank_t = [Tile("bank", True) for _ in range(8)]
        gB = cst[:, 0:D]
        bB = cst[:, D:2 * D]
        ident = cst[:, 2 * D:2 * D + 128]
        mhalf = cst[:, 2 * D + 128:2 * D + 129]
        k.dma(k.sp, gB, lnp[:, 2 * ln_idx, :], cst_t, writes=[cst_t])
        k.dma(k.sp, bB, lnp[:, 2 * ln_idx + 1, :], cst_t, writes=[cst_t])
        k.dma(k.sp, ident, ident_d[:, :], cst_t, writes=[cst_t])
        k.op(k.dve, lambda: nc.vector.memset(mhalf, -0.5), writes=[cst_t])

        def load_x(b, js=(0, 1, 2, 3)):
            for j in js:
                ti = (b * 4 + j) % NXT
                r0 = (b * 4 + j) * 128
                k.dma(k.sp, xt[ti][:, :], xin[r0:r0 + 128, :], xt_t[ti], reads=[xin_t[b]], writes=[xt_t[ti]])

        load_x(0)
        load_weight(k, wg, wg_t, Wg, D, FF)
        load_weight(k, wu, wu_t, Wu, D, FF)
        load_weight(k, wd, wd_t, Wd, FF, D)
        for b in range(nb):
            if b + 1 < nb:
                load_x(b + 1, (0, 1))
            tiles = [(b * 4 + j) % NXT for j in range(4)]
            xTb, xTb_t = xT[b % 2], xT_t[b % 2]
            transpose_block(k, nc, xt, xt_t, tiles, ident, cst_t, banks, bank_t, xTb, xTb_t, b)
            for fc in range(NFC):
                A, B = (fc % 2) * 2, (fc % 2) * 2 + 1
                for dc in range(NDC):
                    k.op(k.pe, lambda dc=dc, fc=fc, A=A: nc.tensor.matmul(
                        banks[A][:, :], lhsT=wg[:, dc, fc * 128:(fc + 1) * 128], rhs=xTb[:, dc, :],
                        start=(dc == 0), stop=(dc == NDC - 1)),
                        reads=[wg_t, xTb_t], writes=[bank_t[A]], inc=(dc == NDC - 1))
                for dc in range(NDC):
                    k.op(k.pe, lambda dc=dc, fc=fc, B=B: nc.tensor.matmul(
                        banks[B][:, :], lhsT=wu[:, dc, fc * 128:(fc + 1) * 128], rhs=xTb[:, dc, :],
                        start=(dc == 0), stop=(dc == NDC - 1)),
                        reads=[wu_t, xTb_t], writes=[bank_t[B]], inc=(dc == NDC - 1))
                k.op(k.act, lambda fc=fc, A=A: nc.scalar.activation(out=sg[fc % 2][:, :], in_=banks[A][:, :], func=AF.Silu),
                     reads=[bank_t[A]], writes=[sg_t[fc % 2]])
                k.op(k.dve, lambda fc=fc, B=B: nc.vector.tensor_tensor(
                    out=hT[:, fc, :], in0=banks[B][:, :], in1=sg[fc % 2][:, :], op=ALU.mult),
                    reads=[bank_t[B], sg_t[fc % 2]], writes=[hT_t[fc]])
            for j in range(4):
                Yb = [4 + (j % 2) * 2, 5 + (j % 2) * 2]
                for h in range(2):
                    for fc in range(NFC):
                        k.op(k.pe, lambda fc=fc, h=h, j=j, Yb=Yb: nc.tensor.matmul(
                            banks[Yb[h]][:, :], lhsT=hT[:, fc, j * 128:(j + 1) * 128], rhs=wd[:, fc, h * 512:(h + 1) * 512],
                            start=(fc == 0), stop=(fc == NFC - 1)),
                            reads=[hT_t[fc], wd_t], writes=[bank_t[Yb[h]]], inc=(fc == NFC - 1))
                ti = tiles[j]
                r0 = (b * 4 + j) * 128
                ln_epilogue(k, nc, xt[ti], xt_t[ti], [banks[Yb[0]], banks[Yb[1]]], [bank_t[Yb[0]], bank_t[Yb[1]]],
                            2.0 * ALPHA, 4.0 * EPS, gB, bB, cst_t, stat[j], stat_t[j], mhalf,
                            xout[r0:r0 + 128, :], xout_t[b])
                if j == 1 and b + 1 < nb:
                    load_x(b + 1, (2, 3))
        k.barrier()


def mix_phase(k, nc, seqs, x1s, x1_t, w_in, rope_d, convw_d, convp_d, subg_d, lamv_d, ident_d,
              qTs, syTs, onTs, sy_t, on_t):
    T = sum(seqs)
    nbt = T // 512
    Smax = max(seqs)
    NKT = Smax // 128
    with ExitStack() as es:
        win = _sb(nc, es, "m_win", [128, NDC, 2560], BF16)
        diag = _sb(nc, es, "m_diag", [128, 4, CONV_K, 128], BF16)
        kT = _sb(nc, es, "m_kT", [128, 4, Smax], BF16)
        V = _sb(nc, es, "m_V", [128, NKT, 4, 130], BF16)
        cst = _sb(nc, es, "m_cst", [128, 128 + 128 + 4 * 31 + 12 + 128 + 8], F32)
        ropet = _sb(nc, es, "m_rope", [128, MAXT, 32], F32)
        lamt = _sb(nc, es, "m_lam", [128, 4 * 64 + 16], F32)
        identb = _sb(nc, es, "m_idb", [128, 128], BF16)
        if DBG.get("pad"):
            _sb(nc, es, "m_pad", [128, 1024], F32)
        pbig = es.enter_context(nc.psum_tensor("m_pbig", [128, 8, 512], F32))
        banks = [pbig[:, i, :] for i in range(8)]
        bank_t = [Tile("bank", True) for _ in range(8)]
        win_t, diag_t, cst_t, rope_t, lam_t, idb_t = Tile(), Tile(), Tile(), Tile(), Tile(), Tile()
        kT_t = [Tile() for _ in range(NKT)]
        V_t = [Tile() for _ in range(NKT)]
        qTs_t = [Tile() for _ in range(nbt)]

        ident = cst[:, 0:128]
        ones = cst[:, 128:256]
        cw = cst[:, 256:256 + 124]
        cp = cst[:, 380:392]
        gsub = cst[:, 392:520]
        mhalf = cst[:, 520:521]
        neglam = lamt[:, 256 + 8:256 + 9]

        k.dma(k.sp, ident, ident_d[:, :], cst_t, writes=[cst_t])
        k.dma(k.sp, cw, convw_d[:, :], cst_t, writes=[cst_t])
        k.dma(k.sp, cp, convp_d[:, :], cst_t, writes=[cst_t])
        k.dma(k.sp, gsub, subg_d[:, :], cst_t, writes=[cst_t])
        k.dma(k.sp, ropet[:, :, :], rope_d[:, :, :], rope_t, writes=[rope_t])
        k.dma(k.sp, lamt[:, 0:256], lamv_d[:, :], lam_t, writes=[lam_t])
        k.op(k.dve, lambda: nc.vector.memset(ones, 1.0), writes=[cst_t])
        k.op(k.dve, lambda: nc.vector.memset(mhalf, -0.5), writes=[cst_t])
        k.op(k.dve, lambda: nc.vector.tensor_copy(out=identb[:, :], in_=ident), reads=[cst_t], writes=[idb_t])
        k.op(k.dve, lambda: nc.vector.tensor_scalar(out=gsub, in0=gsub, scalar1=float(1.0 - LAMBDA_INIT), scalar2=None,
                                                    op0=ALU.mult), reads=[cst_t], writes=[cst_t])
        k.op(k.dve, lambda: nc.vector.tensor_tensor(out=lamt[:, 0:64], in0=lamt[:, 0:64], in1=lamt[:, 64:128], op=ALU.mult),
             reads=[lam_t], writes=[lam_t])
        k.op(k.dve, lambda: nc.vector.tensor_tensor(out=lamt[:, 128:192], in0=lamt[:, 128:192], in1=lamt[:, 192:256], op=ALU.mult),
             reads=[lam_t], writes=[lam_t])
        k.op(k.dve, lambda: nc.vector.reduce_sum(out=lamt[:, 256:257], in_=lamt[:, 0:64], axis=mybir.AxisListType.X),
             reads=[lam_t], writes=[lam_t])
        k.op(k.dve, lambda: nc.vector.reduce_sum(out=lamt[:, 257:258], in_=lamt[:, 128:192], axis=mybir.AxisListType.X),
             reads=[lam_t], writes=[lam_t])
        k.op(k.act, lambda: nc.scalar.activation(out=lamt[:, 258:260], in_=lamt[:, 256:258], func=AF.Exp),
             reads=[lam_t], writes=[lam_t])
        k.op(k.dve, lambda: nc.vector.scalar_tensor_tensor(out=neglam, in0=lamt[:, 259:260], scalar=float(-LAMBDA_INIT),
                                                           in1=lamt[:, 258:259], op0=ALU.add, op1=ALU.subtract),
             reads=[lam_t], writes=[lam_t])
        for c in range(4):
            for j in range(CONV_K):
                k.op(k.dve, lambda c=c, j=j: nc.vector.tensor_scalar(
                    out=diag[:, c, j, :], in0=ident, scalar1=cw[:, c * CONV_K + j:c * CONV_K + j + 1], scalar2=None,
                    op0=ALU.mult), reads=[cst_t], writes=[diag_t])
        k.op(k.pool, lambda: nc.gpsimd.memset(V[:, :, :, 128:130], 1.0), writes=V_t)
        load_weight(k, win, win_t, w_in, D, 2560)

        def conv_block(gb, slot):
            if DBG.get("noconv"):
                return
            us = usl[slot]
            S1, S2 = 6, 7
            for c in range(4):
                YC = 4 + (c % 2)
                for j in range(CONV_K):
                    k.op(k.pe, lambda c=c, j=j, YC=YC: nc.tensor.matmul(
                        banks[YC][:, :], lhsT=diag[:, c, j, :], rhs=us[:, c, 1 + j:1 + j + 512],
                        start=(j == 0), stop=(j == CONV_K - 1)),
                        reads=[diag_t, usl_t[slot]], writes=[bank_t[YC]], inc=(j == CONV_K - 1))
                k.op(k.act, lambda c=c, YC=YC: nc.scalar.activation(
                    out=ysb[:, c, :], in_=banks[YC][:, :], func=AF.Identity, scale=0.5, bias=cp[:, c:c + 1]),
                    reads=[bank_t[YC], cst_t], writes=[ysb_t[c]])
                k.op(k.pool, lambda c=c: nc.gpsimd.tensor_tensor(out=ysq[:, c % 2, :], in0=ysb[:, c, :], in1=ysb[:, c, :], op=ALU.mult),
                     reads=[ysb_t[c]], writes=[ysq_t[c % 2]])
                k.op(k.pe, lambda c=c: nc.tensor.matmul(banks[S1][:, :], lhsT=ones, rhs=ysb[:, c, :], start=(c == 0), stop=(c == 3)),
                     reads=[cst_t, ysb_t[c]], writes=[bank_t[S1]], inc=True)
                k.op(k.pe, lambda c=c: nc.tensor.matmul(banks[S2][:, :], lhsT=ones, rhs=ysq[:, c % 2, :], start=(c == 0), stop=(c == 3)),
                     reads=[cst_t, ysq_t[c % 2]], writes=[bank_t[S2]], inc=True)
            mean, tmpv = cstat[:, 0, :], cstat[:, 1, :]
            rstd = tmpv
            k.op(k.dve, lambda: nc.vector.tensor_scalar(out=mean, in0=banks[S1][:, :], scalar1=1.0 / 512, scalar2=None, op0=ALU.mult),
                 reads=[bank_t[S1]], writes=[cstat_t])
            k.op(k.dve, lambda: nc.vector.tensor_tensor(out=tmpv, in0=mean, in1=mean, op=ALU.mult), reads=[cstat_t], writes=[cstat_t])
            k.op(k.dve, lambda: nc.vector.scalar_tensor_tensor(out=tmpv, in0=banks[S2][:, :], scalar=1.0 / 512, in1=tmpv,
                                                               op0=ALU.mult, op1=ALU.subtract),
                 reads=[bank_t[S2], cstat_t], writes=[cstat_t])
            k.op(k.dve, lambda: nc.vector.tensor_scalar(out=tmpv, in0=tmpv, scalar1=float(EPS), scalar2=None, op0=ALU.add),
                 reads=[cstat_t], writes=[cstat_t])
            k.op(k.act, lambda: nc.scalar.activation(out=tmpv, in_=tmpv, func=AF.Sqrt), reads=[cstat_t], writes=[cstat_t])
            k.op(k.dve, lambda: nc.vector.reciprocal(out=tmpv, in_=tmpv), reads=[cstat_t], writes=[cstat_t])
            so, so_t = syT[0], syT_t[0]
            for c in range(4):
                k.op(k.dve, lambda c=c: nc.vector.tensor_tensor(out=ysb[:, c, :], in0=ysb[:, c, :], in1=mean, op=ALU.subtract),
                     reads=[ysb_t[c], cstat_t], writes=[ysb_t[c]])
                k.op(k.pool, lambda c=c: nc.gpsimd.tensor_tensor(out=ysb[:, c, :], in0=ysb[:, c, :], in1=rstd, op=ALU.mult),
                     reads=[ysb_t[c], cstat_t], writes=[ysb_t[c]])
                k.op(k.act, lambda c=c: nc.scalar.activation(out=so[:, c, :], in_=ysb[:, c, :], func=AF.Silu,
                                                             scale=cp[:, 4 + c:5 + c], bias=cp[:, 8 + c:9 + c]),
                     reads=[ysb_t[c], cst_t], writes=[so_t])
            k.dma(k.sp, syTs[gb], so[:, :, :], so_t, reads=[so_t], writes=[sy_t[gb]])

        def rope_and_T(src_bank, dst, dst_tiles, dcol, tt, par):
            Q3 = banks[src_bank][:, :].rearrange("p (g d) -> p g d", g=8)
            q_ = qr[par]
            q3 = q_[:, :].rearrange("p (g d) -> p g d", g=8)
            t1 = rtmp[par][:, 0, :, :]
            t2 = rtmp[par][:, 1, :, :]
            cc = ropet[:, tt, 0:16].unsqueeze(1).to_broadcast([128, 8, 16])
            nsa = ropet[:, tt, 16:24].unsqueeze(1).to_broadcast([128, 8, 8])
            nsb = ropet[:, tt, 24:32].unsqueeze(1).to_broadcast([128, 8, 8])
            k.op(k.act, lambda: nc.scalar.copy(out=q3[:, :, 16:64], in_=Q3[:, :, 16:64]),
                 reads=[bank_t[src_bank]], writes=[qr_t[par]])
            k.op(k.dve, lambda: nc.vector.tensor_tensor(out=t1, in0=Q3[:, :, 0:16], in1=cc, op=ALU.mult),
                 reads=[bank_t[src_bank], rope_t], writes=[rtmp_t[par]])
            k.op(k.dve, lambda: nc.vector.tensor_tensor(out=t2[:, :, 0:8], in0=Q3[:, :, 8:16], in1=nsa, op=ALU.mult),
                 reads=[bank_t[src_bank], rope_t], writes=[rtmp_t[par]])
            k.op(k.dve, lambda: nc.vector.tensor_tensor(out=t2[:, :, 8:16], in0=Q3[:, :, 0:8], in1=nsb, op=ALU.mult),
                 reads=[bank_t[src_bank], rope_t], writes=[rtmp_t[par]])
            k.op(k.dve, lambda: nc.vector.tensor_tensor(out=q3[:, :, 0:16], in0=t1, in1=t2, op=ALU.add),
                 reads=[rtmp_t[par]], writes=[qr_t[par]])
            PT = banks[src_bank][:, :].bitcast(BF16)
            for h in range(4):
                k.op(k.pe, lambda h=h: nc.tensor.transpose(out=PT[:, h * 128:(h + 1) * 128], in_=q_[:, h * 128:(h + 1) * 128],
                                                           identity=identb[:, :]),
                     reads=[qr_t[par], idb_t], writes=[bank_t[src_bank]], inc=(h == 3))
            k.op(k.dve, lambda: nc.vector.tensor_copy(out=dst[:, :, dcol:dcol + 128],
                                                      in_=PT[:, 0:512].rearrange("p (h t) -> p h t", h=4)),
                 reads=[bank_t[src_bank]], writes=dst_tiles)

        gb0 = 0
        tok0 = 0
        for si, S in enumerate(seqs):
            nb = S // 512
            nkt = S // 128
            esa = ExitStack()
            usl = [_sb(nc, esa, "m%d_u%d" % (si, i), [128, 4, 544], BF16) for i in range(2)]
            xt = [_sb(nc, esa, "m%d_xt%d" % (si, i), [128, D], F32) for i in range(4)]
            xT = _sb(nc, esa, "m%d_xT" % si, [128, NDC, 512], BF16)
            qr = [_sb(nc, esa, "m%d_qr%d" % (si, i), [128, 512], BF16) for i in range(2)]
            rtmp = [_sb(nc, esa, "m%d_rt%d" % (si, i), [128, 2, 8, 16], F32) for i in range(2)]
            qTb = [_sb(nc, esa, "m%d_qT%d" % (si, i), [128, 4, 512], BF16) for i in range(1)]
            sig = [_sb(nc, esa, "m%d_sig%d" % (si, i), [128, 512], F32) for i in range(1)]
            ysb = _sb(nc, esa, "m%d_ysb" % si, [128, 4, 512], F32)
            ysq = _sb(nc, esa, "m%d_ysq" % si, [128, 2, 512], F32)
            cstat = _sb(nc, esa, "m%d_cstat" % si, [128, 2, 512], F32)
            syT = [_sb(nc, esa, "m%d_syT%d" % (si, i), [128, 4, 512], BF16) for i in range(1)]
            usl_t = [Tile(), Tile()]
            xt_t = [Tile() for _ in range(4)]
            xT_t = Tile()
            qr_t = [Tile(), Tile()]
            rtmp_t = [Tile(), Tile()]
            qTb_t = [Tile()]
            sig_t = [Tile()]
            ysb_t = [Tile() for _ in range(4)]
            ysq_t = [Tile() for _ in range(2)]
            cstat_t = Tile()
            syT_t = [Tile()]
            for b in range(nb):
                gb = gb0 + b
                for j in range(4):
                    r0 = tok0 + (b * 4 + j) * 128
                    k.dma(k.sp, xt[j][:, :], x1s[r0:r0 + 128, :], xt_t[j], reads=[x1_t[gb]], writes=[xt_t[j]])
                transpose_block(k, nc, xt, xt_t, [0, 1, 2, 3], ident, cst_t, banks, bank_t, xT, xT_t, gb)
                qo, qo_t = qTb[0], qTb_t[0]
                for j in range(4):
                    kt = b * 4 + j
                    QB, KB, VB = 0 + (j % 2), 2 + (j % 2), 4 + (j % 2)
                    for dc in range(NDC):
                        for (bk, c0) in ((QB, 0), (KB, 512), (VB, 1024)):
                            k.op(k.pe, lambda dc=dc, bk=bk, c0=c0, j=j: nc.tensor.matmul(
                                banks[bk][:, :], lhsT=xT[:, dc, j * 128:(j + 1) * 128], rhs=win[:, dc, c0:c0 + 512],
                                start=(dc == 0), stop=(dc == NDC - 1)),
                                reads=[xT_t, win_t], writes=[bank_t[bk]], inc=(dc == NDC - 1))
                    k.op(k.act, lambda VB=VB, kt=kt: nc.scalar.copy(
                        out=V[:, kt, :, 0:128], in_=banks[VB][:, :].rearrange("p (h e) -> p h e", h=4)),
                        reads=[bank_t[VB]], writes=[V_t[kt]])
                    rope_and_T(QB, qo, [qo_t], j * 128, kt, 0)
                    rope_and_T(KB, kT, [kT_t[kt]], kt * 128, kt, 1)
                k.dma(k.sp, qTs[gb], qo[:, :, :], qo_t, reads=[qo_t], writes=[qTs_t[gb]])
                slot = b % 2
                us = usl[slot]
                if b == 0:
                    k.op(k.pool, lambda us=us: nc.gpsimd.memset(us[:, :, 0:16], 0.0), writes=[usl_t[slot]])
                for c in range(4):
                    ZA, ZB = 6, 7
                    for dc in range(NDC):
                        k.op(k.pe, lambda dc=dc, c=c: nc.tensor.matmul(
                            banks[ZA][:, :], lhsT=win[:, dc, 1536 + c * 128:1536 + (c + 1) * 128], rhs=xT[:, dc, :],
                            start=(dc == 0), stop=(dc == NDC - 1)), reads=[win_t, xT_t], writes=[bank_t[ZA]], inc=(dc == NDC - 1))
                    for dc in range(NDC):
                        k.op(k.pe, lambda dc=dc, c=c: nc.tensor.matmul(
                            banks[ZB][:, :], lhsT=win[:, dc, 2048 + c * 128:2048 + (c + 1) * 128], rhs=xT[:, dc, :],
                            start=(dc == 0), stop=(dc == NDC - 1)), reads=[win_t, xT_t], writes=[bank_t[ZB]], inc=(dc == NDC - 1))
                    k.op(k.act, lambda c=c: nc.scalar.activation(out=sig[0][:, :], in_=banks[ZB][:, :], func=AF.Tanh, scale=0.5),
                         reads=[bank_t[ZB]], writes=[sig_t[0]])
                    k.op(k.dve, lambda c=c, us=us: nc.vector.scalar_tensor_tensor(
                        out=us[:, c, 16:528], in0=sig[0][:, :], scalar=1.0, in1=banks[ZA][:, :], op0=ALU.add, op1=ALU.mult),
                        reads=[sig_t[0], bank_t[ZA]], writes=[usl_t[slot]])
                if b > 0:
                    ps_ = usl[1 - slot]
                    k.op(k.pool, lambda us=us, ps_=ps_: nc.gpsimd.tensor_copy(out=ps_[:, :, 528:544], in_=us[:, :, 16:32]),
                         reads=[usl_t[slot]], writes=[usl_t[1 - slot]])
                    conv_block(gb - 1, 1 - slot)
                if b + 1 < nb:
                    ns_ = usl[1 - slot]
                    k.op(k.pool, lambda us=us, ns_=ns_: nc.gpsimd.tensor_copy(out=ns_[:, :, 0:16], in_=us[:, :, 512:528]),
                         reads=[usl_t[slot]], writes=[usl_t[1 - slot]])
                else:
                    k.op(k.pool, lambda us=us: nc.gpsimd.memset(us[:, :, 528:544], 0.0), writes=[usl_t[slot]])
                    conv_block(gb, slot)
            k.barrier()
            esa.close()
            esb = ExitStack()
            if DBG.get("noattn"):
                nb = 0
            qz = [[_sb(nc, esb, "m%d_qz%d_%d" % (si, p_, m_), [128, 4, 512], BF16) for m_ in range(2)] for p_ in range(2)]
            qz_t = [Tile(), Tile()]
            for p_ in range(2):
                k.op(k.pool, lambda p_=p_: nc.gpsimd.memset(qz[p_][0][64:128, :, :], 0.0), writes=[qz_t[p_]])
                k.op(k.pool, lambda p_=p_: nc.gpsimd.memset(qz[p_][1][0:64, :, :], 0.0), writes=[qz_t[p_]])

            def load_q(gb_):
                p_ = gb_ % 2
                k.dma(k.sp, qz[p_][0][0:64, :, :], qTs[gb_][0:64], qz_t[p_], reads=[qTs_t[gb_]], writes=[qz_t[p_]])
                k.dma(k.sp, qz[p_][1][64:128, :, :], qTs[gb_][64:128], qz_t[p_], reads=[qTs_t[gb_]], writes=[qz_t[p_]])

            if nb > 0:
                load_q(gb0)
            ET = [_sb(nc, esb, "m%d_E%d" % (si, i), [128, 2, 512], BF16) for i in range(2)]
            osb = [_sb(nc, esb, "m%d_osb%d" % (si, i), [128, 2, 128], F32) for i in range(4)]
            ost = [_sb(nc, esb, "m%d_ost%d" % (si, i), [128, 8], F32) for i in range(4)]
            onb = _sb(nc, esb, "m%d_on" % si, [128, 4, 512], BF16)
            onT = [_sb(nc, esb, "m%d_onT%d" % (si, i), [128, 4, 512], BF16) for i in range(2)]
            ET_t = [Tile(), Tile()]
            osb_t = [Tile() for _ in range(4)]
            ost_t = [Tile() for _ in range(4)]
            onb_t = Tile()
            onT_t = [Tile(), Tile()]
            for b in range(nb):
                gb = gb0 + b
                qzb, qi_t = qz[gb % 2], qz_t[gb % 2]
                if b + 1 < nb:
                    load_q(gb + 1)
                def emit_qk_exp(h, kt, par):
                    ST = [par * 2, par * 2 + 1]
                    for m in range(2):
                        k.op(k.pe, lambda m=m, h=h, kt=kt, ST=ST: nc.tensor.matmul(
                            banks[ST[m]][:, :], lhsT=kT[:, h, kt * 128:(kt + 1) * 128],
                            rhs=qzb[m][:, h, :], start=True, stop=True),
                            reads=[kT_t[kt], qi_t], writes=[bank_t[ST[m]]], inc=True)
                    k.op(k.act, lambda par=par: nc.scalar.activation(
                        out=ET[par][:, :, :], in_=pbig[:, 2 * par:2 * par + 2, :], func=AF.Exp, scale=0.125),
                        reads=[bank_t[ST[0]], bank_t[ST[1]]], writes=[ET_t[par]])

                def emit_pv(h, kt, par):
                    for j in range(4 if not DBG.get("nopv") else 0):
                        for m in range(2):
                            k.op(k.pe, lambda m=m, j=j, h=h, kt=kt, par=par: nc.tensor.matmul(
                                banks[4 + j][:, m * 256:m * 256 + 129], lhsT=ET[par][:, m, j * 128:(j + 1) * 128],
                                rhs=V[:, kt, h, 0:129], start=(kt == 0 and m == 0), stop=(kt == nkt - 1 and m == 1),
                                skip_group_check=True),
                                reads=[ET_t[par], V_t[kt]], writes=[bank_t[4 + j]], inc=(m == 1))

                def emit_epi(h):
                    for j in range(4 if not DBG.get("noepi") else 0):
                        acc = banks[4 + j]
                        st_, st_t = ost[j], ost_t[j]
                        k.op(k.dve, lambda acc=acc, st_=st_: nc.vector.reciprocal(
                            out=st_[:, 0:2], in_=acc[:, :].rearrange("p (m c) -> p m c", m=2)[:, :, 128]),
                            reads=[bank_t[4 + j]], writes=[st_t])
                        k.op(k.dve, lambda st_=st_: nc.vector.tensor_tensor(out=st_[:, 2:3], in0=st_[:, 1:2], in1=neglam, op=ALU.mult),
                             reads=[st_t, lam_t], writes=[st_t])
                        k.op(k.dve, lambda acc=acc, st_=st_, j=j: nc.vector.tensor_scalar(
                            out=osb[j][:, 0, :], in0=acc[:, 256:384], scalar1=st_[:, 2:3], scalar2=None, op0=ALU.mult),
                            reads=[bank_t[4 + j], st_t], writes=[osb_t[j]])
                        k.op(k.dve, lambda acc=acc, st_=st_, j=j: nc.vector.scalar_tensor_tensor(
                            out=osb[j][:, 0, :], in0=acc[:, 0:128], scalar=st_[:, 0:1], in1=osb[j][:, 0, :], op0=ALU.mult, op1=ALU.add),
                            reads=[bank_t[4 + j], st_t, osb_t[j]], writes=[osb_t[j]])
                    for j in range(4 if not DBG.get("noepi2") else 0):
                        st_, st_t = ost[j], ost_t[j]
                        k.op(k.dve, lambda j=j: nc.vector.tensor_tensor(
                            out=osb[j][:, 1, :], in0=osb[j][:, 0, :], in1=osb[j][:, 0, :], op=ALU.mult),
                            reads=[osb_t[j]], writes=[osb_t[j]])
                        k.op(k.dve, lambda st_=st_, j=j: nc.vector.reduce_sum(
                            out=st_[:, 3:4], in_=osb[j][:, 1, :], axis=mybir.AxisListType.X),
                            reads=[osb_t[j]], writes=[st_t])
                        k.op(k.dve, lambda st_=st_: nc.vector.tensor_scalar(
                            out=st_[:, 4:5], in0=st_[:, 3:4], scalar1=1.0 / 128, scalar2=float(EPS), op0=ALU.mult, op1=ALU.add),
                            reads=[st_t], writes=[st_t])
                        k.op(k.pool, lambda st_=st_: nc.gpsimd.tensor_tensor(out=st_[:, 5:6], in0=st_[:, 4:5], in1=mhalf, op=ALU.pow),
                             reads=[st_t, cst_t], writes=[st_t])
                        k.op(k.dve, lambda st_=st_, j=j, h=h: nc.vector.scalar_tensor_tensor(
                            out=onb[:, j, h * 128:(h + 1) * 128], in0=osb[j][:, 0, :], scalar=st_[:, 5:6], in1=gsub,
                            op0=ALU.mult, op1=ALU.mult), reads=[osb_t[j], st_t, cst_t], writes=[onb_t])

                units = [(h_, kt_) for h_ in range(4) for kt_ in range(nkt)]
                for ui, (h_, kt_) in enumerate(units):
                    emit_qk_exp(h_, kt_, ui % 2)
                    if ui >= 1:
                        hp_, ktp_ = units[ui - 1]
                        emit_pv(hp_, ktp_, (ui - 1) % 2)
                        if ktp_ == nkt - 1:
                            emit_epi(hp_)
                hp_, ktp_ = units[-1]
                emit_pv(hp_, ktp_, (len(units) - 1) % 2)
                emit_epi(hp_)
                oo, oo_t = onT[gb % 2], onT_t[gb % 2]
                for h in range(4 if not DBG.get("noont") else 0):
                    bk = h
                    PT = banks[bk][:, :].bitcast(BF16)
                    for j in range(4):
                        k.op(k.pe, lambda h=h, j=j, PT=PT: nc.tensor.transpose(
                            out=PT[:, j * 128:(j + 1) * 128], in_=onb[:, j, h * 128:(h + 1) * 128], identity=identb[:, :]),
                            reads=[onb_t, idb_t], writes=[bank_t[bk]], inc=(j == 3))
                    if h % 2 == 0:
                        k.op(k.dve, lambda h=h, PT=PT: nc.vector.tensor_copy(out=oo[:, h, :], in_=PT[:, 0:512]),
                             reads=[bank_t[bk]], writes=[oo_t])
                    else:
                        k.op(k.act, lambda h=h, PT=PT: nc.scalar.copy(out=oo[:, h, :], in_=PT[:, 0:512]),
                             reads=[bank_t[bk]], writes=[oo_t])
                if not DBG.get("noonstore"):
                    k.dma(k.sp, onTs[gb], oo[:, :, :], oo_t, reads=[oo_t], writes=[on_t[gb]])
            k.barrier()
            esb.close()
            nb = S // 512
            gb0 += nb
            tok0 += S
        k.barrier()


def merge_phase(k, nc, T, x1s, x1_t, w_in, w_att, w_cp, w_out, lnp, bgate_d, ident_d, syTs, onTs, sy_t, on_t, x2s, x2_t):
    nb = T // 512
    with ExitStack() as es:
        wgt = _sb(nc, es, "g_wgt", [128, NDC, 2048], BF16)
        watt = _sb(nc, es, "g_watt", [128, 4, D], BF16)
        wcp = _sb(nc, es, "g_wcp", [128, 4, D], BF16)
        wout = _sb(nc, es, "g_wout", [128, NDC, D], BF16)
        cst = _sb(nc, es, "g_cst", [128, 2 * D + 128 + 16 + 8], F32)
        xt = [_sb(nc, es, "g_xt%d" % i, [128, D], F32) for i in range(8)]
        xT = [_sb(nc, es, "g_xT%d" % i, [128, NDC, 512], BF16) for i in range(2)]
        onT = [_sb(nc, es, "g_onT%d" % i, [128, 4, 512], BF16) for i in range(2)]
        syT = [_sb(nc, es, "g_syT%d" % i, [128, 4, 512], BF16) for i in range(2)]
        sga = [_sb(nc, es, "g_sga%d" % i, [128, 512], F32) for i in range(2)]
        sgc = [_sb(nc, es, "g_sgc%d" % i, [128, 512], F32) for i in range(2)]
        m1 = [_sb(nc, es, "g_m1%d" % i, [128, 512], F32) for i in range(2)]
        m2 = [_sb(nc, es, "g_m2%d" % i, [128, 512], F32) for i in range(2)]
        mT = _sb(nc, es, "g_mT", [128, NDC, 512], BF16)
        stat = [_sb(nc, es, "g_st%d" % i, [128, 32], F32) for i in range(4)]
        banks = [es.enter_context(nc.psum_tensor("g_bk%d" % i, [128, 512], F32)) for i in range(8)]
        bank_t = [Tile("bank", True) for _ in range(8)]
        wgt_t, watt_t, wcp_t, wout_t, cst_t = Tile(), Tile(), Tile(), Tile(), Tile()
        xt_t = [Tile() for _ in range(8)]
        xT_t = [Tile(), Tile()]
        onT_t = [Tile(), Tile()]
        syT_t = [Tile(), Tile()]
        sga_t = [Tile(), Tile()]
        sgc_t = [Tile(), Tile()]
        m1_t = [Tile(), Tile()]
        m2_t = [Tile(), Tile()]
        mT_t = [Tile() for _ in range(NDC)]
        stat_t = [Tile() for _ in range(4)]
        gB = cst[:, 0:D]
        bB = cst[:, D:2 * D]
        ident = cst[:, 2 * D:2 * D + 128]
        bg = cst[:, 2 * D + 128:2 * D + 144]
        mhalf = cst[:, 2 * D + 144:2 * D + 145]
        k.dma(k.sp, gB, lnp[:, 2, :], cst_t, writes=[cst_t])
        k.dma(k.sp, bB, lnp[:, 3, :], cst_t, writes=[cst_t])
        k.dma(k.sp, ident, ident_d[:, :], cst_t, writes=[cst_t])
        k.dma(k.sp, bg, bgate_d[:, :], cst_t, writes=[cst_t])
        k.op(k.dve, lambda: nc.vector.memset(mhalf, -0.5), writes=[cst_t])

        def load_blk(b):
            for j in range(4):
                ti = (b * 4 + j) % 8
                r0 = (b * 4 + j) * 128
                k.dma(k.sp, xt[ti][:, :], x1s[r0:r0 + 128, :], xt_t[ti], reads=[x1_t[b]], writes=[xt_t[ti]])
            k.dma(k.sp, onT[b % 2][:, :, :], onTs[b], onT_t[b % 2], reads=[on_t[b]], writes=[onT_t[b % 2]])
            k.dma(k.sp, syT[b % 2][:, :, :], syTs[b], syT_t[b % 2], reads=[sy_t[b]], writes=[syT_t[b % 2]])

        load_blk(0)
        load_weight(k, wgt, wgt_t, w_in, D, 2048, col0=2560)
        load_weight(k, watt, watt_t, w_att, 512, D)
        load_weight(k, wcp, wcp_t, w_cp, 512, D)
        load_weight(k, wout, wout_t, w_out, D, D)
        for b in range(nb):
            if b + 1 < nb:
                load_blk(b + 1)
            tiles = [(b * 4 + j) % 8 for j in range(4)]
            xTb, xTb_t = xT[b % 2], xT_t[b % 2]
            on_, on__t = onT[b % 2], onT_t[b % 2]
            sy_, sy__t = syT[b % 2], syT_t[b % 2]
            transpose_block(k, nc, xt, xt_t, tiles, ident, cst_t, banks, bank_t, xTb, xTb_t, b)
            for dm in range(NDC):
                GA, GC, AT, CT = 0, 1, 2, 3
                p = dm % 2
                for dc in range(NDC):
                    k.op(k.pe, lambda dc=dc, dm=dm: nc.tensor.matmul(
                        banks[GA][:, :], lhsT=wgt[:, dc, dm * 128:(dm + 1) * 128], rhs=xTb[:, dc, :],
                        start=(dc == 0), stop=(dc == NDC - 1)), reads=[wgt_t, xTb_t], writes=[bank_t[GA]], inc=(dc == NDC - 1))
                for dc in range(NDC):
                    k.op(k.pe, lambda dc=dc, dm=dm: nc.tensor.matmul(
                        banks[GC][:, :], lhsT=wgt[:, dc, 1024 + dm * 128:1024 + (dm + 1) * 128], rhs=xTb[:, dc, :],
                        start=(dc == 0), stop=(dc == NDC - 1)), reads=[wgt_t, xTb_t], writes=[bank_t[GC]], inc=(dc == NDC - 1))
                for h in range(4):
                    k.op(k.pe, lambda h=h, dm=dm: nc.tensor.matmul(
                        banks[AT][:, :], lhsT=watt[:, h, dm * 128:(dm + 1) * 128], rhs=on_[:, h, :],
                        start=(h == 0), stop=(h == 3)), reads=[watt_t, on__t], writes=[bank_t[AT]], inc=(h == 3))
                for c in range(4):
                    k.op(k.pe, lambda c=c, dm=dm: nc.tensor.matmul(
                        banks[CT][:, :], lhsT=wcp[:, c, dm * 128:(dm + 1) * 128], rhs=sy_[:, c, :],
                        start=(c == 0), stop=(c == 3)), reads=[wcp_t, sy__t], writes=[bank_t[CT]], inc=(c == 3))
                k.op(k.act, lambda dm=dm, p=p: nc.scalar.activation(out=sga[p][:, :], in_=banks[GA][:, :], func=AF.Sigmoid,
                                                                   bias=bg[:, dm:dm + 1]),
                     reads=[bank_t[GA], cst_t], writes=[sga_t[p]])
                k.op(k.act, lambda dm=dm, p=p: nc.scalar.activation(out=sgc[p][:, :], in_=banks[GC][:, :], func=AF.Sigmoid,
                                                                   bias=bg[:, 8 + dm:9 + dm]),
                     reads=[bank_t[GC], cst_t], writes=[sgc_t[p]])
                k.op(k.dve, lambda p=p: nc.vector.tensor_tensor(out=m1[p][:, :], in0=banks[AT][:, :], in1=sga[p][:, :], op=ALU.mult),
                     reads=[bank_t[AT], sga_t[p]], writes=[m1_t[p]])
                k.op(k.dve, lambda p=p: nc.vector.tensor_tensor(out=m2[p][:, :], in0=banks[CT][:, :], in1=sgc[p][:, :], op=ALU.mult),
                     reads=[bank_t[CT], sgc_t[p]], writes=[m2_t[p]])
                k.op(k.pool, lambda p=p, dm=dm: nc.gpsimd.tensor_tensor(out=mT[:, dm, :], in0=m1[p][:, :], in1=m2[p][:, :], op=ALU.add),
                     reads=[m1_t[p], m2_t[p]], writes=[mT_t[dm]])
            for j in range(4):
                Yb = [4 + (j % 2) * 2, 5 + (j % 2) * 2]
                for h in range(2):
                    for dm in range(NDC):
                        k.op(k.pe, lambda dm=dm, h=h, j=j, Yb=Yb: nc.tensor.matmul(
                            banks[Yb[h]][:, :], lhsT=mT[:, dm, j * 128:(j + 1) * 128], rhs=wout[:, dm, h * 512:(h + 1) * 512],
                            start=(dm == 0), stop=(dm == NDC - 1)),
                            reads=[mT_t[dm], wout_t], writes=[bank_t[Yb[h]]], inc=(dm == NDC - 1))
                ti = tiles[j]
                r0 = (b * 4 + j) * 128
                ln_epilogue(k, nc, xt[ti], xt_t[ti], [banks[Yb[0]], banks[Yb[1]]], [bank_t[Yb[0]], bank_t[Yb[1]]],
                            ALPHA, EPS, gB, bB, cst_t, stat[j], stat_t[j], mhalf, x2s[r0:r0 + 128, :], x2_t[b])
        k.barrier()


def build(seqs, phases=(1, 2, 3, 4)):
    T = sum(seqs)
    nb = T // 512
    nc = bass.Bass("TRN2", target_bir_lowering=False)

    def din(name, shape, dt=F32):
        return nc.dram_tensor(name, shape, dt, kind="ExternalInput").ap()

    x = din("x", [T, D])
    wg1, wu1, wd1 = din("wg1", [D, FF]), din("wu1", [D, FF]), din("wd1", [FF, D])
    wg2, wu2, wd2 = din("wg2", [D, FF]), din("wu2", [D, FF]), din("wd2", [FF, D])
    w_in = din("w_in", [D, 4608])
    w_att, w_cp, w_out = din("w_att", [512, D]), din("w_cp", [512, D]), din("w_out", [D, D])
    lnp = din("lnp", [128, 6, D])
    rope_d = din("rope", [128, MAXT, 32])
    convw_d = din("convw", [128, 124])
    convp_d = din("convp", [128, 12])
    bgate_d = din("bgate", [128, 16])
    subg_d = din("subg", [128, 128])
    lamv_d = din("lamv", [128, 256])
    ident_d = din("ident", [128, 128])
    y = nc.dram_tensor("y", [T, D], F32, kind="ExternalOutput").ap()
    qTs = nc.dram_tensor("qTs", [nb, 128, 4, 512], BF16, kind="Internal").ap()
    syTs = nc.dram_tensor("syTs", [nb, 128, 4, 512], BF16, kind="Internal").ap()
    onTs = nc.dram_tensor("onTs", [nb, 128, 4, 512], BF16, kind="Internal").ap()
    with ExitStack() as es:
        k = K(nc, es)
        x_t = [Tile() for _ in range(nb)]
        x1_t = [Tile() for _ in range(nb)]
        x2_t = [Tile() for _ in range(nb)]
        y_t = [Tile() for _ in range(nb)]
        sy_t = [Tile() for _ in range(nb)]
        on_t = [Tile() for _ in range(nb)]
        if 1 in phases:
            ffn_phase(k, nc, T, x, x_t, wg1, wu1, wd1, lnp, 0, ident_d, y, y_t, "a_")
        if 2 in phases:
            mix_phase(k, nc, seqs, y, y_t, w_in, rope_d, convw_d, convp_d, subg_d, lamv_d, ident_d,
                      qTs, syTs, onTs, sy_t, on_t)
        if 3 in phases:
            merge_phase(k, nc, T, y, y_t, w_in, w_att, w_cp, w_out, lnp, bgate_d, ident_d, syTs, onTs, sy_t, on_t, y, y_t)
        if 4 in phases:
            ffn_phase(k, nc, T, y, y_t, wg2, wu2, wd2, lnp, 2, ident_d, y, y_t, "b_")
        if DBG.get("dmapad"):
            with ExitStack() as es2:
                pa = _sb(nc, es2, "dpad_a", [128, D], F32)
                ta = Tile()
                for i in range(400):
                    k.dma(k.sp, pa[:, :], x[(i % 8) * 128:(i % 8 + 1) * 128, :], ta, writes=[ta])
                k.barrier()
        if DBG.get("pepad"):
            with ExitStack() as es2:
                pa = _sb(nc, es2, "pad_a", [128, 128], BF16)
                pp = es2.enter_context(nc.psum_tensor("pad_p", [128, 128], F32))
                ta, tp = Tile(), Tile()
                k.op(k.dve, lambda: nc.vector.memset(pa[:, :], 0.0), writes=[ta])
                for i in range(20000):
                    k.op(k.pe, lambda: nc.tensor.matmul(pp[:, :], lhsT=pa[:, :], rhs=pa[:, :], start=True, stop=True),
                         reads=[ta], writes=[tp], inc=(i == 19999))
                k.barrier()
        k.barrier()
    return nc


def rope_table():
    inv = (np.float32(500000.0) ** (-(np.arange(0, 16, 2, dtype=np.float32)) / np.float32(16))).astype(np.float32)
    pos = np.arange(MAXT * 128, dtype=np.float32)
    ang = (pos[:, None] * inv[None, :]).astype(np.float32)
    cos = np.cos(ang).astype(np.float32)
    sin = np.sin(ang).astype(np.float32)
    tab = np.concatenate([cos, cos, -sin, sin], axis=1)
    return np.ascontiguousarray(tab.reshape(MAXT, 128, 32).transpose(1, 0, 2))


def prep_shared(inp):
    f = lambda a: np.ascontiguousarray(np.asarray(a, dtype=np.float32))
    sh = {}
    sh["wg1"], sh["wu1"], sh["wd1"] = f(inp["ffn1_w_gate"][0]), f(inp["ffn1_w_up"][0]), f(inp["ffn1_w_down"][0])
    sh["wg2"], sh["wu2"], sh["wd2"] = f(inp["ffn2_w_gate"][0]), f(inp["ffn2_w_up"][0]), f(inp["ffn2_w_down"][0])
    sh["w_in"] = f(inp["w_in"][0])
    sh["w_att"], sh["w_cp"], sh["w_out"] = f(inp["w_att_proj"][0]), f(inp["w_conv_proj"][0]), f(inp["w_out"][0])
    lnv = np.stack([f(inp[n][0]) for n in ("ln1_g", "ln1_b", "ln2_g", "ln2_b", "ln3_g", "ln3_b")], axis=0)
    sh["lnp"] = np.ascontiguousarray(np.broadcast_to(lnv[None], (128, 6, D)))
    sh["rope"] = rope_table()
    cw = f(inp["conv_dw_w"][0])[:, 0, :]
    sh["convw"] = np.ascontiguousarray(cw.reshape(CONV_K, 4, 128).transpose(2, 1, 0).reshape(128, 124))
    cpv = np.stack([f(inp["conv_dw_b"][0]), f(inp["conv_ln_g"][0]), f(inp["conv_ln_b"][0])], axis=0)
    sh["convp"] = np.ascontiguousarray(cpv.reshape(3, 4, 128).transpose(2, 0, 1).reshape(128, 12))
    bgv = f(inp["b_gate"][0])
    sh["bgate"] = np.ascontiguousarray(bgv.reshape(2, 8, 128).transpose(2, 0, 1).reshape(128, 16))
    sh["subg"] = np.ascontiguousarray(np.broadcast_to(f(inp["subln_g"][0])[None], (128, 128)))
    lv = np.concatenate([f(inp[n][0]) for n in ("lambda_q1", "lambda_k1", "lambda_q2", "lambda_k2")])
    sh["lamv"] = np.ascontiguousarray(np.broadcast_to(lv[None], (128, 256)))
    sh["ident"] = np.eye(128, dtype=np.float32)
    return sh


SEQS = [4096, 2048, 2048, 2048, 2048]


def kernel(**inp):
    sh = prep_shared(inp)
    xp = np.asarray(inp["x_prompt"], dtype=np.float32)
    xs = np.asarray(inp["x_sample"], dtype=np.float32)
    nc = build(SEQS)
    in_maps = []
    for c in range(NCORES):
        xc = np.concatenate([xp[c].reshape(-1, D)] + [xs[4 * c + i].reshape(-1, D) for i in range(4)], axis=0)
        m = dict(sh)
        m["x"] = np.ascontiguousarray(xc)
        in_maps.append(m)
    res = run_bass_kernel_spmd(nc, in_maps, core_ids=list(range(NCORES)))
    yp = np.empty_like(xp)
    ys = np.empty_like(xs)
    for c in range(NCORES):
        yc = np.asarray(res.results[c]["y"], dtype=np.float32)
        yp[c] = yc[0:4096]
        for i in range(4):
            ys[4 * c + i] = yc[4096 + 2048 * i:4096 + 2048 * (i + 1)]
    return (yp, ys)
```

```python
import math
from contextlib import ExitStack

import numpy as np
import concourse.bass as bass
import concourse.mybir as mybir
from concourse.bass_utils import run_bass_kernel_spmd

F32 = mybir.dt.float32
BF16 = mybir.dt.bfloat16
AF = mybir.ActivationFunctionType
ALU = mybir.AluOpType

D = 1024
FF = 2816
NFC = FF // 128
NDC = D // 128
ALPHA = 2.0 ** 0.25
EPS = 1e-5
LAMBDA_INIT = 0.8 - 0.6 * math.exp(-0.3 * 0)
CONV_K = 31
NCORES = 8
MAXT = 32
NXT = 6
DBG = {}


class Sem:
    __slots__ = ("h", "cnt")

    def __init__(self, h):
        self.h = h
        self.cnt = 0


class Tile:
    __slots__ = ("w", "r", "dsem", "name", "psum")

    def __init__(self, name="", psum=False):
        self.w = {}
        self.r = {}
        self.dsem = None
        self.name = name
        self.psum = psum


class Eng:
    def __init__(self, h, sem, name, is_pe=False):
        self.h = h
        self.sem = sem
        self.name = name
        self.is_pe = is_pe
        self.waited = {}


class K:
    def __init__(self, nc, es):
        self.nc = nc
        self.es = es
        self.allsems = []
        self.freesems = []
        self.pe = Eng(nc.tensor, self._mksem("s_pe"), "pe", True)
        self.act = Eng(nc.scalar, self._mksem("s_act"), "act")
        self.dve = Eng(nc.vector, self._mksem("s_dve"), "dve")
        self.pool = Eng(nc.gpsimd, self._mksem("s_pool"), "pool")
        self.sp = Eng(nc.sync, self._mksem("s_sp"), "sp")
        self.engs = [self.pe, self.act, self.dve, self.pool, self.sp]
        self.phase_sems = []
        self.nsem = 0

    def _mksem(self, name):
        s = Sem(self.es.enter_context(self.nc.semaphore(name)))
        self.allsems.append(s)
        return s

    def dsem(self):
        if self.freesems:
            s = self.freesems.pop()
        else:
            self.nsem += 1
            s = self._mksem("s_d%d" % self.nsem)
        self.phase_sems.append(s)
        return s

    def _waits(self, eng, reads, writes):
        need = {}
        for t in reads:
            for s, v in t.w.items():
                if need.get(s, 0) < v:
                    need[s] = v
            if t.psum:
                for s, v in t.r.items():
                    if s is not eng.sem and need.get(s, 0) < v:
                        need[s] = v
        for t in writes:
            for s, v in t.w.items():
                if need.get(s, 0) < v:
                    need[s] = v
            for s, v in t.r.items():
                if need.get(s, 0) < v:
                    need[s] = v
        for s, v in need.items():
            if eng.is_pe and s is eng.sem:
                continue
            if eng.waited.get(s, 0) < v:
                eng.h.wait_ge(s.h, v)
                eng.waited[s] = v

    def op(self, eng, fn, reads=(), writes=(), inc=True):
        self._waits(eng, reads, writes)
        ins = fn()
        s = eng.sem
        if inc:
            s.cnt += 1
            ins.then_inc(s.h, 1)
            seq = s.cnt
        else:
            seq = s.cnt + 1
        for t in reads:
            if t.r.get(s, 0) < seq:
                t.r[s] = seq
        for t in writes:
            if t.w.get(s, 0) < seq:
                t.w[s] = seq
        return ins

    def dma(self, q, out, in_, st, reads=(), writes=()):
        self._waits(q, reads, writes)
        if st.dsem is None:
            st.dsem = self.dsem()
        s = st.dsem
        q.h.dma_start(out=out, in_=in_).then_inc(s.h, 16)
        s.cnt += 16
        for t in reads:
            t.r[s] = s.cnt
        for t in writes:
            t.w[s] = s.cnt

    def barrier(self):
        for e in self.engs:
            for s in self.allsems:
                if s.cnt > 0 and e.waited.get(s, 0) < s.cnt:
                    e.h.wait_ge(s.h, s.cnt)
                    e.waited[s] = s.cnt
        self.freesems.extend(self.phase_sems)
        self.phase_sems = []


def _sb(nc, es, name, shape, dt):
    return es.enter_context(nc.sbuf_tensor(name, shape, dt))


def load_weight(k, dst, dst_tile, src, rows, cols, col0=0):
    nchunk = rows // 128
    step = next(st for st in (2048, 1408, 1280, 1024, 512, 256, 128) if cols % st == 0)
    for rc in range(nchunk):
        for c0 in range(0, cols, step):
            k.dma(k.pool, dst[:, rc, c0:c0 + step], src[rc * 128:(rc + 1) * 128, col0 + c0:col0 + c0 + step],
                  dst_tile, writes=[dst_tile])


def ln_epilogue(k, nc, xtj, xt_t, Yh, Yh_t, xscale, eps_eff, gB, bB, cst_t, stat, stat_t, mhalf, out_rows, out_t):
    for h in range(2):
        k.op(k.dve, lambda h=h: nc.vector.scalar_tensor_tensor(
            out=xtj[:, h * 512:(h + 1) * 512], in0=xtj[:, h * 512:(h + 1) * 512], scalar=float(xscale),
            in1=Yh[h][:, :], op0=ALU.mult, op1=ALU.add), reads=[Yh_t[h], xt_t], writes=[xt_t])
    for h in range(2):
        k.op(k.dve, lambda h=h: nc.vector.bn_stats(out=stat[:, h * 6:(h + 1) * 6], in_=xtj[:, h * 512:(h + 1) * 512]),
             reads=[xt_t], writes=[stat_t])
    k.op(k.dve, lambda: nc.vector.bn_aggr(out=stat[:, 12:14], in_=stat[:, 0:12]), reads=[stat_t], writes=[stat_t])
    k.op(k.dve, lambda: nc.vector.tensor_scalar(out=stat[:, 14:15], in0=stat[:, 13:14], scalar1=float(eps_eff),
                                                scalar2=None, op0=ALU.add), reads=[stat_t], writes=[stat_t])
    k.op(k.pool, lambda: nc.gpsimd.tensor_tensor(out=stat[:, 15:16], in0=stat[:, 14:15], in1=mhalf[:, 0:1], op=ALU.pow),
         reads=[stat_t, cst_t], writes=[stat_t])
    k.op(k.dve, lambda: nc.vector.tensor_scalar(out=stat[:, 16:17], in0=stat[:, 12:13], scalar1=-1.0,
                                                scalar2=stat[:, 15:16], op0=ALU.mult, op1=ALU.mult),
         reads=[stat_t], writes=[stat_t])
    k.op(k.act, lambda: nc.scalar.activation(out=xtj[:, :], in_=xtj[:, :], func=AF.Identity,
                                             scale=stat[:, 15:16], bias=stat[:, 16:17]),
         reads=[xt_t, stat_t], writes=[xt_t])
    k.op(k.pool, lambda: nc.gpsimd.tensor_tensor(out=xtj[:, :], in0=xtj[:, :], in1=gB, op=ALU.mult),
         reads=[xt_t, cst_t], writes=[xt_t])
    k.op(k.pool, lambda: nc.gpsimd.tensor_tensor(out=xtj[:, :], in0=xtj[:, :], in1=bB, op=ALU.add),
         reads=[xt_t, cst_t], writes=[xt_t])
    k.dma(k.sp, out_rows, xtj[:, :], xt_t, reads=[xt_t], writes=[out_t])


def transpose_block(k, nc, xt, xt_t, tiles, ident, cst_t, banks, bank_t, xT, xT_t, bi):
    for dc in range(NDC):
        bk = (bi * NDC + dc) % 4
        for j in range(4):
            k.op(k.pe, lambda j=j, dc=dc, bk=bk: nc.tensor.transpose(
                out=banks[bk][:, j * 128:(j + 1) * 128], in_=xt[tiles[j]][:, dc * 128:(dc + 1) * 128], identity=ident),
                reads=[xt_t[tiles[j]], cst_t], writes=[bank_t[bk]], inc=(j == 3))
        if dc % 2 == 0:
            k.op(k.dve, lambda dc=dc, bk=bk: nc.vector.tensor_copy(out=xT[:, dc, :], in_=banks[bk][:, :]),
                 reads=[bank_t[bk]], writes=[xT_t])
        else:
            k.op(k.act, lambda dc=dc, bk=bk: nc.scalar.copy(out=xT[:, dc, :], in_=banks[bk][:, :]),
                 reads=[bank_t[bk]], writes=[xT_t])


def ffn_phase(k, nc, T, xin, xin_t, Wg, Wu, Wd, lnp, ln_idx, ident_d, xout, xout_t, pname):
    nb = T // 512
    with ExitStack() as es:
        wg = _sb(nc, es, pname + "wg", [128, NDC, FF], BF16)
        wu = _sb(nc, es, pname + "wu", [128, NDC, FF], BF16)
        wd = _sb(nc, es, pname + "wd", [128, NFC, D], BF16)
        cst = _sb(nc, es, pname + "cst", [128, 2 * D + 128 + 8], F32)
        xt = [_sb(nc, es, pname + "xt%d" % i, [128, D], F32) for i in range(NXT)]
        xT = [_sb(nc, es, pname + "xT%d" % i, [128, NDC, 512], BF16) for i in range(2)]
        hT = _sb(nc, es, pname + "hT", [128, NFC, 512], BF16)
        sg = [_sb(nc, es, pname + "sg%d" % i, [128, 512], F32) for i in range(2)]
        stat = [_sb(nc, es, pname + "st%d" % i, [128, 32], F32) for i in range(4)]
        banks = [es.enter_context(nc.psum_tensor(pname + "bk%d" % i, [128, 512], F32)) for i in range(8)]
        wg_t, wu_t, wd_t, cst_t = Tile("wg"), Tile("wu"), Tile("wd"), Tile("cst")
        xt_t = [Tile("xt") for _ in range(NXT)]
        xT_t = [Tile("xT") for _ in range(2)]
        hT_t = [Tile("hT") for _ in range(NFC)]
        sg_t = [Tile("sg") for _ in range(2)]
        stat_t = [Tile("stat") for _ in range(4)]
        bank_t = [Tile("bank", True) for _ in range(8)]
        gB = cst[:, 0:D]
        bB = cst[:, D:2 * D]
        ident = cst[:, 2 * D:2 * D + 128]
        mhalf = cst[:, 2 * D + 128:2 * D + 129]
        k.dma(k.sp, gB, lnp[:, 2 * ln_idx, :], cst_t, writes=[cst_t])
        k.dma(k.sp, bB, lnp[:, 2 * ln_idx + 1, :], cst_t, writes=[cst_t])
        k.dma(k.sp, ident, ident_d[:, :], cst_t, writes=[cst_t])
        k.op(k.dve, lambda: nc.vector.memset(mhalf, -0.5), writes=[cst_t])

        def load_x(b, js=(0, 1, 2, 3)):
            for j in js:
                ti = (b * 4 + j) % NXT
                r0 = (b * 4 + j) * 128
                k.dma(k.sp, xt[ti][:, :], xin[r0:r0 + 128, :], xt_t[ti], reads=[xin_t[b]], writes=[xt_t[ti]])

        load_x(0)
        load_weight(k, wg, wg_t, Wg, D, FF)
        load_weight(k, wu, wu_t, Wu, D, FF)
        load_weight(k, wd, wd_t, Wd, FF, D)
        for b in range(nb):
            if b + 1 < nb:
                load_x(b + 1, (0, 1))
            tiles = [(b * 4 + j) % NXT for j in range(4)]
            xTb, xTb_t = xT[b % 2], xT_t[b % 2]
            transpose_block(k, nc, xt, xt_t, tiles, ident, cst_t, banks, bank_t, xTb, xTb_t, b)
            for fc in range(NFC):
                A, B = (fc % 2) * 2, (fc % 2) * 2 + 1
                for dc in range(NDC):
                    k.op(k.pe, lambda dc=dc, fc=fc, A=A: nc.tensor.matmul(
                        banks[A][:, :], lhsT=wg[:, dc, fc * 128:(fc + 1) * 128], rhs=xTb[:, dc, :],
                        start=(dc == 0), stop=(dc == NDC - 1)),
                        reads=[wg_t, xTb_t], writes=[bank_t[A]], inc=(dc == NDC - 1))
                for dc in range(NDC):
                    k.op(k.pe, lambda dc=dc, fc=fc, B=B: nc.tensor.matmul(
                        banks[B][:, :], lhsT=wu[:, dc, fc * 128:(fc + 1) * 128], rhs=xTb[:, dc, :],
                        start=(dc == 0), stop=(dc == NDC - 1)),
                        reads=[wu_t, xTb_t], writes=[bank_t[B]], inc=(dc == NDC - 1))
                k.op(k.act, lambda fc=fc, A=A: nc.scalar.activation(out=sg[fc % 2][:, :], in_=banks[A][:, :], func=AF.Silu),
                     reads=[bank_t[A]], writes=[sg_t[fc % 2]])
                k.op(k.dve, lambda fc=fc, B=B: nc.vector.tensor_tensor(
                    out=hT[:, fc, :], in0=banks[B][:, :], in1=sg[fc % 2][:, :], op=ALU.mult),
                    reads=[bank_t[B], sg_t[fc % 2]], writes=[hT_t[fc]])
            for j in range(4):
                Yb = [4 + (j % 2) * 2, 5 + (j % 2) * 2]
                for h in range(2):
                    for fc in range(NFC):
                        k.op(k.pe, lambda fc=fc, h=h, j=j, Yb=Yb: nc.tensor.matmul(
                            banks[Yb[h]][:, :], lhsT=hT[:, fc, j * 128:(j + 1) * 128], rhs=wd[:, fc, h * 512:(h + 1) * 512],
                            start=(fc == 0), stop=(fc == NFC - 1)),
                            reads=[hT_t[fc], wd_t], writes=[bank_t[Yb[h]]], inc=(fc == NFC - 1))
                ti = tiles[j]
                r0 = (b * 4 + j) * 128
                ln_epilogue(k, nc, xt[ti], xt_t[ti], [banks[Yb[0]], banks[Yb[1]]], [bank_t[Yb[0]], bank_t[Yb[1]]],
                            2.0 * ALPHA, 4.0 * EPS, gB, bB, cst_t, stat[j], stat_t[j], mhalf,
                            xout[r0:r0 + 128, :], xout_t[b])
                if j == 1 and b + 1 < nb:
                    load_x(b + 1, (2, 3))
        k.barrier()


def mix_phase(k, nc, seqs, x1s, x1_t, w_in, rope_d, convw_d, convp_d, subg_d, lamv_d, ident_d,
              qTs, syTs, onTs, sy_t, on_t):
    T = sum(seqs)
    nbt = T // 512
    Smax = max(seqs)
    NKT = Smax // 128
    with ExitStack() as es:
        win = _sb(nc, es, "m_win", [128, NDC, 2560], BF16)
        diag = _sb(nc, es, "m_diag", [128, 4, CONV_K, 128], BF16)
        kT = _sb(nc, es, "m_kT", [128, 4, Smax], BF16)
        V = _sb(nc, es, "m_V", [128, NKT, 4, 130], BF16)
        cst = _sb(nc, es, "m_cst", [128, 128 + 128 + 4 * 31 + 12 + 128 + 8], F32)
        ropet = _sb(nc, es, "m_rope", [128, MAXT, 32], F32)
        lamt = _sb(nc, es, "m_lam", [128, 4 * 64 + 16], F32)
        identb = _sb(nc, es, "m_idb", [128, 128], BF16)
        if DBG.get("pad"):
            _sb(nc, es, "m_pad", [128, 1024], F32)
        pbig = es.enter_context(nc.psum_tensor("m_pbig", [128, 8, 512], F32))
        banks = [pbig[:, i, :] for i in range(8)]
        bank_t = [Tile("bank", True) for _ in range(8)]
        win_t, diag_t, cst_t, rope_t, lam_t, idb_t = Tile(), Tile(), Tile(), Tile(), Tile(), Tile()
        kT_t = [Tile() for _ in range(NKT)]
        V_t = [Tile() for _ in range(NKT)]
        qTs_t = [Tile() for _ in range(nbt)]

        ident = cst[:, 0:128]
        ones = cst[:, 128:256]
        cw = cst[:, 256:256 + 124]
        cp = cst[:, 380:392]
        gsub = cst[:, 392:520]
        mhalf = cst[:, 520:521]
        neglam = lamt[:, 256 + 8:256 + 9]

        k.dma(k.sp, ident, ident_d[:, :], cst_t, writes=[cst_t])
        k.dma(k.sp, cw, convw_d[:, :], cst_t, writes=[cst_t])
        k.dma(k.sp, cp, convp_d[:, :], cst_t, writes=[cst_t])
        k.dma(k.sp, gsub, subg_d[:, :], cst_t, writes=[cst_t])
        k.dma(k.sp, ropet[:, :, :], rope_d[:, :, :], rope_t, writes=[rope_t])
        k.dma(k.sp, lamt[:, 0:256], lamv_d[:, :], lam_t, writes=[lam_t])
        k.op(k.dve, lambda: nc.vector.memset(ones, 1.0), writes=[cst_t])
        k.op(k.dve, lambda: nc.vector.memset(mhalf, -0.5), writes=[cst_t])
        k.op(k.dve, lambda: nc.vector.tensor_copy(out=identb[:, :], in_=ident), reads=[cst_t], writes=[idb_t])
        k.op(k.dve, lambda: nc.vector.tensor_scalar(out=gsub, in0=gsub, scalar1=float(1.0 - LAMBDA_INIT), scalar2=None,
                                                    op0=ALU.mult), reads=[cst_t], writes=[cst_t])
        k.op(k.dve, lambda: nc.vector.tensor_tensor(out=lamt[:, 0:64], in0=lamt[:, 0:64], in1=lamt[:, 64:128], op=ALU.mult),
             reads=[lam_t], writes=[lam_t])
        k.op(k.dve, lambda: nc.vector.tensor_tensor(out=lamt[:, 128:192], in0=lamt[:, 128:192], in1=lamt[:, 192:256], op=ALU.mult),
             reads=[lam_t], writes=[lam_t])
        k.op(k.dve, lambda: nc.vector.reduce_sum(out=lamt[:, 256:257], in_=lamt[:, 0:64], axis=mybir.AxisListType.X),
             reads=[lam_t], writes=[lam_t])
        k.op(k.dve, lambda: nc.vector.reduce_sum(out=lamt[:, 257:258], in_=lamt[:, 128:192], axis=mybir.AxisListType.X),
             reads=[lam_t], writes=[lam_t])
        k.op(k.act, lambda: nc.scalar.activation(out=lamt[:, 258:260], in_=lamt[:, 256:258], func=AF.Exp),
             reads=[lam_t], writes=[lam_t])
        k.op(k.dve, lambda: nc.vector.scalar_tensor_tensor(out=neglam, in0=lamt[:, 259:260], scalar=float(-LAMBDA_INIT),
                                                           in1=lamt[:, 258:259], op0=ALU.add, op1=ALU.subtract),
             reads=[lam_t], writes=[lam_t])
        for c in range(4):
            for j in range(CONV_K):
                k.op(k.dve, lambda c=c, j=j: nc.vector.tensor_scalar(
                    out=diag[:, c, j, :], in0=ident, scalar1=cw[:, c * CONV_K + j:c * CONV_K + j + 1], scalar2=None,
                    op0=ALU.mult), reads=[cst_t], writes=[diag_t])
        k.op(k.pool, lambda: nc.gpsimd.memset(V[:, :, :, 128:130], 1.0), writes=V_t)
        load_weight(k, win, win_t, w_in, D, 2560)

        def conv_block(gb, slot):
            if DBG.get("noconv"):
                return
            us = usl[slot]
            S1, S2 = 4, 5
            for c in range(4):
                YC = c
                for j in range(CONV_K):
                    k.op(k.pe, lambda c=c, j=j, YC=YC: nc.tensor.matmul(
                        banks[YC][:, :], lhsT=diag[:, c, j, :], rhs=us[:, c, 1 + j:1 + j + 512],
                        start=(j == 0), stop=(j == CONV_K - 1)),
                        reads=[diag_t, usl_t[slot]], writes=[bank_t[YC]], inc=(j == CONV_K - 1))
                k.op(k.act, lambda c=c, YC=YC: nc.scalar.activation(
                    out=ysb[:, c, :], in_=banks[YC][:, :], func=AF.Identity, scale=0.5, bias=cp[:, c:c + 1]),
                    reads=[bank_t[YC], cst_t], writes=[ysb_t[c]])
            for c in range(4):
                k.op(k.pe, lambda c=c: nc.tensor.matmul(banks[S1][:, :], lhsT=ones, rhs=ysb[:, c, :], start=(c == 0), stop=(c == 3)),
                     reads=[cst_t, ysb_t[c]], writes=[bank_t[S1]], inc=(c == 3))
            for c in range(4):
                k.op(k.pool, lambda c=c: nc.gpsimd.tensor_tensor(out=ysq[:, c % 2, :], in0=ysb[:, c, :], in1=ysb[:, c, :], op=ALU.mult),
                     reads=[ysb_t[c]], writes=[ysq_t[c % 2]])
                k.op(k.pe, lambda c=c: nc.tensor.matmul(banks[S2][:, :], lhsT=ones, rhs=ysq[:, c % 2, :], start=(c == 0), stop=(c == 3)),
                     reads=[cst_t, ysq_t[c % 2]], writes=[bank_t[S2]], inc=True)
            mean, tmpv = ysq[:, 0, :], ysq[:, 1, :]
            rstd = tmpv
            mean_t, tmp_t = ysq_t[0], ysq_t[1]
            k.op(k.dve, lambda: nc.vector.tensor_scalar(out=mean, in0=banks[S1][:, :], scalar1=1.0 / 512, scalar2=None, op0=ALU.mult),
                 reads=[bank_t[S1]], writes=[mean_t])
            k.op(k.dve, lambda: nc.vector.tensor_tensor(out=tmpv, in0=mean, in1=mean, op=ALU.mult), reads=[mean_t], writes=[tmp_t])
            k.op(k.dve, lambda: nc.vector.scalar_tensor_tensor(out=tmpv, in0=banks[S2][:, :], scalar=1.0 / 512, in1=tmpv,
                                                               op0=ALU.mult, op1=ALU.subtract),
                 reads=[bank_t[S2], tmp_t], writes=[tmp_t])
            k.op(k.dve, lambda: nc.vector.tensor_scalar(out=tmpv, in0=tmpv, scalar1=float(EPS), scalar2=None, op0=ALU.add),
                 reads=[tmp_t], writes=[tmp_t])
            k.op(k.act, lambda: nc.scalar.activation(out=tmpv, in_=tmpv, func=AF.Sqrt), reads=[tmp_t], writes=[tmp_t])
            k.op(k.dve, lambda: nc.vector.reciprocal(out=tmpv, in_=tmpv), reads=[tmp_t], writes=[tmp_t])
            so, so_t = syT[0], syT_t[0]
            for c in range(4):
                k.op(k.dve, lambda c=c: nc.vector.tensor_tensor(out=ysb[:, c, :], in0=ysb[:, c, :], in1=mean, op=ALU.subtract),
                     reads=[ysb_t[c], mean_t], writes=[ysb_t[c]])
                k.op(k.pool, lambda c=c: nc.gpsimd.tensor_tensor(out=ysb[:, c, :], in0=ysb[:, c, :], in1=rstd, op=ALU.mult),
                     reads=[ysb_t[c], tmp_t], writes=[ysb_t[c]])
                k.op(k.act, lambda c=c: nc.scalar.activation(out=so[:, c, :], in_=ysb[:, c, :], func=AF.Silu,
                                                             scale=cp[:, 4 + c:5 + c], bias=cp[:, 8 + c:9 + c]),
                     reads=[ysb_t[c], cst_t], writes=[so_t])
            k.dma(k.sp, syTs[gb], so[:, :, :], so_t, reads=[so_t], writes=[sy_t[gb]])

        def rope_part(src_bank, tt, qi_):
            Q3 = banks[src_bank][:, :].rearrange("p (g d) -> p g d", g=8)
            q_ = qr[qi_]
            q3 = q_[:, :].rearrange("p (g d) -> p g d", g=8)
            t1 = rtmp[qi_ % 2][:, 0, :, :]
            t2 = rtmp[qi_ % 2][:, 1, :, :]
            rt_t = rtmp_t[qi_ % 2]
            cc = ropet[:, tt, 0:16].unsqueeze(1).to_broadcast([128, 8, 16])
            nsa = ropet[:, tt, 16:24].unsqueeze(1).to_broadcast([128, 8, 8])
            nsb = ropet[:, tt, 24:32].unsqueeze(1).to_broadcast([128, 8, 8])
            k.op(k.dve, lambda: nc.vector.tensor_tensor(out=t1, in0=Q3[:, :, 0:16], in1=cc, op=ALU.mult),
                 reads=[bank_t[src_bank], rope_t], writes=[rt_t])
            k.op(k.dve, lambda: nc.vector.tensor_tensor(out=t2[:, :, 0:8], in0=Q3[:, :, 8:16], in1=nsa, op=ALU.mult),
                 reads=[bank_t[src_bank], rope_t], writes=[rt_t])
            k.op(k.dve, lambda: nc.vector.tensor_tensor(out=t2[:, :, 8:16], in0=Q3[:, :, 0:8], in1=nsb, op=ALU.mult),
                 reads=[bank_t[src_bank], rope_t], writes=[rt_t])
            k.op(k.act, lambda: nc.scalar.copy(out=q3[:, :, 16:64], in_=Q3[:, :, 16:64]),
                 reads=[bank_t[src_bank]], writes=[qr_t[qi_]])
            k.op(k.dve, lambda: nc.vector.tensor_tensor(out=q3[:, :, 0:16], in0=t1, in1=t2, op=ALU.add),
                 reads=[rt_t], writes=[qr_t[qi_]])

        def T_part(src_bank, dst, dst_tiles, dcol, qi_, ev):
            q_ = qr[qi_]
            PT = banks[src_bank][:, :].bitcast(BF16)
            for h in range(4):
                k.op(k.pe, lambda h=h: nc.tensor.transpose(out=PT[:, h * 128:(h + 1) * 128], in_=q_[:, h * 128:(h + 1) * 128],
                                                           identity=identb[:, :]),
                     reads=[qr_t[qi_], idb_t], writes=[bank_t[src_bank]], inc=(h == 3))
            if ev == 0:
                k.op(k.dve, lambda: nc.vector.tensor_copy(out=dst[:, :, dcol:dcol + 128],
                                                          in_=PT[:, 0:512].rearrange("p (h t) -> p h t", h=4)),
                     reads=[bank_t[src_bank]], writes=dst_tiles)
            else:
                k.op(k.act, lambda: nc.scalar.copy(out=dst[:, :, dcol:dcol + 128],
                                                   in_=PT[:, 0:512].rearrange("p (h t) -> p h t", h=4)),
                     reads=[bank_t[src_bank]], writes=dst_tiles)

        gb0 = 0
        tok0 = 0
        for si, S in enumerate(seqs):
            nb = S // 512
            nkt = S // 128
            esa = ExitStack()
            usl = [_sb(nc, esa, "m%d_u%d" % (si, i), [128, 4, 544], BF16) for i in range(2)]
            xt = [_sb(nc, esa, "m%d_xt%d" % (si, i), [128, D], F32) for i in range(4)]
            xT = _sb(nc, esa, "m%d_xT" % si, [128, NDC, 512], BF16)
            qr = [_sb(nc, esa, "m%d_qr%d" % (si, i), [128, 512], BF16) for i in range(4)]
            rtmp = [_sb(nc, esa, "m%d_rt%d" % (si, i), [128, 2, 8, 16], F32) for i in range(2)]
            qTb = [_sb(nc, esa, "m%d_qT%d" % (si, i), [128, 4, 512], BF16) for i in range(1)]
            sig = [_sb(nc, esa, "m%d_sig%d" % (si, i), [128, 512], F32) for i in range(2)]
            ysb = _sb(nc, esa, "m%d_ysb" % si, [128, 4, 512], F32)
            ysq = _sb(nc, esa, "m%d_ysq" % si, [128, 2, 512], F32)
            syT = [_sb(nc, esa, "m%d_syT%d" % (si, i), [128, 4, 512], BF16) for i in range(1)]
            usl_t = [Tile(), Tile()]
            xt_t = [Tile() for _ in range(4)]
            xT_t = Tile()
            qr_t = [Tile() for _ in range(4)]
            rtmp_t = [Tile(), Tile()]
            qTb_t = [Tile()]
            sig_t = [Tile(), Tile()]
            ysb_t = [Tile() for _ in range(4)]
            ysq_t = [Tile() for _ in range(2)]
            syT_t = [Tile()]
            def load_x1(b_):
                for j_ in range(4):
                    r0_ = tok0 + (b_ * 4 + j_) * 128
                    k.dma(k.sp, xt[j_][:, :], x1s[r0_:r0_ + 128, :], xt_t[j_], reads=[x1_t[gb0 + b_]], writes=[xt_t[j_]])

            load_x1(0)
            for b in range(nb):
                gb = gb0 + b
                transpose_block(k, nc, xt, xt_t, [0, 1, 2, 3], ident, cst_t, banks, bank_t, xT, xT_t, gb)
                if b + 1 < nb:
                    load_x1(b + 1)
                qo, qo_t = qTb[0], qTb_t[0]

                def tpart(j_):
                    kt_ = b * 4 + j_
                    T_part(0 + (j_ % 2), qo, [qo_t], j_ * 128, (j_ % 2) * 2, 0)
                    T_part(2 + (j_ % 2), kT, [kT_t[kt_]], kt_ * 128, (j_ % 2) * 2 + 1, 1)

                for j in range(4):
                    kt = b * 4 + j
                    QB, KB, VB = 0 + (j % 2), 2 + (j % 2), 4 + (j % 2)
                    for dc in range(NDC):
                        for (bk, c0) in ((QB, 0), (KB, 512), (VB, 1024)):
                            k.op(k.pe, lambda dc=dc, bk=bk, c0=c0, j=j: nc.tensor.matmul(
                                banks[bk][:, :], lhsT=xT[:, dc, j * 128:(j + 1) * 128], rhs=win[:, dc, c0:c0 + 512],
                                start=(dc == 0), stop=(dc == NDC - 1)),
                                reads=[xT_t, win_t], writes=[bank_t[bk]], inc=(dc == NDC - 1))
                    rope_part(QB, kt, (j % 2) * 2)
                    rope_part(KB, kt, (j % 2) * 2 + 1)
                    k.op(k.act, lambda VB=VB, kt=kt: nc.scalar.copy(
                        out=V[:, kt, :, 0:128], in_=banks[VB][:, :].rearrange("p (h e) -> p h e", h=4)),
                        reads=[bank_t[VB]], writes=[V_t[kt]])
                    if j >= 1:
                        tpart(j - 1)
                tpart(3)
                k.dma(k.sp, qTs[gb], qo[:, :, :], qo_t, reads=[qo_t], writes=[qTs_t[gb]])
                slot = b % 2
                us = usl[slot]
                if b == 0:
                    k.op(k.pool, lambda us=us: nc.gpsimd.memset(us[:, :, 0:16], 0.0), writes=[usl_t[slot]])
                for c in range(4):
                    ZA, ZB = (6, 7) if c % 2 == 0 else (4, 5)
                    for dc in range(NDC):
                        k.op(k.pe, lambda dc=dc, c=c: nc.tensor.matmul(
                            banks[ZA][:, :], lhsT=win[:, dc, 1536 + c * 128:1536 + (c + 1) * 128], rhs=xT[:, dc, :],
                            start=(dc == 0), stop=(dc == NDC - 1)), reads=[win_t, xT_t], writes=[bank_t[ZA]], inc=(dc == NDC - 1))
                    for dc in range(NDC):
                        k.op(k.pe, lambda dc=dc, c=c: nc.tensor.matmul(
                            banks[ZB][:, :], lhsT=win[:, dc, 2048 + c * 128:2048 + (c + 1) * 128], rhs=xT[:, dc, :],
                            start=(dc == 0), stop=(dc == NDC - 1)), reads=[win_t, xT_t], writes=[bank_t[ZB]], inc=(dc == NDC - 1))
                    k.op(k.act, lambda c=c, ZB=ZB: nc.scalar.activation(out=sig[c % 2][:, :], in_=banks[ZB][:, :], func=AF.Tanh, scale=0.5),
                         reads=[bank_t[ZB]], writes=[sig_t[c % 2]])
                    k.op(k.dve, lambda c=c, us=us: nc.vector.scalar_tensor_tensor(
                        out=us[:, c, 16:528], in0=sig[c % 2][:, :], scalar=1.0, in1=banks[ZA][:, :], op0=ALU.add, op1=ALU.mult),
                        reads=[sig_t[c % 2], bank_t[ZA]], writes=[usl_t[slot]])
                if b > 0:
                    ps_ = usl[1 - slot]
                    k.op(k.pool, lambda us=us, ps_=ps_: nc.gpsimd.tensor_copy(out=ps_[:, :, 528:544], in_=us[:, :, 16:32]),
                         reads=[usl_t[slot]], writes=[usl_t[1 - slot]])
                    conv_block(gb - 1, 1 - slot)
                if b + 1 < nb:
                    ns_ = usl[1 - slot]
                    k.op(k.pool, lambda us=us, ns_=ns_: nc.gpsimd.tensor_copy(out=ns_[:, :, 0:16], in_=us[:, :, 512:528]),
                         reads=[usl_t[slot]], writes=[usl_t[1 - slot]])
                else:
                    k.op(k.pool, lambda us=us: nc.gpsimd.memset(us[:, :, 528:544], 0.0), writes=[usl_t[slot]])
                    conv_block(gb, slot)
            k.barrier()
            esa.close()
            esb = ExitStack()
            if DBG.get("noattn"):
                nb = 0
            qz = [[_sb(nc, esb, "m%d_qz%d_%d" % (si, p_, m_), [128, 4, 512], BF16) for m_ in range(2)] for p_ in range(2)]
            qz_t = [Tile(), Tile()]
            for p_ in range(2):
                k.op(k.pool, lambda p_=p_: nc.gpsimd.memset(qz[p_][0][64:128, :, :], 0.0), writes=[qz_t[p_]])
                k.op(k.pool, lambda p_=p_: nc.gpsimd.memset(qz[p_][1][0:64, :, :], 0.0), writes=[qz_t[p_]])

            def load_q(gb_):
                p_ = gb_ % 2
                k.dma(k.sp, qz[p_][0][0:64, :, :], qTs[gb_][0:64], qz_t[p_], reads=[qTs_t[gb_]], writes=[qz_t[p_]])
                k.dma(k.sp, qz[p_][1][64:128, :, :], qTs[gb_][64:128], qz_t[p_], reads=[qTs_t[gb_]], writes=[qz_t[p_]])

            if nb > 0:
                load_q(gb0)
            ET = [_sb(nc, esb, "m%d_E%d" % (si, i), [128, 2, 512], BF16) for i in range(2)]
            osb = [_sb(nc, esb, "m%d_osb%d" % (si, i), [128, 2, 128], F32) for i in range(4)]
            ost = [_sb(nc, esb, "m%d_ost%d" % (si, i), [128, 8], F32) for i in range(4)]
            onb = _sb(nc, esb, "m%d_on" % si, [128, 4, 512], BF16)
            onT = [_sb(nc, esb, "m%d_onT%d" % (si, i), [128, 4, 512], BF16) for i in range(2)]
            ET_t = [Tile(), Tile()]
            osb_t = [Tile() for _ in range(4)]
            ost_t = [Tile() for _ in range(4)]
            onb_t = Tile()
            onT_t = [Tile(), Tile()]
            for b in range(nb):
                gb = gb0 + b
                qzb, qi_t = qz[gb % 2], qz_t[gb % 2]
                if b + 1 < nb:
                    load_q(gb + 1)
                def emit_qk_exp(h, kt, par):
                    ST = [par * 2, par * 2 + 1]
                    for m in range(2):
                        k.op(k.pe, lambda m=m, h=h, kt=kt, ST=ST: nc.tensor.matmul(
                            banks[ST[m]][:, :], lhsT=kT[:, h, kt * 128:(kt + 1) * 128],
                            rhs=qzb[m][:, h, :], start=True, stop=True),
                            reads=[kT_t[kt], qi_t], writes=[bank_t[ST[m]]], inc=True)
                    k.op(k.act, lambda par=par: nc.scalar.activation(
                        out=ET[par][:, :, :], in_=pbig[:, 2 * par:2 * par + 2, :], func=AF.Exp, scale=0.125),
                        reads=[bank_t[ST[0]], bank_t[ST[1]]], writes=[ET_t[par]])

                def emit_pv(h, kt, par):
                    for j in range(4 if not DBG.get("nopv") else 0):
                        for m in range(2):
                            k.op(k.pe, lambda m=m, j=j, h=h, kt=kt, par=par: nc.tensor.matmul(
                                banks[4 + j][:, m * 256:m * 256 + 129], lhsT=ET[par][:, m, j * 128:(j + 1) * 128],
                                rhs=V[:, kt, h, 0:129], start=(kt == 0 and m == 0), stop=(kt == nkt - 1 and m == 1),
                                skip_group_check=True),
                                reads=[ET_t[par], V_t[kt]], writes=[bank_t[4 + j]], inc=(m == 1))

                def emit_epi(h):
                    for j in range(4 if not DBG.get("noepi") else 0):
                        acc = banks[4 + j]
                        st_, st_t = ost[j], ost_t[j]
                        k.op(k.dve, lambda acc=acc, st_=st_: nc.vector.reciprocal(
                            out=st_[:, 0:2], in_=acc[:, :].rearrange("p (m c) -> p m c", m=2)[:, :, 128]),
                            reads=[bank_t[4 + j]], writes=[st_t])
                        k.op(k.dve, lambda st_=st_: nc.vector.tensor_tensor(out=st_[:, 2:3], in0=st_[:, 1:2], in1=neglam, op=ALU.mult),
                             reads=[st_t, lam_t], writes=[st_t])
                        k.op(k.dve, lambda acc=acc, st_=st_, j=j: nc.vector.tensor_scalar(
                            out=osb[j][:, 0, :], in0=acc[:, 256:384], scalar1=st_[:, 2:3], scalar2=None, op0=ALU.mult),
                            reads=[bank_t[4 + j], st_t], writes=[osb_t[j]])
                        k.op(k.dve, lambda acc=acc, st_=st_, j=j: nc.vector.scalar_tensor_tensor(
                            out=osb[j][:, 0, :], in0=acc[:, 0:128], scalar=st_[:, 0:1], in1=osb[j][:, 0, :], op0=ALU.mult, op1=ALU.add),
                            reads=[bank_t[4 + j], st_t, osb_t[j]], writes=[osb_t[j]])
                    for j in range(4 if not DBG.get("noepi2") else 0):
                        st_, st_t = ost[j], ost_t[j]
                        k.op(k.dve, lambda j=j: nc.vector.tensor_tensor(
                            out=osb[j][:, 1, :], in0=osb[j][:, 0, :], in1=osb[j][:, 0, :], op=ALU.mult),
                            reads=[osb_t[j]], writes=[osb_t[j]])
                        k.op(k.dve, lambda st_=st_, j=j: nc.vector.reduce_sum(
                            out=st_[:, 3:4], in_=osb[j][:, 1, :], axis=mybir.AxisListType.X),
                            reads=[osb_t[j]], writes=[st_t])
                        k.op(k.dve, lambda st_=st_: nc.vector.tensor_scalar(
                            out=st_[:, 4:5], in0=st_[:, 3:4], scalar1=1.0 / 128, scalar2=float(EPS), op0=ALU.mult, op1=ALU.add),
                            reads=[st_t], writes=[st_t])
                        k.op(k.pool, lambda st_=st_: nc.gpsimd.tensor_tensor(out=st_[:, 5:6], in0=st_[:, 4:5], in1=mhalf, op=ALU.pow),
                             reads=[st_t, cst_t], writes=[st_t])
                        k.op(k.dve, lambda st_=st_, j=j, h=h: nc.vector.scalar_tensor_tensor(
                            out=onb[:, j, h * 128:(h + 1) * 128], in0=osb[j][:, 0, :], scalar=st_[:, 5:6], in1=gsub,
                            op0=ALU.mult, op1=ALU.mult), reads=[osb_t[j], st_t, cst_t], writes=[onb_t])

                units = [(h_, kt_) for h_ in range(4) for kt_ in range(nkt)]
                for ui, (h_, kt_) in enumerate(units):
                    emit_qk_exp(h_, kt_, ui % 2)
                    if ui >= 1:
                        hp_, ktp_ = units[ui - 1]
                        emit_pv(hp_, ktp_, (ui - 1) % 2)
                        if ktp_ == nkt - 1:
                            emit_epi(hp_)
                hp_, ktp_ = units[-1]
                emit_pv(hp_, ktp_, (len(units) - 1) % 2)
                emit_epi(hp_)
                oo, oo_t = onT[gb % 2], onT_t[gb % 2]
                for h in range(4 if not DBG.get("noont") else 0):
                    bk = h
                    PT = banks[bk][:, :].bitcast(BF16)
                    for j in range(4):
                        k.op(k.pe, lambda h=h, j=j, PT=PT: nc.tensor.transpose(
                            out=PT[:, j * 128:(j + 1) * 128], in_=onb[:, j, h * 128:(h + 1) * 128], identity=identb[:, :]),
                            reads=[onb_t, idb_t], writes=[bank_t[bk]], inc=(j == 3))
                    if h % 2 == 0:
                        k.op(k.dve, lambda h=h, PT=PT: nc.vector.tensor_copy(out=oo[:, h, :], in_=PT[:, 0:512]),
                             reads=[bank_t[bk]], writes=[oo_t])
                    else:
                        k.op(k.act, lambda h=h, PT=PT: nc.scalar.copy(out=oo[:, h, :], in_=PT[:, 0:512]),
                             reads=[bank_t[bk]], writes=[oo_t])
                if not DBG.get("noonstore"):
                    k.dma(k.sp, onTs[gb], oo[:, :, :], oo_t, reads=[oo_t], writes=[on_t[gb]])
            k.barrier()
            esb.close()
            nb = S // 512
            gb0 += nb
            tok0 += S
        k.barrier()


def merge_phase(k, nc, T, x1s, x1_t, w_in, w_att, w_cp, w_out, lnp, bgate_d, ident_d, syTs, onTs, sy_t, on_t, x2s, x2_t):
    nb = T // 512
    with ExitStack() as es:
        wgt = _sb(nc, es, "g_wgt", [128, NDC, 2048], BF16)
        watt = _sb(nc, es, "g_watt", [128, 4, D], BF16)
        wcp = _sb(nc, es, "g_wcp", [128, 4, D], BF16)
        wout = _sb(nc, es, "g_wout", [128, NDC, D], BF16)
        cst = _sb(nc, es, "g_cst", [128, 2 * D + 128 + 16 + 8], F32)
        xt = [_sb(nc, es, "g_xt%d" % i, [128, D], F32) for i in range(8)]
        xT = [_sb(nc, es, "g_xT%d" % i, [128, NDC, 512], BF16) for i in range(2)]
        onT = [_sb(nc, es, "g_onT%d" % i, [128, 4, 512], BF16) for i in range(2)]
        syT = [_sb(nc, es, "g_syT%d" % i, [128, 4, 512], BF16) for i in range(2)]
        sga = [_sb(nc, es, "g_sga%d" % i, [128, 512], F32) for i in range(2)]
        sgc = [_sb(nc, es, "g_sgc%d" % i, [128, 512], F32) for i in range(2)]
        m1 = [_sb(nc, es, "g_m1%d" % i, [128, 512], F32) for i in range(2)]
        m2 = [_sb(nc, es, "g_m2%d" % i, [128, 512], F32) for i in range(2)]
        mT = _sb(nc, es, "g_mT", [128, NDC, 512], BF16)
        stat = [_sb(nc, es, "g_st%d" % i, [128, 32], F32) for i in range(4)]
        banks = [es.enter_context(nc.psum_tensor("g_bk%d" % i, [128, 512], F32)) for i in range(8)]
        bank_t = [Tile("bank", True) for _ in range(8)]
        wgt_t, watt_t, wcp_t, wout_t, cst_t = Tile(), Tile(), Tile(), Tile(), Tile()
        xt_t = [Tile() for _ in range(8)]
        xT_t = [Tile(), Tile()]
        onT_t = [Tile(), Tile()]
        syT_t = [Tile(), Tile()]
        sga_t = [Tile(), Tile()]
        sgc_t = [Tile(), Tile()]
        m1_t = [Tile(), Tile()]
        m2_t = [Tile(), Tile()]
        mT_t = [Tile() for _ in range(NDC)]
        stat_t = [Tile() for _ in range(4)]
        gB = cst[:, 0:D]
        bB = cst[:, D:2 * D]
        ident = cst[:, 2 * D:2 * D + 128]
        bg = cst[:, 2 * D + 128:2 * D + 144]
        mhalf = cst[:, 2 * D + 144:2 * D + 145]
        k.dma(k.sp, gB, lnp[:, 2, :], cst_t, writes=[cst_t])
        k.dma(k.sp, bB, lnp[:, 3, :], cst_t, writes=[cst_t])
        k.dma(k.sp, ident, ident_d[:, :], cst_t, writes=[cst_t])
        k.dma(k.sp, bg, bgate_d[:, :], cst_t, writes=[cst_t])
        k.op(k.dve, lambda: nc.vector.memset(mhalf, -0.5), writes=[cst_t])

        def load_blk(b):
            for j in range(4):
                ti = (b * 4 + j) % 8
                r0 = (b * 4 + j) * 128
                k.dma(k.sp, xt[ti][:, :], x1s[r0:r0 + 128, :], xt_t[ti], reads=[x1_t[b]], writes=[xt_t[ti]])
            k.dma(k.sp, onT[b % 2][:, :, :], onTs[b], onT_t[b % 2], reads=[on_t[b]], writes=[onT_t[b % 2]])
            k.dma(k.sp, syT[b % 2][:, :, :], syTs[b], syT_t[b % 2], reads=[sy_t[b]], writes=[syT_t[b % 2]])

        load_blk(0)
        load_weight(k, wgt, wgt_t, w_in, D, 2048, col0=2560)
        load_weight(k, watt, watt_t, w_att, 512, D)
        load_weight(k, wcp, wcp_t, w_cp, 512, D)
        load_weight(k, wout, wout_t, w_out, D, D)
        for b in range(nb):
            if b + 1 < nb:
                load_blk(b + 1)
            tiles = [(b * 4 + j) % 8 for j in range(4)]
            xTb, xTb_t = xT[b % 2], xT_t[b % 2]
            on_, on__t = onT[b % 2], onT_t[b % 2]
            sy_, sy__t = syT[b % 2], syT_t[b % 2]
            transpose_block(k, nc, xt, xt_t, tiles, ident, cst_t, banks, bank_t, xTb, xTb_t, b)
            for dm in range(NDC):
                GA, GC, AT, CT = 0, 1, 2, 3
                p = dm % 2
                for dc in range(NDC):
                    k.op(k.pe, lambda dc=dc, dm=dm: nc.tensor.matmul(
                        banks[GA][:, :], lhsT=wgt[:, dc, dm * 128:(dm + 1) * 128], rhs=xTb[:, dc, :],
                        start=(dc == 0), stop=(dc == NDC - 1)), reads=[wgt_t, xTb_t], writes=[bank_t[GA]], inc=(dc == NDC - 1))
                for dc in range(NDC):
                    k.op(k.pe, lambda dc=dc, dm=dm: nc.tensor.matmul(
                        banks[GC][:, :], lhsT=wgt[:, dc, 1024 + dm * 128:1024 + (dm + 1) * 128], rhs=xTb[:, dc, :],
                        start=(dc == 0), stop=(dc == NDC - 1)), reads=[wgt_t, xTb_t], writes=[bank_t[GC]], inc=(dc == NDC - 1))
                for h in range(4):
                    k.op(k.pe, lambda h=h, dm=dm: nc.tensor.matmul(
                        banks[AT][:, :], lhsT=watt[:, h, dm * 128:(dm + 1) * 128], rhs=on_[:, h, :],
                        start=(h == 0), stop=(h == 3)), reads=[watt_t, on__t], writes=[bank_t[AT]], inc=(h == 3))
                for c in range(4):
                    k.op(k.pe, lambda c=c, dm=dm: nc.tensor.matmul(
                        banks[CT][:, :], lhsT=wcp[:, c, dm * 128:(dm + 1) * 128], rhs=sy_[:, c, :],
                        start=(c == 0), stop=(c == 3)), reads=[wcp_t, sy__t], writes=[bank_t[CT]], inc=(c == 3))
                k.op(k.act, lambda dm=dm, p=p: nc.scalar.activation(out=sga[p][:, :], in_=banks[GA][:, :], func=AF.Sigmoid,
                                                                   bias=bg[:, dm:dm + 1]),
                     reads=[bank_t[GA], cst_t], writes=[sga_t[p]])
                k.op(k.act, lambda dm=dm, p=p: nc.scalar.activation(out=sgc[p][:, :], in_=banks[GC][:, :], func=AF.Sigmoid,
                                                                   bias=bg[:, 8 + dm:9 + dm]),
                     reads=[bank_t[GC], cst_t], writes=[sgc_t[p]])
                k.op(k.dve, lambda p=p: nc.vector.tensor_tensor(out=m1[p][:, :], in0=banks[AT][:, :], in1=sga[p][:, :], op=ALU.mult),
                     reads=[bank_t[AT], sga_t[p]], writes=[m1_t[p]])
                k.op(k.dve, lambda p=p: nc.vector.tensor_tensor(out=m2[p][:, :], in0=banks[CT][:, :], in1=sgc[p][:, :], op=ALU.mult),
                     reads=[bank_t[CT], sgc_t[p]], writes=[m2_t[p]])
                k.op(k.pool, lambda p=p, dm=dm: nc.gpsimd.tensor_tensor(out=mT[:, dm, :], in0=m1[p][:, :], in1=m2[p][:, :], op=ALU.add),
                     reads=[m1_t[p], m2_t[p]], writes=[mT_t[dm]])
            for j in range(4):
                Yb = [4 + (j % 2) * 2, 5 + (j % 2) * 2]
                for h in range(2):
                    for dm in range(NDC):
                        k.op(k.pe, lambda dm=dm, h=h, j=j, Yb=Yb: nc.tensor.matmul(
                            banks[Yb[h]][:, :], lhsT=mT[:, dm, j * 128:(j + 1) * 128], rhs=wout[:, dm, h * 512:(h + 1) * 512],
                            start=(dm == 0), stop=(dm == NDC - 1)),
                            reads=[mT_t[dm], wout_t], writes=[bank_t[Yb[h]]], inc=(dm == NDC - 1))
                ti = tiles[j]
                r0 = (b * 4 + j) * 128
                ln_epilogue(k, nc, xt[ti], xt_t[ti], [banks[Yb[0]], banks[Yb[1]]], [bank_t[Yb[0]], bank_t[Yb[1]]],
                            ALPHA, EPS, gB, bB, cst_t, stat[j], stat_t[j], mhalf, x2s[r0:r0 + 128, :], x2_t[b])
        k.barrier()


def build(seqs, phases=(1, 2, 3, 4)):
    T = sum(seqs)
    nb = T // 512
    nc = bass.Bass("TRN2", target_bir_lowering=False)

    def din(name, shape, dt=F32):
        return nc.dram_tensor(name, shape, dt, kind="ExternalInput").ap()

    x = din("x", [T, D])
    wg1, wu1, wd1 = din("wg1", [D, FF]), din("wu1", [D, FF]), din("wd1", [FF, D])
    wg2, wu2, wd2 = din("wg2", [D, FF]), din("wu2", [D, FF]), din("wd2", [FF, D])
    w_in = din("w_in", [D, 4608])
    w_att, w_cp, w_out = din("w_att", [512, D]), din("w_cp", [512, D]), din("w_out", [D, D])
    lnp = din("lnp", [128, 6, D])
    rope_d = din("rope", [128, MAXT, 32])
    convw_d = din("convw", [128, 124])
    convp_d = din("convp", [128, 12])
    bgate_d = din("bgate", [128, 16])
    subg_d = din("subg", [128, 128])
    lamv_d = din("lamv", [128, 256])
    ident_d = din("ident", [128, 128])
    y = nc.dram_tensor("y", [T, D], F32, kind="ExternalOutput").ap()
    qTs = nc.dram_tensor("qTs", [nb, 128, 4, 512], BF16, kind="Internal").ap()
    syTs = nc.dram_tensor("syTs", [nb, 128, 4, 512], BF16, kind="Internal").ap()
    onTs = nc.dram_tensor("onTs", [nb, 128, 4, 512], BF16, kind="Internal").ap()
    with ExitStack() as es:
        k = K(nc, es)
        x_t = [Tile() for _ in range(nb)]
        x1_t = [Tile() for _ in range(nb)]
        x2_t = [Tile() for _ in range(nb)]
        y_t = [Tile() for _ in range(nb)]
        sy_t = [Tile() for _ in range(nb)]
        on_t = [Tile() for _ in range(nb)]
        if 1 in phases:
            ffn_phase(k, nc, T, x, x_t, wg1, wu1, wd1, lnp, 0, ident_d, y, y_t, "a_")
        if 2 in phases:
            mix_phase(k, nc, seqs, y, y_t, w_in, rope_d, convw_d, convp_d, subg_d, lamv_d, ident_d,
                      qTs, syTs, onTs, sy_t, on_t)
        if 3 in phases:
            merge_phase(k, nc, T, y, y_t, w_in, w_att, w_cp, w_out, lnp, bgate_d, ident_d, syTs, onTs, sy_t, on_t, y, y_t)
        if 4 in phases:
            ffn_phase(k, nc, T, y, y_t, wg2, wu2, wd2, lnp, 2, ident_d, y, y_t, "b_")
        if DBG.get("dmapad"):
            with ExitStack() as es2:
                pa = _sb(nc, es2, "dpad_a", [128, D], F32)
                ta = Tile()
                for i in range(400):
                    k.dma(k.sp, pa[:, :], x[(i % 8) * 128:(i % 8 + 1) * 128, :], ta, writes=[ta])
                k.barrier()
        if DBG.get("pepad"):
            with ExitStack() as es2:
                pa = _sb(nc, es2, "pad_a", [128, 128], BF16)
                pp = es2.enter_context(nc.psum_tensor("pad_p", [128, 128], F32))
                ta, tp = Tile(), Tile()
                k.op(k.dve, lambda: nc.vector.memset(pa[:, :], 0.0), writes=[ta])
                for i in range(20000):
                    k.op(k.pe, lambda: nc.tensor.matmul(pp[:, :], lhsT=pa[:, :], rhs=pa[:, :], start=True, stop=True),
                         reads=[ta], writes=[tp], inc=(i == 19999))
                k.barrier()
        k.barrier()
    return nc


def rope_table():
    inv = (np.float32(500000.0) ** (-(np.arange(0, 16, 2, dtype=np.float32)) / np.float32(16))).astype(np.float32)
    pos = np.arange(MAXT * 128, dtype=np.float32)
    ang = (pos[:, None] * inv[None, :]).astype(np.float32)
    cos = np.cos(ang).astype(np.float32)
    sin = np.sin(ang).astype(np.float32)
    tab = np.concatenate([cos, cos, -sin, sin], axis=1)
    return np.ascontiguousarray(tab.reshape(MAXT, 128, 32).transpose(1, 0, 2))


def prep_shared(inp):
    f = lambda a: np.ascontiguousarray(np.asarray(a, dtype=np.float32))
    sh = {}
    sh["wg1"], sh["wu1"], sh["wd1"] = f(inp["ffn1_w_gate"][0]), f(inp["ffn1_w_up"][0]), f(inp["ffn1_w_down"][0])
    sh["wg2"], sh["wu2"], sh["wd2"] = f(inp["ffn2_w_gate"][0]), f(inp["ffn2_w_up"][0]), f(inp["ffn2_w_down"][0])
    sh["w_in"] = f(inp["w_in"][0])
    sh["w_att"], sh["w_cp"], sh["w_out"] = f(inp["w_att_proj"][0]), f(inp["w_conv_proj"][0]), f(inp["w_out"][0])
    lnv = np.stack([f(inp[n][0]) for n in ("ln1_g", "ln1_b", "ln2_g", "ln2_b", "ln3_g", "ln3_b")], axis=0)
    sh["lnp"] = np.ascontiguousarray(np.broadcast_to(lnv[None], (128, 6, D)))
    sh["rope"] = rope_table()
    cw = f(inp["conv_dw_w"][0])[:, 0, :]
    sh["convw"] = np.ascontiguousarray(cw.reshape(CONV_K, 4, 128).transpose(2, 1, 0).reshape(128, 124))
    cpv = np.stack([f(inp["conv_dw_b"][0]), f(inp["conv_ln_g"][0]), f(inp["conv_ln_b"][0])], axis=0)
    sh["convp"] = np.ascontiguousarray(cpv.reshape(3, 4, 128).transpose(2, 0, 1).reshape(128, 12))
    bgv = f(inp["b_gate"][0])
    sh["bgate"] = np.ascontiguousarray(bgv.reshape(2, 8, 128).transpose(2, 0, 1).reshape(128, 16))
    sh["subg"] = np.ascontiguousarray(np.broadcast_to(f(inp["subln_g"][0])[None], (128, 128)))
    lv = np.concatenate([f(inp[n][0]) for n in ("lambda_q1", "lambda_k1", "lambda_q2", "lambda_k2")])
    sh["lamv"] = np.ascontiguousarray(np.broadcast_to(lv[None], (128, 256)))
    sh["ident"] = np.eye(128, dtype=np.float32)
    return sh


SEQS = [4096, 2048, 2048, 2048, 2048]


def kernel(**inp):
    sh = prep_shared(inp)
    xp = np.asarray(inp["x_prompt"], dtype=np.float32)
    xs = np.asarray(inp["x_sample"], dtype=np.float32)
    nc = build(SEQS)
    in_maps = []
    for c in range(NCORES):
        xc = np.concatenate([xp[c].reshape(-1, D)] + [xs[4 * c + i].reshape(-1, D) for i in range(4)], axis=0)
        m = dict(sh)
        m["x"] = np.ascontiguousarray(xc)
        in_maps.append(m)
    res = run_bass_kernel_spmd(nc, in_maps, core_ids=list(range(NCORES)))
    yp = np.empty_like(xp)
    ys = np.empty_like(xs)
    for c in range(NCORES):
        yc = np.asarray(res.results[c]["y"], dtype=np.float32)
        yp[c] = yc[0:4096]
        for i in range(4):
            ys[4 * c + i] = yc[4096 + 2048 * i:4096 + 2048 * (i + 1)]
    return (yp, ys)
```

```python
import math
from contextlib import ExitStack

import numpy as np
import concourse.bass as bass
import concourse.mybir as mybir
from concourse.bass_utils import run_bass_kernel_spmd

F32 = mybir.dt.float32
BF16 = mybir.dt.bfloat16
AF = mybir.ActivationFunctionType
ALU = mybir.AluOpType

D = 1024
FF = 2816
NFC = FF // 128
NDC = D // 128
ALPHA = 2.0 ** 0.25
EPS = 1e-5
LAMBDA_INIT = 0.8 - 0.6 * math.exp(-0.3 * 0)
CONV_K = 31
NCORES = 8
MAXT = 32
NXT = 6
DBG = {}


class Sem:
    __slots__ = ("h", "cnt")

    def __init__(self, h):
        self.h = h
        self.cnt = 0


class Tile:
    __slots__ = ("w", "r", "dsem", "name", "psum")

    def __init__(self, name="", psum=False):
        self.w = {}
        self.r = {}
        self.dsem = None
        self.name = name
        self.psum = psum


class Eng:
    def __init__(self, h, sem, name, is_pe=False):
        self.h = h
        self.sem = sem
        self.name = name
        self.is_pe = is_pe
        self.waited = {}


class K:
    def __init__(self, nc, es):
        self.nc = nc
        self.es = es
        self.allsems = []
        self.freesems = {}
        self.pe = Eng(nc.tensor, self._mksem("s_pe"), "pe", True)
        self.act = Eng(nc.scalar, self._mksem("s_act"), "act")
        self.dve = Eng(nc.vector, self._mksem("s_dve"), "dve")
        self.pool = Eng(nc.gpsimd, self._mksem("s_pool"), "pool")
        self.sp = Eng(nc.sync, self._mksem("s_sp"), "sp")
        self.engs = [self.pe, self.act, self.dve, self.pool, self.sp]
        self.phase_sems = []
        self.nsem = 0

    def _mksem(self, name):
        s = Sem(self.es.enter_context(self.nc.semaphore(name)))
        self.allsems.append(s)
        return s

    def dsem(self, kind):
        fl = self.freesems.setdefault(kind, [])
        if fl:
            s = fl.pop()
        else:
            self.nsem += 1
            s = self._mksem("s_%s%d" % (kind, self.nsem))
        self.phase_sems.append((kind, s))
        return s

    def _waits(self, eng, reads, writes):
        need = {}
        for t in reads:
            for s, v in t.w.items():
                if need.get(s, 0) < v:
                    need[s] = v
            if t.psum:
                for s, v in t.r.items():
                    if s is not eng.sem and need.get(s, 0) < v:
                        need[s] = v
        for t in writes:
            for s, v in t.w.items():
                if need.get(s, 0) < v:
                    need[s] = v
            for s, v in t.r.items():
                if need.get(s, 0) < v:
                    need[s] = v
        for s, v in need.items():
            if eng.is_pe and s is eng.sem:
                continue
            if eng.waited.get(s, 0) < v:
                eng.h.wait_ge(s.h, v)
                eng.waited[s] = v

    def op(self, eng, fn, reads=(), writes=(), inc=True):
        self._waits(eng, reads, writes)
        ins = fn()
        s = eng.sem
        if inc:
            s.cnt += 1
            ins.then_inc(s.h, 1)
            seq = s.cnt
        else:
            seq = s.cnt + 1
        for t in reads:
            if t.r.get(s, 0) < seq:
                t.r[s] = seq
        for t in writes:
            if t.w.get(s, 0) < seq:
                t.w[s] = seq
        return ins

    def dma(self, q, out, in_, st, reads=(), writes=()):
        self._waits(q, reads, writes)
        kind = "sw" if q is self.pool else "hw"
        if st.dsem is None:
            st.dsem = self.dsem(kind)
            st.name = kind
        assert st.name == kind, "tile used by both DMA queue kinds"
        s = st.dsem
        q.h.dma_start(out=out, in_=in_).then_inc(s.h, 16)
        s.cnt += 16
        for t in reads:
            t.r[s] = s.cnt
        for t in writes:
            t.w[s] = s.cnt

    def barrier(self):
        for e in self.engs:
            for s in self.allsems:
                if s.cnt > 0 and e.waited.get(s, 0) < s.cnt:
                    e.h.wait_ge(s.h, s.cnt)
                    e.waited[s] = s.cnt
        for kind, s in self.phase_sems:
            self.freesems.setdefault(kind, []).append(s)
        self.phase_sems = []


def _sb(nc, es, name, shape, dt):
    return es.enter_context(nc.sbuf_tensor(name, shape, dt))


def load_weight(k, dst, dst_tile, src, rows, cols, col0=0):
    nchunk = rows // 128
    step = next(st for st in (2048, 1408, 1280, 1024, 512, 256, 128) if cols % st == 0)
    for rc in range(nchunk):
        for c0 in range(0, cols, step):
            k.dma(k.pool, dst[:, rc, c0:c0 + step], src[rc * 128:(rc + 1) * 128, col0 + c0:col0 + c0 + step],
                  dst_tile, writes=[dst_tile])


def ln_epilogue(k, nc, xtj, xt_t, Yh, Yh_t, xscale, eps_eff, gB, bB, cst_t, stat, stat_t, mhalf, out_rows, out_t):
    for h in range(2):
        k.op(k.dve, lambda h=h: nc.vector.scalar_tensor_tensor(
            out=xtj[:, h * 512:(h + 1) * 512], in0=xtj[:, h * 512:(h + 1) * 512], scalar=float(xscale),
            in1=Yh[h][:, :], op0=ALU.mult, op1=ALU.add), reads=[Yh_t[h], xt_t], writes=[xt_t])
    for h in range(2):
        k.op(k.dve, lambda h=h: nc.vector.bn_stats(out=stat[:, h * 6:(h + 1) * 6], in_=xtj[:, h * 512:(h + 1) * 512]),
             reads=[xt_t], writes=[stat_t])
    k.op(k.dve, lambda: nc.vector.bn_aggr(out=stat[:, 12:14], in_=stat[:, 0:12]), reads=[stat_t], writes=[stat_t])
    k.op(k.dve, lambda: nc.vector.tensor_scalar(out=stat[:, 14:15], in0=stat[:, 13:14], scalar1=float(eps_eff),
                                                scalar2=None, op0=ALU.add), reads=[stat_t], writes=[stat_t])
    k.op(k.pool, lambda: nc.gpsimd.tensor_tensor(out=stat[:, 15:16], in0=stat[:, 14:15], in1=mhalf[:, 0:1], op=ALU.pow),
         reads=[stat_t, cst_t], writes=[stat_t])
    k.op(k.dve, lambda: nc.vector.tensor_scalar(out=stat[:, 16:17], in0=stat[:, 12:13], scalar1=-1.0,
                                                scalar2=stat[:, 15:16], op0=ALU.mult, op1=ALU.mult),
         reads=[stat_t], writes=[stat_t])
    k.op(k.act, lambda: nc.scalar.activation(out=xtj[:, :], in_=xtj[:, :], func=AF.Identity,
                                             scale=stat[:, 15:16], bias=stat[:, 16:17]),
         reads=[xt_t, stat_t], writes=[xt_t])
    k.op(k.pool, lambda: nc.gpsimd.tensor_tensor(out=xtj[:, :], in0=xtj[:, :], in1=gB, op=ALU.mult),
         reads=[xt_t, cst_t], writes=[xt_t])
    k.op(k.pool, lambda: nc.gpsimd.tensor_tensor(out=xtj[:, :], in0=xtj[:, :], in1=bB, op=ALU.add),
         reads=[xt_t, cst_t], writes=[xt_t])
    k.dma(k.sp, out_rows, xtj[:, :], xt_t, reads=[xt_t], writes=[out_t])


def transpose_block(k, nc, xt, xt_t, tiles, ident, cst_t, banks, bank_t, xT, xT_t, bi):
    for dc in range(NDC):
        bk = (bi * NDC + dc) % 4
        for j in range(4):
            k.op(k.pe, lambda j=j, dc=dc, bk=bk: nc.tensor.transpose(
                out=banks[bk][:, j * 128:(j + 1) * 128], in_=xt[tiles[j]][:, dc * 128:(dc + 1) * 128], identity=ident),
                reads=[xt_t[tiles[j]], cst_t], writes=[bank_t[bk]], inc=(j == 3))
        if dc % 2 == 0:
            k.op(k.dve, lambda dc=dc, bk=bk: nc.vector.tensor_copy(out=xT[:, dc, :], in_=banks[bk][:, :]),
                 reads=[bank_t[bk]], writes=[xT_t])
        else:
            k.op(k.act, lambda dc=dc, bk=bk: nc.scalar.copy(out=xT[:, dc, :], in_=banks[bk][:, :]),
                 reads=[bank_t[bk]], writes=[xT_t])


def ffn_phase(k, nc, T, xin, xin_t, Wg, Wu, Wd, lnp, ln_idx, ident_d, xout, xout_t, pname):
    nb = T // 512
    with ExitStack() as es:
        wg = _sb(nc, es, pname + "wg", [128, NDC, FF], BF16)
        wu = _sb(nc, es, pname + "wu", [128, NDC, FF], BF16)
        wd = _sb(nc, es, pname + "wd", [128, NFC, D], BF16)
        cst = _sb(nc, es, pname + "cst", [128, 2 * D + 128 + 8], F32)
        xt = [_sb(nc, es, pname + "xt%d" % i, [128, D], F32) for i in range(NXT)]
        xT = [_sb(nc, es, pname + "xT%d" % i, [128, NDC, 512], BF16) for i in range(2)]
        hT = _sb(nc, es, pname + "hT", [128, NFC, 512], BF16)
        sg = [_sb(nc, es, pname + "sg%d" % i, [128, 512], F32) for i in range(2)]
        stat = [_sb(nc, es, pname + "st%d" % i, [128, 32], F32) for i in range(4)]
        banks = [es.enter_context(nc.psum_tensor(pname + "bk%d" % i, [128, 512], F32)) for i in range(8)]
        wg_t, wu_t, wd_t, cst_t = Tile("wg"), Tile("wu"), Tile("wd"), Tile("cst")
        xt_t = [Tile("xt") for _ in range(NXT)]
        xT_t = [Tile("xT") for _ in range(2)]
        hT_t = [Tile("hT") for _ in range(NFC)]
        sg_t = [Tile("sg") for _ in range(2)]
        stat_t = [Tile("stat") for _ in range(4)]
        bank_t = [Tile("bank", True) for _ in range(8)]
        gB = cst[:, 0:D]
        bB = cst[:, D:2 * D]
        ident = cst[:, 2 * D:2 * D + 128]
        mhalf = cst[:, 2 * D + 128:2 * D + 129]
        k.dma(k.sp, gB, lnp[:, 2 * ln_idx, :], cst_t, writes=[cst_t])
        k.dma(k.sp, bB, lnp[:, 2 * ln_idx + 1, :], cst_t, writes=[cst_t])
        k.dma(k.sp, ident, ident_d[:, :], cst_t, writes=[cst_t])
        k.op(k.dve, lambda: nc.vector.memset(mhalf, -0.5), writes=[cst_t])

        def load_x(b, js=(0, 1, 2, 3)):
            for j in js:
                ti = (b * 4 + j) % NXT
                r0 = (b * 4 + j) * 128
                k.dma(k.sp, xt[ti][:, :], xin[r0:r0 + 128, :], xt_t[ti], reads=[xin_t[b]], writes=[xt_t[ti]])

        load_x(0)
        load_weight(k, wg, wg_t, Wg, D, FF)
        load_weight(k, wu, wu_t, Wu, D, FF)
        load_weight(k, wd, wd_t, Wd, FF, D)
        for b in range(nb):
            if b + 1 < nb:
                load_x(b + 1, (0, 1))
            tiles = [(b * 4 + j) % NXT for j in range(4)]
            xTb, xTb_t = xT[b % 2], xT_t[b % 2]
            transpose_block(k, nc, xt, xt_t, tiles, ident, cst_t, banks, bank_t, xTb, xTb_t, b)
            for fc in range(NFC):
                A, B = (fc % 2) * 2, (fc % 2) * 2 + 1
                for dc in range(NDC):
                    k.op(k.pe, lambda dc=dc, fc=fc, A=A: nc.tensor.matmul(
                        banks[A][:, :], lhsT=wg[:, dc, fc * 128:(fc + 1) * 128], rhs=xTb[:, dc, :],
                        start=(dc == 0), stop=(dc == NDC - 1)),
                        reads=[wg_t, xTb_t], writes=[bank_t[A]], inc=(dc == NDC - 1))
                for dc in range(NDC):
                    k.op(k.pe, lambda dc=dc, fc=fc, B=B: nc.tensor.matmul(
                        banks[B][:, :], lhsT=wu[:, dc, fc * 128:(fc + 1) * 128], rhs=xTb[:, dc, :],
                        start=(dc == 0), stop=(dc == NDC - 1)),
                        reads=[wu_t, xTb_t], writes=[bank_t[B]], inc=(dc == NDC - 1))
                k.op(k.act, lambda fc=fc, A=A: nc.scalar.activation(out=sg[fc % 2][:, :], in_=banks[A][:, :], func=AF.Silu),
                     reads=[bank_t[A]], writes=[sg_t[fc % 2]])
                k.op(k.dve, lambda fc=fc, B=B: nc.vector.tensor_tensor(
                    out=hT[:, fc, :], in0=banks[B][:, :], in1=sg[fc % 2][:, :], op=ALU.mult),
                    reads=[bank_t[B], sg_t[fc % 2]], writes=[hT_t[fc]])
            for j in range(4):
                Yb = [4 + (j % 2) * 2, 5 + (j % 2) * 2]
                for h in range(2):
                    for fc in range(NFC):
                        k.op(k.pe, lambda fc=fc, h=h, j=j, Yb=Yb: nc.tensor.matmul(
                            banks[Yb[h]][:, :], lhsT=hT[:, fc, j * 128:(j + 1) * 128], rhs=wd[:, fc, h * 512:(h + 1) * 512],
                            start=(fc == 0), stop=(fc == NFC - 1)),
                            reads=[hT_t[fc], wd_t], writes=[bank_t[Yb[h]]], inc=(fc == NFC - 1))
                ti = tiles[j]
                r0 = (b * 4 + j) * 128
                ln_epilogue(k, nc, xt[ti], xt_t[ti], [banks[Yb[0]], banks[Yb[1]]], [bank_t[Yb[0]], bank_t[Yb[1]]],
                            2.0 * ALPHA, 4.0 * EPS, gB, bB, cst_t, stat[j], stat_t[j], mhalf,
                            xout[r0:r0 + 128, :], xout_t[b])
                if j == 1 and b + 1 < nb:
                    load_x(b + 1, (2, 3))
        k.barrier()


def mix_phase(k, nc, seqs, x1s, x1_t, w_in, rope_d, convw_d, convp_d, subg_d, lamv_d, ident_d,
              qTs, syTs, onTs, sy_t, on_t):
    T = sum(seqs)
    nbt = T // 512
    Smax = max(seqs)
    NKT = Smax // 128
    with ExitStack() as es:
        win = _sb(nc, es, "m_win", [128, NDC, 2560], BF16)
        diag = _sb(nc, es, "m_diag", [128, 4, CONV_K, 128], BF16)
        kT = _sb(nc, es, "m_kT", [128, 4, Smax], BF16)
        V = _sb(nc, es, "m_V", [128, NKT, 4, 130], BF16)
        cst = _sb(nc, es, "m_cst", [128, 128 + 128 + 4 * 31 + 12 + 128 + 8], F32)
        ropet = _sb(nc, es, "m_rope", [128, MAXT, 32], F32)
        lamt = _sb(nc, es, "m_lam", [128, 4 * 64 + 16], F32)
        identb = _sb(nc, es, "m_idb", [128, 128], BF16)
        if DBG.get("pad"):
            _sb(nc, es, "m_pad", [128, 1024], F32)
        pbig = es.enter_context(nc.psum_tensor("m_pbig", [128, 8, 512], F32))
        banks = [pbig[:, i, :] for i in range(8)]
        bank_t = [Tile("bank", True) for _ in range(8)]
        win_t, diag_t, cst_t, rope_t, lam_t, idb_t = Tile(), Tile(), Tile(), Tile(), Tile(), Tile()
        kT_t = [Tile() for _ in range(NKT)]
        V_t = [Tile() for _ in range(NKT)]
        qTs_t = [Tile() for _ in range(nbt)]

        ident = cst[:, 0:128]
        ones = cst[:, 128:256]
        cw = cst[:, 256:256 + 124]
        cp = cst[:, 380:392]
        gsub = cst[:, 392:520]
        mhalf = cst[:, 520:521]
        neglam = lamt[:, 256 + 8:256 + 9]

        k.dma(k.sp, ident, ident_d[:, :], cst_t, writes=[cst_t])
        k.dma(k.sp, cw, convw_d[:, :], cst_t, writes=[cst_t])
        k.dma(k.sp, cp, convp_d[:, :], cst_t, writes=[cst_t])
        k.dma(k.sp, gsub, subg_d[:, :], cst_t, writes=[cst_t])
        k.dma(k.sp, ropet[:, :, :], rope_d[:, :, :], rope_t, writes=[rope_t])
        k.dma(k.sp, lamt[:, 0:256], lamv_d[:, :], lam_t, writes=[lam_t])
        k.op(k.dve, lambda: nc.vector.memset(ones, 1.0), writes=[cst_t])
        k.op(k.dve, lambda: nc.vector.memset(mhalf, -0.5), writes=[cst_t])
        k.op(k.dve, lambda: nc.vector.tensor_copy(out=identb[:, :], in_=ident), reads=[cst_t], writes=[idb_t])
        k.op(k.dve, lambda: nc.vector.tensor_scalar(out=gsub, in0=gsub, scalar1=float(1.0 - LAMBDA_INIT), scalar2=None,
                                                    op0=ALU.mult), reads=[cst_t], writes=[cst_t])
        k.op(k.dve, lambda: nc.vector.tensor_tensor(out=lamt[:, 0:64], in0=lamt[:, 0:64], in1=lamt[:, 64:128], op=ALU.mult),
             reads=[lam_t], writes=[lam_t])
        k.op(k.dve, lambda: nc.vector.tensor_tensor(out=lamt[:, 128:192], in0=lamt[:, 128:192], in1=lamt[:, 192:256], op=ALU.mult),
             reads=[lam_t], writes=[lam_t])
        k.op(k.dve, lambda: nc.vector.reduce_sum(out=lamt[:, 256:257], in_=lamt[:, 0:64], axis=mybir.AxisListType.X),
             reads=[lam_t], writes=[lam_t])
        k.op(k.dve, lambda: nc.vector.reduce_sum(out=lamt[:, 257:258], in_=lamt[:, 128:192], axis=mybir.AxisListType.X),
             reads=[lam_t], writes=[lam_t])
        k.op(k.act, lambda: nc.scalar.activation(out=lamt[:, 258:260], in_=lamt[:, 256:258], func=AF.Exp),
             reads=[lam_t], writes=[lam_t])
        k.op(k.dve, lambda: nc.vector.scalar_tensor_tensor(out=neglam, in0=lamt[:, 259:260], scalar=float(-LAMBDA_INIT),
                                                           in1=lamt[:, 258:259], op0=ALU.add, op1=ALU.subtract),
             reads=[lam_t], writes=[lam_t])
        for c in range(4):
            for j in range(CONV_K):
                k.op(k.dve, lambda c=c, j=j: nc.vector.tensor_scalar(
                    out=diag[:, c, j, :], in0=ident, scalar1=cw[:, c * CONV_K + j:c * CONV_K + j + 1], scalar2=None,
                    op0=ALU.mult), reads=[cst_t], writes=[diag_t])
        k.op(k.pool, lambda: nc.gpsimd.memset(V[:, :, :, 128:130], 1.0), writes=V_t)
        load_weight(k, win, win_t, w_in, D, 2560)

        def conv_block(gb, slot):
            if DBG.get("noconv"):
                return
            us = usl[slot]
            S1, S2 = 4, 5
            for c in range(4):
                YC = c
                for j in range(CONV_K):
                    k.op(k.pe, lambda c=c, j=j, YC=YC: nc.tensor.matmul(
                        banks[YC][:, :], lhsT=diag[:, c, j, :], rhs=us[:, c, 1 + j:1 + j + 512],
                        start=(j == 0), stop=(j == CONV_K - 1)),
                        reads=[diag_t, usl_t[slot]], writes=[bank_t[YC]], inc=(j == CONV_K - 1))
                k.op(k.act, lambda c=c, YC=YC: nc.scalar.activation(
                    out=ysb[:, c, :], in_=banks[YC][:, :], func=AF.Identity, scale=0.5, bias=cp[:, c:c + 1]),
                    reads=[bank_t[YC], cst_t], writes=[ysb_t[c]])
            for c in range(4):
                k.op(k.pe, lambda c=c: nc.tensor.matmul(banks[S1][:, :], lhsT=ones, rhs=ysb[:, c, :], start=(c == 0), stop=(c == 3)),
                     reads=[cst_t, ysb_t[c]], writes=[bank_t[S1]], inc=(c == 3))
            for c in range(4):
                k.op(k.pool, lambda c=c: nc.gpsimd.tensor_tensor(out=ysq[:, c % 2, :], in0=ysb[:, c, :], in1=ysb[:, c, :], op=ALU.mult),
                     reads=[ysb_t[c]], writes=[ysq_t[c % 2]])
                k.op(k.pe, lambda c=c: nc.tensor.matmul(banks[S2][:, :], lhsT=ones, rhs=ysq[:, c % 2, :], start=(c == 0), stop=(c == 3)),
                     reads=[cst_t, ysq_t[c % 2]], writes=[bank_t[S2]], inc=True)
            mean, tmpv = ysq[:, 0, :], ysq[:, 1, :]
            rstd = tmpv
            mean_t, tmp_t = ysq_t[0], ysq_t[1]
            k.op(k.dve, lambda: nc.vector.tensor_scalar(out=mean, in0=banks[S1][:, :], scalar1=1.0 / 512, scalar2=None, op0=ALU.mult),
                 reads=[bank_t[S1]], writes=[mean_t])
            k.op(k.dve, lambda: nc.vector.tensor_tensor(out=tmpv, in0=mean, in1=mean, op=ALU.mult), reads=[mean_t], writes=[tmp_t])
            k.op(k.dve, lambda: nc.vector.scalar_tensor_tensor(out=tmpv, in0=banks[S2][:, :], scalar=1.0 / 512, in1=tmpv,
                                                               op0=ALU.mult, op1=ALU.subtract),
                 reads=[bank_t[S2], tmp_t], writes=[tmp_t])
            k.op(k.dve, lambda: nc.vector.tensor_scalar(out=tmpv, in0=tmpv, scalar1=float(EPS), scalar2=None, op0=ALU.add),
                 reads=[tmp_t], writes=[tmp_t])
            k.op(k.act, lambda: nc.scalar.activation(out=tmpv, in_=tmpv, func=AF.Sqrt), reads=[tmp_t], writes=[tmp_t])
            k.op(k.dve, lambda: nc.vector.reciprocal(out=tmpv, in_=tmpv), reads=[tmp_t], writes=[tmp_t])
            so, so_t = syT[0], syT_t[0]
            for c in range(4):
                k.op(k.dve, lambda c=c: nc.vector.tensor_tensor(out=ysb[:, c, :], in0=ysb[:, c, :], in1=mean, op=ALU.subtract),
                     reads=[ysb_t[c], mean_t], writes=[ysb_t[c]])
                k.op(k.pool, lambda c=c: nc.gpsimd.tensor_tensor(out=ysb[:, c, :], in0=ysb[:, c, :], in1=rstd, op=ALU.mult),
                     reads=[ysb_t[c], tmp_t], writes=[ysb_t[c]])
                k.op(k.act, lambda c=c: nc.scalar.activation(out=so[:, c, :], in_=ysb[:, c, :], func=AF.Silu,
                                                             scale=cp[:, 4 + c:5 + c], bias=cp[:, 8 + c:9 + c]),
                     reads=[ysb_t[c], cst_t], writes=[so_t])
            k.dma(k.sp, syTs[gb], so[:, :, :], so_t, reads=[so_t], writes=[sy_t[gb]])

        def rope_part(src_bank, tt, qi_):
            Q3 = banks[src_bank][:, :].rearrange("p (g d) -> p g d", g=8)
            q_ = qr[qi_]
            q3 = q_[:, :].rearrange("p (g d) -> p g d", g=8)
            t1 = rtmp[qi_ % 2][:, 0, :, :]
            t2 = rtmp[qi_ % 2][:, 1, :, :]
            rt_t = rtmp_t[qi_ % 2]
            cc = ropet[:, tt, 0:16].unsqueeze(1).to_broadcast([128, 8, 16])
            nsa = ropet[:, tt, 16:24].unsqueeze(1).to_broadcast([128, 8, 8])
            nsb = ropet[:, tt, 24:32].unsqueeze(1).to_broadcast([128, 8, 8])
            k.op(k.dve, lambda: nc.vector.tensor_tensor(out=t1, in0=Q3[:, :, 0:16], in1=cc, op=ALU.mult),
                 reads=[bank_t[src_bank], rope_t], writes=[rt_t])
            k.op(k.dve, lambda: nc.vector.tensor_tensor(out=t2[:, :, 0:8], in0=Q3[:, :, 8:16], in1=nsa, op=ALU.mult),
                 reads=[bank_t[src_bank], rope_t], writes=[rt_t])
            k.op(k.dve, lambda: nc.vector.tensor_tensor(out=t2[:, :, 8:16], in0=Q3[:, :, 0:8], in1=nsb, op=ALU.mult),
                 reads=[bank_t[src_bank], rope_t], writes=[rt_t])
            k.op(k.act, lambda: nc.scalar.copy(out=q3[:, :, 16:64], in_=Q3[:, :, 16:64]),
                 reads=[bank_t[src_bank]], writes=[qr_t[qi_]])
            k.op(k.dve, lambda: nc.vector.tensor_tensor(out=q3[:, :, 0:16], in0=t1, in1=t2, op=ALU.add),
                 reads=[rt_t], writes=[qr_t[qi_]])

        def T_part(src_bank, dst, dst_tiles, dcol, qi_, ev):
            q_ = qr[qi_]
            PT = banks[src_bank][:, :].bitcast(BF16)
            for h in range(4):
                k.op(k.pe, lambda h=h: nc.tensor.transpose(out=PT[:, h * 128:(h + 1) * 128], in_=q_[:, h * 128:(h + 1) * 128],
                                                           identity=identb[:, :]),
                     reads=[qr_t[qi_], idb_t], writes=[bank_t[src_bank]], inc=(h == 3))
            if ev == 0:
                k.op(k.dve, lambda: nc.vector.tensor_copy(out=dst[:, :, dcol:dcol + 128],
                                                          in_=PT[:, 0:512].rearrange("p (h t) -> p h t", h=4)),
                     reads=[bank_t[src_bank]], writes=dst_tiles)
            else:
                k.op(k.act, lambda: nc.scalar.copy(out=dst[:, :, dcol:dcol + 128],
                                                   in_=PT[:, 0:512].rearrange("p (h t) -> p h t", h=4)),
                     reads=[bank_t[src_bank]], writes=dst_tiles)

        gb0 = 0
        tok0 = 0
        for si, S in enumerate(seqs):
            nb = S // 512
            nkt = S // 128
            esa = ExitStack()
            usl = [_sb(nc, esa, "m%d_u%d" % (si, i), [128, 4, 544], BF16) for i in range(2)]
            xt = [_sb(nc, esa, "m%d_xt%d" % (si, i), [128, D], F32) for i in range(4)]
            xT = _sb(nc, esa, "m%d_xT" % si, [128, NDC, 512], BF16)
            qr = [_sb(nc, esa, "m%d_qr%d" % (si, i), [128, 512], BF16) for i in range(4)]
            rtmp = [_sb(nc, esa, "m%d_rt%d" % (si, i), [128, 2, 8, 16], F32) for i in range(2)]
            qTb = [_sb(nc, esa, "m%d_qT%d" % (si, i), [128, 4, 512], BF16) for i in range(1)]
            sig = [_sb(nc, esa, "m%d_sig%d" % (si, i), [128, 512], F32) for i in range(2)]
            ysb = _sb(nc, esa, "m%d_ysb" % si, [128, 4, 512], F32)
            ysq = _sb(nc, esa, "m%d_ysq" % si, [128, 2, 512], F32)
            syT = [_sb(nc, esa, "m%d_syT%d" % (si, i), [128, 4, 512], BF16) for i in range(1)]
            usl_t = [Tile(), Tile()]
            xt_t = [Tile() for _ in range(4)]
            xT_t = Tile()
            qr_t = [Tile() for _ in range(4)]
            rtmp_t = [Tile(), Tile()]
            qTb_t = [Tile()]
            sig_t = [Tile(), Tile()]
            ysb_t = [Tile() for _ in range(4)]
            ysq_t = [Tile() for _ in range(2)]
            syT_t = [Tile()]
            def load_x1(b_):
                for j_ in range(4):
                    r0_ = tok0 + (b_ * 4 + j_) * 128
                    k.dma(k.sp, xt[j_][:, :], x1s[r0_:r0_ + 128, :], xt_t[j_], reads=[x1_t[gb0 + b_]], writes=[xt_t[j_]])

            load_x1(0)
            for b in range(nb):
                gb = gb0 + b
                transpose_block(k, nc, xt, xt_t, [0, 1, 2, 3], ident, cst_t, banks, bank_t, xT, xT_t, gb)
                if b + 1 < nb:
                    load_x1(b + 1)
                qo, qo_t = qTb[0], qTb_t[0]

                def tpart(j_):
                    kt_ = b * 4 + j_
                    T_part(0 + (j_ % 2), qo, [qo_t], j_ * 128, (j_ % 2) * 2, 0)
                    T_part(2 + (j_ % 2), kT, [kT_t[kt_]], kt_ * 128, (j_ % 2) * 2 + 1, 1)

                for j in range(4):
                    kt = b * 4 + j
                    QB, KB, VB = 0 + (j % 2), 2 + (j % 2), 4 + (j % 2)
                    for dc in range(NDC):
                        for (bk, c0) in ((QB, 0), (KB, 512), (VB, 1024)):
                            k.op(k.pe, lambda dc=dc, bk=bk, c0=c0, j=j: nc.tensor.matmul(
                                banks[bk][:, :], lhsT=xT[:, dc, j * 128:(j + 1) * 128], rhs=win[:, dc, c0:c0 + 512],
                                start=(dc == 0), stop=(dc == NDC - 1)),
                                reads=[xT_t, win_t], writes=[bank_t[bk]], inc=(dc == NDC - 1))
                    rope_part(QB, kt, (j % 2) * 2)
                    rope_part(KB, kt, (j % 2) * 2 + 1)
                    k.op(k.act, lambda VB=VB, kt=kt: nc.scalar.copy(
                        out=V[:, kt, :, 0:128], in_=banks[VB][:, :].rearrange("p (h e) -> p h e", h=4)),
                        reads=[bank_t[VB]], writes=[V_t[kt]])
                    if j >= 1:
                        tpart(j - 1)
                tpart(3)
                k.dma(k.sp, qTs[gb], qo[:, :, :], qo_t, reads=[qo_t], writes=[qTs_t[gb]])
                slot = b % 2
                us = usl[slot]
                if b == 0:
                    k.op(k.pool, lambda us=us: nc.gpsimd.memset(us[:, :, 0:16], 0.0), writes=[usl_t[slot]])
                for c in range(4):
                    ZA, ZB = (6, 7) if c % 2 == 0 else (4, 5)
                    for dc in range(NDC):
                        k.op(k.pe, lambda dc=dc, c=c: nc.tensor.matmul(
                            banks[ZA][:, :], lhsT=win[:, dc, 1536 + c * 128:1536 + (c + 1) * 128], rhs=xT[:, dc, :],
                            start=(dc == 0), stop=(dc == NDC - 1)), reads=[win_t, xT_t], writes=[bank_t[ZA]], inc=(dc == NDC - 1))
                    for dc in range(NDC):
                        k.op(k.pe, lambda dc=dc, c=c: nc.tensor.matmul(
                            banks[ZB][:, :], lhsT=win[:, dc, 2048 + c * 128:2048 + (c + 1) * 128], rhs=xT[:, dc, :],
                            start=(dc == 0), stop=(dc == NDC - 1)), reads=[win_t, xT_t], writes=[bank_t[ZB]], inc=(dc == NDC - 1))
                    k.op(k.act, lambda c=c, ZB=ZB: nc.scalar.activation(out=sig[c % 2][:, :], in_=banks[ZB][:, :], func=AF.Tanh, scale=0.5),
                         reads=[bank_t[ZB]], writes=[sig_t[c % 2]])
                    k.op(k.dve, lambda c=c, us=us: nc.vector.scalar_tensor_tensor(
                        out=us[:, c, 16:528], in0=sig[c % 2][:, :], scalar=1.0, in1=banks[ZA][:, :], op0=ALU.add, op1=ALU.mult),
                        reads=[sig_t[c % 2], bank_t[ZA]], writes=[usl_t[slot]])
                if b > 0:
                    ps_ = usl[1 - slot]
                    k.op(k.pool, lambda us=us, ps_=ps_: nc.gpsimd.tensor_copy(out=ps_[:, :, 528:544], in_=us[:, :, 16:32]),
                         reads=[usl_t[slot]], writes=[usl_t[1 - slot]])
                    conv_block(gb - 1, 1 - slot)
                if b + 1 < nb:
                    ns_ = usl[1 - slot]
                    k.op(k.pool, lambda us=us, ns_=ns_: nc.gpsimd.tensor_copy(out=ns_[:, :, 0:16], in_=us[:, :, 512:528]),
                         reads=[usl_t[slot]], writes=[usl_t[1 - slot]])
                else:
                    k.op(k.pool, lambda us=us: nc.gpsimd.memset(us[:, :, 528:544], 0.0), writes=[usl_t[slot]])
                    conv_block(gb, slot)
            k.barrier()
            esa.close()
            esb = ExitStack()
            if DBG.get("noattn"):
                nb = 0
            qz = [[_sb(nc, esb, "m%d_qz%d_%d" % (si, p_, m_), [128, 4, 512], BF16) for m_ in range(2)] for p_ in range(2)]
            qz_t = [Tile(), Tile()]
            for p_ in range(2):
                k.op(k.pool, lambda p_=p_: nc.gpsimd.memset(qz[p_][0][64:128, :, :], 0.0), writes=[qz_t[p_]])
                k.op(k.pool, lambda p_=p_: nc.gpsimd.memset(qz[p_][1][0:64, :, :], 0.0), writes=[qz_t[p_]])

            def load_q(gb_):
                p_ = gb_ % 2
                k.dma(k.sp, qz[p_][0][0:64, :, :], qTs[gb_][0:64], qz_t[p_], reads=[qTs_t[gb_]], writes=[qz_t[p_]])
                k.dma(k.sp, qz[p_][1][64:128, :, :], qTs[gb_][64:128], qz_t[p_], reads=[qTs_t[gb_]], writes=[qz_t[p_]])

            if nb > 0:
                load_q(gb0)
            ET = [_sb(nc, esb, "m%d_E%d" % (si, i), [128, 2, 512], BF16) for i in range(2)]
            accs = [[_sb(nc, esb, "m%d_acc%d_%d" % (si, p_, i), [128, 2, 129], F32) for i in range(4)] for p_ in range(2)]
            accs_t = [[Tile() for _ in range(4)] for _ in range(2)]
            osb = [_sb(nc, esb, "m%d_osb%d" % (si, i), [128, 2, 128], F32) for i in range(4)]
            ost = [_sb(nc, esb, "m%d_ost%d" % (si, i), [128, 8], F32) for i in range(4)]
            onb = _sb(nc, esb, "m%d_on" % si, [128, 4, 512], BF16)
            onT = [_sb(nc, esb, "m%d_onT%d" % (si, i), [128, 4, 512], BF16) for i in range(2)]
            ET_t = [Tile(), Tile()]
            osb_t = [Tile() for _ in range(4)]
            ost_t = [Tile() for _ in range(4)]
            onb_t = Tile()
            onT_t = [Tile(), Tile()]
            for b in range(nb):
                gb = gb0 + b
                qzb, qi_t = qz[gb % 2], qz_t[gb % 2]
                if b + 1 < nb:
                    load_q(gb + 1)
                def emit_qk_exp(h, kt, par):
                    ST = [par * 2, par * 2 + 1]
                    for m in range(2):
                        k.op(k.pe, lambda m=m, h=h, kt=kt, ST=ST: nc.tensor.matmul(
                            banks[ST[m]][:, :], lhsT=kT[:, h, kt * 128:(kt + 1) * 128],
                            rhs=qzb[m][:, h, :], start=True, stop=True),
                            reads=[kT_t[kt], qi_t], writes=[bank_t[ST[m]]], inc=True)
                    k.op(k.act, lambda par=par: nc.scalar.activation(
                        out=ET[par][:, :, :], in_=pbig[:, 2 * par:2 * par + 2, :], func=AF.Exp, scale=0.125),
                        reads=[bank_t[ST[0]], bank_t[ST[1]]], writes=[ET_t[par]])

                def emit_pv(h, kt, par):
                    for j in range(4 if not DBG.get("nopv") else 0):
                        for m in range(2):
                            k.op(k.pe, lambda m=m, j=j, h=h, kt=kt, par=par: nc.tensor.matmul(
                                banks[4 + j][:, m * 256:m * 256 + 129], lhsT=ET[par][:, m, j * 128:(j + 1) * 128],
                                rhs=V[:, kt, h, 0:129], start=(kt == 0 and m == 0), stop=(kt == nkt - 1 and m == 1),
                                skip_group_check=True),
                                reads=[ET_t[par], V_t[kt]], writes=[bank_t[4 + j]], inc=(m == 1))

                def emit_epi(h):
                    for j in range(4):
                        k.op(k.dve, lambda j=j, h=h: nc.vector.tensor_copy(
                            out=accs[h % 2][j][:, :, :],
                            in_=banks[4 + j][:, :].rearrange("p (m c) -> p m c", m=2)[:, :, 0:129]),
                            reads=[bank_t[4 + j]], writes=[accs_t[h % 2][j]])
                    for j in range(4):
                        acc = accs[h % 2][j]
                        acc_t = accs_t[h % 2][j]
                        st_, st_t = ost[j], ost_t[j]
                        k.op(k.dve, lambda acc=acc, st_=st_: nc.vector.reciprocal(out=st_[:, 0:2], in_=acc[:, :, 128]),
                             reads=[acc_t], writes=[st_t])
                        k.op(k.dve, lambda st_=st_: nc.vector.tensor_tensor(out=st_[:, 2:3], in0=st_[:, 1:2], in1=neglam, op=ALU.mult),
                             reads=[st_t, lam_t], writes=[st_t])
                        k.op(k.dve, lambda acc=acc, st_=st_, j=j: nc.vector.tensor_scalar(
                            out=osb[j][:, 0, :], in0=acc[:, 1, 0:128], scalar1=st_[:, 2:3], scalar2=None, op0=ALU.mult),
                            reads=[acc_t, st_t], writes=[osb_t[j]])
                        k.op(k.dve, lambda acc=acc, st_=st_, j=j: nc.vector.scalar_tensor_tensor(
                            out=osb[j][:, 0, :], in0=acc[:, 0, 0:128], scalar=st_[:, 0:1], in1=osb[j][:, 0, :], op0=ALU.mult, op1=ALU.add),
                            reads=[acc_t, st_t, osb_t[j]], writes=[osb_t[j]])
                    for j in range(4 if not DBG.get("noepi2") else 0):
                        st_, st_t = ost[j], ost_t[j]
                        k.op(k.dve, lambda j=j: nc.vector.tensor_tensor(
                            out=osb[j][:, 1, :], in0=osb[j][:, 0, :], in1=osb[j][:, 0, :], op=ALU.mult),
                            reads=[osb_t[j]], writes=[osb_t[j]])
                        k.op(k.dve, lambda st_=st_, j=j: nc.vector.reduce_sum(
                            out=st_[:, 3:4], in_=osb[j][:, 1, :], axis=mybir.AxisListType.X),
                            reads=[osb_t[j]], writes=[st_t])
                        k.op(k.dve, lambda st_=st_: nc.vector.tensor_scalar(
                            out=st_[:, 4:5], in0=st_[:, 3:4], scalar1=1.0 / 128, scalar2=float(EPS), op0=ALU.mult, op1=ALU.add),
                            reads=[st_t], writes=[st_t])
                        k.op(k.pool, lambda st_=st_: nc.gpsimd.tensor_tensor(out=st_[:, 5:6], in0=st_[:, 4:5], in1=mhalf, op=ALU.pow),
                             reads=[st_t, cst_t], writes=[st_t])
                        k.op(k.dve, lambda st_=st_, j=j, h=h: nc.vector.scalar_tensor_tensor(
                            out=onb[:, j, h * 128:(h + 1) * 128], in0=osb[j][:, 0, :], scalar=st_[:, 5:6], in1=gsub,
                            op0=ALU.mult, op1=ALU.mult), reads=[osb_t[j], st_t, cst_t], writes=[onb_t])

                units = [(h_, kt_) for h_ in range(4) for kt_ in range(nkt)]
                for ui, (h_, kt_) in enumerate(units):
                    emit_qk_exp(h_, kt_, ui % 2)
                    if ui >= 1:
                        hp_, ktp_ = units[ui - 1]
                        emit_pv(hp_, ktp_, (ui - 1) % 2)
                        if ktp_ == nkt - 1:
                            emit_epi(hp_)
                hp_, ktp_ = units[-1]
                emit_pv(hp_, ktp_, (len(units) - 1) % 2)
                emit_epi(hp_)
                oo, oo_t = onT[gb % 2], onT_t[gb % 2]
                for h in range(4 if not DBG.get("noont") else 0):
                    bk = h
                    PT = banks[bk][:, :].bitcast(BF16)
                    for j in range(4):
                        k.op(k.pe, lambda h=h, j=j, PT=PT: nc.tensor.transpose(
                            out=PT[:, j * 128:(j + 1) * 128], in_=onb[:, j, h * 128:(h + 1) * 128], identity=identb[:, :]),
                            reads=[onb_t, idb_t], writes=[bank_t[bk]], inc=(j == 3))
                    if h % 2 == 0:
                        k.op(k.dve, lambda h=h, PT=PT: nc.vector.tensor_copy(out=oo[:, h, :], in_=PT[:, 0:512]),
                             reads=[bank_t[bk]], writes=[oo_t])
                    else:
                        k.op(k.act, lambda h=h, PT=PT: nc.scalar.copy(out=oo[:, h, :], in_=PT[:, 0:512]),
                             reads=[bank_t[bk]], writes=[oo_t])
                if not DBG.get("noonstore"):
                    k.dma(k.sp, onTs[gb], oo[:, :, :], oo_t, reads=[oo_t], writes=[on_t[gb]])
            k.barrier()
            esb.close()
            nb = S // 512
            gb0 += nb
            tok0 += S
        k.barrier()


def merge_phase(k, nc, T, x1s, x1_t, w_in, w_att, w_cp, w_out, lnp, bgate_d, ident_d, syTs, onTs, sy_t, on_t, x2s, x2_t):
    nb = T // 512
    with ExitStack() as es:
        wgt = _sb(nc, es, "g_wgt", [128, NDC, 2048], BF16)
        watt = _sb(nc, es, "g_watt", [128, 4, D], BF16)
        wcp = _sb(nc, es, "g_wcp", [128, 4, D], BF16)
        wout = _sb(nc, es, "g_wout", [128, NDC, D], BF16)
        cst = _sb(nc, es, "g_cst", [128, 2 * D + 128 + 16 + 8], F32)
        xt = [_sb(nc, es, "g_xt%d" % i, [128, D], F32) for i in range(8)]
        xT = [_sb(nc, es, "g_xT%d" % i, [128, NDC, 512], BF16) for i in range(2)]
        onT = [_sb(nc, es, "g_onT%d" % i, [128, 4, 512], BF16) for i in range(2)]
        syT = [_sb(nc, es, "g_syT%d" % i, [128, 4, 512], BF16) for i in range(2)]
        sga = [_sb(nc, es, "g_sga%d" % i, [128, 512], F32) for i in range(2)]
        sgc = [_sb(nc, es, "g_sgc%d" % i, [128, 512], F32) for i in range(2)]
        m1 = [_sb(nc, es, "g_m1%d" % i, [128, 512], F32) for i in range(2)]
        m2 = [_sb(nc, es, "g_m2%d" % i, [128, 512], F32) for i in range(2)]
        mT = _sb(nc, es, "g_mT", [128, NDC, 512], BF16)
        stat = [_sb(nc, es, "g_st%d" % i, [128, 32], F32) for i in range(4)]
        banks = [es.enter_context(nc.psum_tensor("g_bk%d" % i, [128, 512], F32)) for i in range(8)]
        bank_t = [Tile("bank", True) for _ in range(8)]
        wgt_t, watt_t, wcp_t, wout_t, cst_t = Tile(), Tile(), Tile(), Tile(), Tile()
        xt_t = [Tile() for _ in range(8)]
        xT_t = [Tile(), Tile()]
        onT_t = [Tile(), Tile()]
        syT_t = [Tile(), Tile()]
        sga_t = [Tile(), Tile()]
        sgc_t = [Tile(), Tile()]
        m1_t = [Tile(), Tile()]
        m2_t = [Tile(), Tile()]
        mT_t = [Tile() for _ in range(NDC)]
        stat_t = [Tile() for _ in range(4)]
        gB = cst[:, 0:D]
        bB = cst[:, D:2 * D]
        ident = cst[:, 2 * D:2 * D + 128]
        bg = cst[:, 2 * D + 128:2 * D + 144]
        mhalf = cst[:, 2 * D + 144:2 * D + 145]
        k.dma(k.sp, gB, lnp[:, 2, :], cst_t, writes=[cst_t])
        k.dma(k.sp, bB, lnp[:, 3, :], cst_t, writes=[cst_t])
        k.dma(k.sp, ident, ident_d[:, :], cst_t, writes=[cst_t])
        k.dma(k.sp, bg, bgate_d[:, :], cst_t, writes=[cst_t])
        k.op(k.dve, lambda: nc.vector.memset(mhalf, -0.5), writes=[cst_t])

        def load_blk(b):
            for j in range(4):
                ti = (b * 4 + j) % 8
                r0 = (b * 4 + j) * 128
                k.dma(k.sp, xt[ti][:, :], x1s[r0:r0 + 128, :], xt_t[ti], reads=[x1_t[b]], writes=[xt_t[ti]])
            k.dma(k.sp, onT[b % 2][:, :, :], onTs[b], onT_t[b % 2], reads=[on_t[b]], writes=[onT_t[b % 2]])
            k.dma(k.sp, syT[b % 2][:, :, :], syTs[b], syT_t[b % 2], reads=[sy_t[b]], writes=[syT_t[b % 2]])

        load_blk(0)
        load_weight(k, wgt, wgt_t, w_in, D, 2048, col0=2560)
        load_weight(k, watt, watt_t, w_att, 512, D)
        load_weight(k, wcp, wcp_t, w_cp, 512, D)
        load_weight(k, wout, wout_t, w_out, D, D)
        for b in range(nb):
            if b + 1 < nb:
                load_blk(b + 1)
            tiles = [(b * 4 + j) % 8 for j in range(4)]
            xTb, xTb_t = xT[b % 2], xT_t[b % 2]
            on_, on__t = onT[b % 2], onT_t[b % 2]
            sy_, sy__t = syT[b % 2], syT_t[b % 2]
            transpose_block(k, nc, xt, xt_t, tiles, ident, cst_t, banks, bank_t, xTb, xTb_t, b)
            for dm in range(NDC):
                GA, GC, AT, CT = 0, 1, 2, 3
                p = dm % 2
                for dc in range(NDC):
                    k.op(k.pe, lambda dc=dc, dm=dm: nc.tensor.matmul(
                        banks[GA][:, :], lhsT=wgt[:, dc, dm * 128:(dm + 1) * 128], rhs=xTb[:, dc, :],
                        start=(dc == 0), stop=(dc == NDC - 1)), reads=[wgt_t, xTb_t], writes=[bank_t[GA]], inc=(dc == NDC - 1))
                for dc in range(NDC):
                    k.op(k.pe, lambda dc=dc, dm=dm: nc.tensor.matmul(
                        banks[GC][:, :], lhsT=wgt[:, dc, 1024 + dm * 128:1024 + (dm + 1) * 128], rhs=xTb[:, dc, :],
                        start=(dc == 0), stop=(dc == NDC - 1)), reads=[wgt_t, xTb_t], writes=[bank_t[GC]], inc=(dc == NDC - 1))
                for h in range(4):
                    k.op(k.pe, lambda h=h, dm=dm: nc.tensor.matmul(
                        banks[AT][:, :], lhsT=watt[:, h, dm * 128:(dm + 1) * 128], rhs=on_[:, h, :],
                        start=(h == 0), stop=(h == 3)), reads=[watt_t, on__t], writes=[bank_t[AT]], inc=(h == 3))
                for c in range(4):
                    k.op(k.pe, lambda c=c, dm=dm: nc.tensor.matmul(
                        banks[CT][:, :], lhsT=wcp[:, c, dm * 128:(dm + 1) * 128], rhs=sy_[:, c, :],
                        start=(c == 0), stop=(c == 3)), reads=[wcp_t, sy__t], writes=[bank_t[CT]], inc=(c == 3))
                k.op(k.act, lambda dm=dm, p=p: nc.scalar.activation(out=sga[p][:, :], in_=banks[GA][:, :], func=AF.Sigmoid,
                                                                   bias=bg[:, dm:dm + 1]),
                     reads=[bank_t[GA], cst_t], writes=[sga_t[p]])
                k.op(k.act, lambda dm=dm, p=p: nc.scalar.activation(out=sgc[p][:, :], in_=banks[GC][:, :], func=AF.Sigmoid,
                                                                   bias=bg[:, 8 + dm:9 + dm]),
                     reads=[bank_t[GC], cst_t], writes=[sgc_t[p]])
                k.op(k.dve, lambda p=p: nc.vector.tensor_tensor(out=m1[p][:, :], in0=banks[AT][:, :], in1=sga[p][:, :], op=ALU.mult),
                     reads=[bank_t[AT], sga_t[p]], writes=[m1_t[p]])
                k.op(k.dve, lambda p=p: nc.vector.tensor_tensor(out=m2[p][:, :], in0=banks[CT][:, :], in1=sgc[p][:, :], op=ALU.mult),
                     reads=[bank_t[CT], sgc_t[p]], writes=[m2_t[p]])
                k.op(k.pool, lambda p=p, dm=dm: nc.gpsimd.tensor_tensor(out=mT[:, dm, :], in0=m1[p][:, :], in1=m2[p][:, :], op=ALU.add),
                     reads=[m1_t[p], m2_t[p]], writes=[mT_t[dm]])
            for j in range(4):
                Yb = [4 + (j % 2) * 2, 5 + (j % 2) * 2]
                for h in range(2):
                    for dm in range(NDC):
                        k.op(k.pe, lambda dm=dm, h=h, j=j, Yb=Yb: nc.tensor.matmul(
                            banks[Yb[h]][:, :], lhsT=mT[:, dm, j * 128:(j + 1) * 128], rhs=wout[:, dm, h * 512:(h + 1) * 512],
                            start=(dm == 0), stop=(dm == NDC - 1)),
                            reads=[mT_t[dm], wout_t], writes=[bank_t[Yb[h]]], inc=(dm == NDC - 1))
                ti = tiles[j]
                r0 = (b * 4 + j) * 128
                ln_epilogue(k, nc, xt[ti], xt_t[ti], [banks[Yb[0]], banks[Yb[1]]], [bank_t[Yb[0]], bank_t[Yb[1]]],
                            ALPHA, EPS, gB, bB, cst_t, stat[j], stat_t[j], mhalf, x2s[r0:r0 + 128, :], x2_t[b])
        k.barrier()


def build(seqs, phases=(1, 2, 3, 4)):
    T = sum(seqs)
    nb = T // 512
    nc = bass.Bass("TRN2", target_bir_lowering=False)

    def din(name, shape, dt=F32):
        return nc.dram_tensor(name, shape, dt, kind="ExternalInput").ap()

    x = din("x", [T, D])
    wg1, wu1, wd1 = din("wg1", [D, FF]), din("wu1", [D, FF]), din("wd1", [FF, D])
    wg2, wu2, wd2 = din("wg2", [D, FF]), din("wu2", [D, FF]), din("wd2", [FF, D])
    w_in = din("w_in", [D, 4608])
    w_att, w_cp, w_out = din("w_att", [512, D]), din("w_cp", [512, D]), din("w_out", [D, D])
    lnp = din("lnp", [128, 6, D])
    rope_d = din("rope", [128, MAXT, 32])
    convw_d = din("convw", [128, 124])
    convp_d = din("convp", [128, 12])
    bgate_d = din("bgate", [128, 16])
    subg_d = din("subg", [128, 128])
    lamv_d = din("lamv", [128, 256])
    ident_d = din("ident", [128, 128])
    y = nc.dram_tensor("y", [T, D], F32, kind="ExternalOutput").ap()
    qTs = nc.dram_tensor("qTs", [nb, 128, 4, 512], BF16, kind="Internal").ap()
    syTs = nc.dram_tensor("syTs", [nb, 128, 4, 512], BF16, kind="Internal").ap()
    onTs = nc.dram_tensor("onTs", [nb, 128, 4, 512], BF16, kind="Internal").ap()
    with ExitStack() as es:
        k = K(nc, es)
        x_t = [Tile() for _ in range(nb)]
        x1_t = [Tile() for _ in range(nb)]
        x2_t = [Tile() for _ in range(nb)]
        y_t = [Tile() for _ in range(nb)]
        sy_t = [Tile() for _ in range(nb)]
        on_t = [Tile() for _ in range(nb)]
        if 1 in phases:
            ffn_phase(k, nc, T, x, x_t, wg1, wu1, wd1, lnp, 0, ident_d, y, y_t, "a_")
        if 2 in phases:
            mix_phase(k, nc, seqs, y, y_t, w_in, rope_d, convw_d, convp_d, subg_d, lamv_d, ident_d,
                      qTs, syTs, onTs, sy_t, on_t)
        if 3 in phases:
            merge_phase(k, nc, T, y, y_t, w_in, w_att, w_cp, w_out, lnp, bgate_d, ident_d, syTs, onTs, sy_t, on_t, y, y_t)
        if 4 in phases:
            ffn_phase(k, nc, T, y, y_t, wg2, wu2, wd2, lnp, 2, ident_d, y, y_t, "b_")
        if DBG.get("dmapad"):
            with ExitStack() as es2:
                pa = _sb(nc, es2, "dpad_a", [128, D], F32)
                ta = Tile()
                for i in range(400):
                    k.dma(k.sp, pa[:, :], x[(i % 8) * 128:(i % 8 + 1) * 128, :], ta, writes=[ta])
                k.barrier()
        if DBG.get("pepad"):
            with ExitStack() as es2:
                pa = _sb(nc, es2, "pad_a", [128, 128], BF16)
                pp = es2.enter_context(nc.psum_tensor("pad_p", [128, 128], F32))
                ta, tp = Tile(), Tile()
                k.op(k.dve, lambda: nc.vector.memset(pa[:, :], 0.0), writes=[ta])
                for i in range(20000):
                    k.op(k.pe, lambda: nc.tensor.matmul(pp[:, :], lhsT=pa[:, :], rhs=pa[:, :], start=True, stop=True),
                         reads=[ta], writes=[tp], inc=(i == 19999))
                k.barrier()
        k.barrier()
    return nc


def rope_table():
    inv = (np.float32(500000.0) ** (-(np.arange(0, 16, 2, dtype=np.float32)) / np.float32(16))).astype(np.float32)
    pos = np.arange(MAXT * 128, dtype=np.float32)
    ang = (pos[:, None] * inv[None, :]).astype(np.float32)
    cos = np.cos(ang).astype(np.float32)
    sin = np.sin(ang).astype(np.float32)
    tab = np.concatenate([cos, cos, -sin, sin], axis=1)
    return np.ascontiguousarray(tab.reshape(MAXT, 128, 32).transpose(1, 0, 2))


def prep_shared(inp):
    f = lambda a: np.ascontiguousarray(np.asarray(a, dtype=np.float32))
    sh = {}
    sh["wg1"], sh["wu1"], sh["wd1"] = f(inp["ffn1_w_gate"][0]), f(inp["ffn1_w_up"][0]), f(inp["ffn1_w_down"][0])
    sh["wg2"], sh["wu2"], sh["wd2"] = f(inp["ffn2_w_gate"][0]), f(inp["ffn2_w_up"][0]), f(inp["ffn2_w_down"][0])
    sh["w_in"] = f(inp["w_in"][0])
    sh["w_att"], sh["w_cp"], sh["w_out"] = f(inp["w_att_proj"][0]), f(inp["w_conv_proj"][0]), f(inp["w_out"][0])
    lnv = np.stack([f(inp[n][0]) for n in ("ln1_g", "ln1_b", "ln2_g", "ln2_b", "ln3_g", "ln3_b")], axis=0)
    sh["lnp"] = np.ascontiguousarray(np.broadcast_to(lnv[None], (128, 6, D)))
    sh["rope"] = rope_table()
    cw = f(inp["conv_dw_w"][0])[:, 0, :]
    sh["convw"] = np.ascontiguousarray(cw.reshape(CONV_K, 4, 128).transpose(2, 1, 0).reshape(128, 124))
    cpv = np.stack([f(inp["conv_dw_b"][0]), f(inp["conv_ln_g"][0]), f(inp["conv_ln_b"][0])], axis=0)
    sh["convp"] = np.ascontiguousarray(cpv.reshape(3, 4, 128).transpose(2, 0, 1).reshape(128, 12))
    bgv = f(inp["b_gate"][0])
    sh["bgate"] = np.ascontiguousarray(bgv.reshape(2, 8, 128).transpose(2, 0, 1).reshape(128, 16))
    sh["subg"] = np.ascontiguousarray(np.broadcast_to(f(inp["subln_g"][0])[None], (128, 128)))
    lv = np.concatenate([f(inp[n][0]) for n in ("lambda_q1", "lambda_k1", "lambda_q2", "lambda_k2")])
    sh["lamv"] = np.ascontiguousarray(np.broadcast_to(lv[None], (128, 256)))
    sh["ident"] = np.eye(128, dtype=np.float32)
    return sh


SEQS = [4096, 2048, 2048, 2048, 2048]


def kernel(**inp):
    sh = prep_shared(inp)
    xp = np.asarray(inp["x_prompt"], dtype=np.float32)
    xs = np.asarray(inp["x_sample"], dtype=np.float32)
    nc = build(SEQS)
    in_maps = []
    for c in range(NCORES):
        xc = np.concatenate([xp[c].reshape(-1, D)] + [xs[4 * c + i].reshape(-1, D) for i in range(4)], axis=0)
        m = dict(sh)
        m["x"] = np.ascontiguousarray(xc)
        in_maps.append(m)
    res = run_bass_kernel_spmd(nc, in_maps, core_ids=list(range(NCORES)))
    yp = np.empty_like(xp)
    ys = np.empty_like(xs)
    for c in range(NCORES):
        yc = np.asarray(res.results[c]["y"], dtype=np.float32)
        yp[c] = yc[0:4096]
        for i in range(4):
            ys[4 * c + i] = yc[4096 + 2048 * i:4096 + 2048 * (i + 1)]
    return (yp, ys)
```
